# Optimizing a Trainium2 kernel written in Bass

```python
import math
import jax, jax.numpy as jnp
from jax import lax
import numpy as np

D_MODEL = 2048
BATCH = 4
SEQ = 4096
DEPTH = 1

NSA_HEADS = 16
NSA_GROUPS = 4
NSA_HEAD_DIM = 64
NSA_REP = NSA_HEADS // NSA_GROUPS
NSA_WIDTH = NSA_HEADS * NSA_HEAD_DIM
KV_WIDTH = NSA_GROUPS * NSA_HEAD_DIM
CMP_BLOCK = 32
CMP_STRIDE = 16
CMP_HIDDEN = 128
SEL_BLOCK = 64
SEL_TOPK = 8
WINDOW = 512
Q_BLOCK = 128

SSM_WIDTH = 1024
SSM_GROUP = 16
SSM_GROUPS = SSM_WIDTH // SSM_GROUP
SSM_STATE = 64
DT_MIN = 1e-3
DT_MAX = 1e-1

REL_BUCKETS = 32
REL_MAX_DIST = 128

DEEPNORM_ALPHA = (2 * DEPTH) ** 0.25
DEEPNORM_BETA = (8 * DEPTH) ** -0.25
LN_EPS = 1e-5
MASK_VALUE = -1e30
FORCE_VALUE = 1e4

IN_SPLITS = (NSA_WIDTH, 6 * KV_WIDTH, 3 * NSA_HEADS, NSA_WIDTH, SSM_WIDTH, SSM_WIDTH, D_MODEL, D_MODEL)
IN_WIDTH = sum(IN_SPLITS)

kernel_name = 'hybrid_nsa_s5_gated_block'


def _ln(x):
    xf = x.astype(jnp.float32)
    mu = jnp.mean(xf, axis=-1, keepdims=True)
    var = jnp.mean(jnp.square(xf - mu), axis=-1, keepdims=True)
    return ((xf - mu) * lax.rsqrt(var + LN_EPS)).astype(x.dtype)


def _t5_bucket(dist):
    n = jnp.maximum(dist, 0)
    max_exact = REL_BUCKETS // 2
    nf = jnp.maximum(n, max_exact).astype(jnp.float32)
    large = max_exact + (jnp.log(nf / max_exact) / math.log(REL_MAX_DIST / max_exact)
                         * (REL_BUCKETS - max_exact)).astype(jnp.int32)
    large = jnp.minimum(large, REL_BUCKETS - 1)
    return jnp.where(n < max_exact, n, large)


def _masked_softmax(s, valid):
    p = jax.nn.softmax(jnp.where(valid, s, MASK_VALUE), axis=-1)
    return jnp.where(valid, p, 0.0)


def _nsa(q, kv, gates, rel_bias, pos_k, pos_v, wk1, wk2, wv1, wv2):
    B, L, _ = q.shape
    G, R, dh = NSA_GROUPS, NSA_REP, NSA_HEAD_DIM
    dtype = q.dtype
    kc_in, vc_in, ks, vs, kw, vw = [t.reshape(B, L, G, dh) for t in jnp.split(kv, 6, axis=-1)]

    n_cmp = (L - CMP_BLOCK) // CMP_STRIDE + 1
    tok = jnp.arange(n_cmp)[:, None] * CMP_STRIDE + jnp.arange(CMP_BLOCK)[None, :]
    cmp_end = jnp.arange(n_cmp) * CMP_STRIDE + CMP_BLOCK - 1

    def compress(t, pos, w1, w2):
        blk = t[:, tok] + pos[:, None, :]
        blk = blk.transpose(0, 1, 3, 2, 4).reshape(B, n_cmp, G, CMP_BLOCK * dh)
        out = jax.nn.gelu(blk @ w1) @ w2
        return out.transpose(0, 2, 1, 3)

    kc = compress(kc_in, pos_k, wk1, wk2)
    vc = compress(vc_in, pos_v, wv1, wv2)

    n_sel = L // SEL_BLOCK
    n_top = min(SEL_TOPK, n_sel)
    ks_blk = ks.reshape(B, n_sel, SEL_BLOCK, G, dh).transpose(0, 3, 1, 2, 4)
    vs_blk = vs.reshape(B, n_sel, SEL_BLOCK, G, dh).transpose(0, 3, 1, 2, 4)
    ci = jnp.arange(n_cmp)[:, None] * CMP_STRIDE
    sj = jnp.arange(n_sel)[None, :] * SEL_BLOCK
    overlap = ((ci < sj + SEL_BLOCK) & (ci + CMP_BLOCK > sj)).astype(jnp.float32)
    blk_id = jnp.arange(n_sel)

    kw_pad = jnp.pad(kw, ((0, 0), (WINDOW, 0), (0, 0), (0, 0)))
    vw_pad = jnp.pad(vw, ((0, 0), (WINDOW, 0), (0, 0), (0, 0)))

    tbl = rel_bias.astype(jnp.float32)
    tbl_g = tbl.reshape(REL_BUCKETS, G, R).transpose(1, 0, 2)
    bi = jnp.arange(B)[:, None, None, None]
    gi = jnp.arange(G)[None, :, None, None]
    gi5 = jnp.arange(G)[None, :, None, None, None]
    scale = dh ** -0.5

    def head_bias(dist):
        b = tbl[_t5_bucket(dist)]
        return jnp.moveaxis(b, -1, 0).reshape(G, R, *dist.shape)

    n_qb = L // Q_BLOCK
    q_blocks = q.reshape(B, n_qb, Q_BLOCK, G, R, dh).transpose(1, 0, 3, 4, 2, 5)
    g_blocks = gates.reshape(B, n_qb, Q_BLOCK, NSA_HEADS, 3).transpose(1, 0, 2, 3, 4)

    def block_fn(args):
        cidx, qb, gb = args
        t = cidx * Q_BLOCK + jnp.arange(Q_BLOCK)
        qs = qb * scale

        s = jnp.einsum('bgrqd,bgnd->bgrqn', qs, kc).astype(jnp.float32)
        s = s + head_bias(t[:, None] - cmp_end[None, :])
        p_c = _masked_softmax(s, cmp_end[None, :] <= t[:, None])
        o_c = jnp.einsum('bgrqn,bgnd->bgrqd', p_c.astype(dtype), vc)

        imp = jnp.sum(p_c, axis=2) @ overlap
        cur = t // SEL_BLOCK
        forced = (blk_id[None, :] == 0) | (blk_id[None, :] == cur[:, None]) | (blk_id[None, :] == cur[:, None] - 1)
        imp = jnp.where(forced, FORCE_VALUE, imp)
        imp = jnp.where(blk_id[None, :] <= cur[:, None], imp, MASK_VALUE)
        _, sel = lax.top_k(imp, n_top)

        k_g = ks_blk[bi, gi, sel]
        v_g = vs_blk[bi, gi, sel]
        pos = sel[..., None] * SEL_BLOCK + jnp.arange(SEL_BLOCK)
        dist = t[None, None, :, None, None] - pos
        sb = jnp.moveaxis(tbl_g[gi5, _t5_bucket(dist)], -1, 2)
        s = jnp.einsum('bgrqd,bgqnkd->bgrqnk', qs, k_g).astype(jnp.float32) + sb
        n_tok = n_top * SEL_BLOCK
        s = s.reshape(B, G, R, Q_BLOCK, n_tok)
        valid_s = (dist >= 0).reshape(B, G, 1, Q_BLOCK, n_tok)
        p_s = _masked_softmax(s, valid_s)
        o_s = jnp.einsum('bgrqm,bgqmd->bgrqd', p_s.astype(dtype), v_g.reshape(B, G, Q_BLOCK, n_tok, dh))

        kwb = lax.dynamic_slice_in_dim(kw_pad, cidx * Q_BLOCK, WINDOW + Q_BLOCK, axis=1)
        vwb = lax.dynamic_slice_in_dim(vw_pad, cidx * Q_BLOCK, WINDOW + Q_BLOCK, axis=1)
        wpos = cidx * Q_BLOCK - WINDOW + jnp.arange(WINDOW + Q_BLOCK)
        wdist = t[:, None] - wpos[None, :]
        valid_w = (wpos[None, :] >= 0) & (wdist >= 0) & (wdist < WINDOW)
        s = jnp.einsum('bgrqd,bkgd->bgrqk', qs, kwb).astype(jnp.float32) + head_bias(wdist)
        p_w = _masked_softmax(s, valid_w)
        o_w = jnp.einsum('bgrqk,bkgd->bgrqd', p_w.astype(dtype), vwb)

        g = gb.transpose(0, 2, 1, 3).reshape(B, G, R, Q_BLOCK, 3)
        o = o_c * g[..., 0:1] + o_s * g[..., 1:2] + o_w * g[..., 2:3]
        return o.transpose(0, 3, 1, 2, 4).reshape(B, Q_BLOCK, NSA_WIDTH)

    out = lax.map(block_fn, (jnp.arange(n_qb), q_blocks, g_blocks))
    return out.transpose(1, 0, 2, 3).reshape(B, L, NSA_WIDTH)


def _cmul_combine(e1, e2):
    a1r, a1i, b1r, b1i = e1
    a2r, a2i, b2r, b2i = e2
    return (a2r * a1r - a2i * a1i,
            a2r * a1i + a2i * a1r,
            a2r * b1r - a2i * b1i + b2r,
            a2r * b1i + a2i * b1r + b2i)


def _s5(u, a_re, a_im, log_dt, b_re, b_im, c_re, c_im, d_skip, w_glu, b_glu):
    B, L, _ = u.shape
    f32 = jnp.float32
    dt = jnp.exp(log_dt.astype(f32))[:, None]
    ar, ai = a_re.astype(f32), a_im.astype(f32)
    mag = jnp.exp(ar * dt)
    lr, li = mag * jnp.cos(ai * dt), mag * jnp.sin(ai * dt)
    den = ar * ar + ai * ai
    nr, ni = lr - 1.0, li
    fr, fi = (nr * ar + ni * ai) / den, (ni * ar - nr * ai) / den
    br, bim = b_re.astype(f32), b_im.astype(f32)
    bbr = fr[..., None] * br - fi[..., None] * bim
    bbi = fr[..., None] * bim + fi[..., None] * br
    ug = u.astype(f32).reshape(B, L, SSM_GROUPS, SSM_GROUP).transpose(1, 0, 2, 3)
    xr = jnp.einsum('lbgc,gpc->lbgp', ug, bbr)
    xi = jnp.einsum('lbgc,gpc->lbgp', ug, bbi)
    lam_r = jnp.broadcast_to(lr[None, None], xr.shape)
    lam_i = jnp.broadcast_to(li[None, None], xr.shape)
    _, _, hr, hi = lax.associative_scan(_cmul_combine, (lam_r, lam_i, xr, xi), axis=0)
    y = (jnp.einsum('lbgp,gcp->lbgc', hr, c_re.astype(f32))
         - jnp.einsum('lbgp,gcp->lbgc', hi, c_im.astype(f32)))
    y = y.transpose(1, 0, 2, 3).reshape(B, L, SSM_WIDTH) + d_skip.astype(f32) * u.astype(f32)
    y = jax.nn.gelu(y).astype(u.dtype)
    return y * jax.nn.sigmoid(y @ w_glu + b_glu)


def setup_inputs(seed: int = 0) -> dict:
    key = jax.random.key(seed)
    k = jax.random.split(key, 32)
    f32 = jnp.float32
    D, dh = D_MODEL, NSA_HEAD_DIM

    def nrm(i, shape, std):
        return jax.random.normal(k[i], shape, f32) * std

    return {
        'x': nrm(0, (BATCH, SEQ, D), 1.0),
        'c': nrm(1, (BATCH, D), 1.0),
        'w_ada': nrm(2, (DEPTH, D, 3 * D), 0.5 * D ** -0.5),
        'b_ada': nrm(3, (DEPTH, 3 * D), 0.02),
        'w_in': nrm(4, (DEPTH, D, IN_WIDTH), D ** -0.5),
        'rel_bias': nrm(5, (REL_BUCKETS, NSA_HEADS), 0.2),
        'cmp_pos_k': nrm(6, (DEPTH, CMP_BLOCK, dh), 0.1),
        'cmp_pos_v': nrm(7, (DEPTH, CMP_BLOCK, dh), 0.1),
        'w_cmp_k1': nrm(8, (DEPTH, CMP_BLOCK * dh, CMP_HIDDEN), (CMP_BLOCK * dh) ** -0.5),
        'w_cmp_k2': nrm(9, (DEPTH, CMP_HIDDEN, dh), CMP_HIDDEN ** -0.5),
        'w_cmp_v1': nrm(10, (DEPTH, CMP_BLOCK * dh, CMP_HIDDEN), (CMP_BLOCK * dh) ** -0.5),
        'w_cmp_v2': nrm(11, (DEPTH, CMP_HIDDEN, dh), CMP_HIDDEN ** -0.5),
        'ssm_a_re': -0.5 + nrm(12, (DEPTH, SSM_GROUPS, SSM_STATE), 0.01),
        'ssm_a_im': jnp.pi * jnp.arange(SSM_STATE, dtype=f32)[None, None, :] + nrm(13, (DEPTH, SSM_GROUPS, SSM_STATE), 0.01),
        'ssm_log_dt': jax.random.uniform(k[14], (DEPTH, SSM_GROUPS), f32, math.log(DT_MIN), math.log(DT_MAX)),
        'ssm_b_re': nrm(15, (DEPTH, SSM_GROUPS, SSM_STATE, SSM_GROUP), (2 * SSM_GROUP) ** -0.5),
        'ssm_b_im': nrm(16, (DEPTH, SSM_GROUPS, SSM_STATE, SSM_GROUP), (2 * SSM_GROUP) ** -0.5),
        'ssm_c_re': nrm(17, (DEPTH, SSM_GROUPS, SSM_GROUP, SSM_STATE), 0.25),
        'ssm_c_im': nrm(18, (DEPTH, SSM_GROUPS, SSM_GROUP, SSM_STATE), 0.25),
        'ssm_d': nrm(19, (DEPTH, SSM_WIDTH), 1.0),
        'w_glu': nrm(20, (DEPTH, SSM_WIDTH, SSM_WIDTH), SSM_WIDTH ** -0.5),
        'b_glu': nrm(21, (DEPTH, SSM_WIDTH), 0.02),
        'w_branch_nsa': nrm(22, (DEPTH, NSA_WIDTH, D), NSA_WIDTH ** -0.5 * DEEPNORM_BETA),
        'w_branch_ssm': nrm(23, (DEPTH, SSM_WIDTH, D), SSM_WIDTH ** -0.5 * DEEPNORM_BETA),
        'w_out': nrm(24, (DEPTH, D, D), D ** -0.5 * DEEPNORM_BETA),
        'ln_g': 1.0 + nrm(25, (DEPTH, D), 0.02),
        'ln_b': nrm(26, (DEPTH, D), 0.02),
    }


def reference(x, c, w_ada, b_ada, w_in, rel_bias, cmp_pos_k, cmp_pos_v, w_cmp_k1, w_cmp_k2,
              w_cmp_v1, w_cmp_v2, ssm_a_re, ssm_a_im, ssm_log_dt, ssm_b_re, ssm_b_im, ssm_c_re,
              ssm_c_im, ssm_d, w_glu, b_glu, w_branch_nsa, w_branch_ssm, w_out, ln_g, ln_b):
    offs = np.cumsum(IN_SPLITS)[:-1].tolist()
    for i in range(DEPTH):
        mod = c @ w_ada[i] + b_ada[i]
        shift, scale, gate = jnp.split(mod, 3, axis=-1)
        h = _ln(x) * (1.0 + scale[:, None, :]) + shift[:, None, :]

        proj = h @ w_in[i]
        q, kv, ng, za, us, zb, ga, gb = jnp.split(proj, offs, axis=-1)

        o_a = _nsa(q, kv, jax.nn.sigmoid(ng), rel_bias, cmp_pos_k[i], cmp_pos_v[i],
                   w_cmp_k1[i], w_cmp_k2[i], w_cmp_v1[i], w_cmp_v2[i]) * jax.nn.silu(za)
        o_b = _s5(us, ssm_a_re[i], ssm_a_im[i], ssm_log_dt[i], ssm_b_re[i], ssm_b_im[i],
                  ssm_c_re[i], ssm_c_im[i], ssm_d[i], w_glu[i], b_glu[i]) * jax.nn.silu(zb)

        m = jax.nn.sigmoid(ga) * (o_a @ w_branch_nsa[i]) + jax.nn.sigmoid(gb) * (o_b @ w_branch_ssm[i])
        y = m @ w_out[i]

        x = _ln(DEEPNORM_ALPHA * x + gate[:, None, :] * y) * ln_g[i] + ln_b[i]
    return x
```

```python
import numpy as np
import ml_dtypes
import concourse.bass as bass
import concourse.mybir as mybir
from concourse.bass_utils import run_bass_kernel_spmd
from contextlib import ExitStack

F32 = mybir.dt.float32
BF16 = mybir.dt.bfloat16
AF = mybir.ActivationFunctionType
ALU = mybir.AluOpType

DEBUG = False
NEG = -30000.0
LO = 2048
LC = 2048
LT = 4096
D = 2048
KT = 16
C_Q, C_KC, C_VC, C_KS, C_VS, C_KW, C_VW, C_NG, C_ZA, C_US, C_ZB, C_GA, C_GB = (
    0, 1024, 1280, 1536, 1792, 2048, 2304, 2560, 2608, 3632, 4656, 5680, 7728)
ALPHA = 2 ** 0.25
NJ = 16


class Plan:
    ENG = ["pe", "act", "dve", "pool", "sp"]

    def __init__(self, nc, stack):
        self.nc = nc
        self.q = {e: [] for e in self.ENG}
        self.tr = {e: [] for e in self.ENG}
        self.sem = {e: stack.enter_context(nc.semaphore("s_" + e)) for e in self.ENG}
        self.cnt = {e: 0 for e in self.ENG}
        self.ndma = 24
        self.dsem = [stack.enter_context(nc.semaphore("d%d" % i)) for i in range(self.ndma)]
        self.dcnt = [0] * self.ndma
        self.nsw = 46
        self.swsem = [stack.enter_context(nc.semaphore("w%d" % i)) for i in range(self.nsw)]
        self.swnext = 0
        self.dnext = 0
        self.seen = {e: {} for e in self.ENG}
        self.lastw = {}
        self.reads = {}

    def _semobj(self, key):
        if isinstance(key, str):
            return self.sem[key]
        return self.dsem[key] if key < 1000 else self.swsem[key - 1000]

    def _wait(self, eng, key, val):
        if eng == "pe" and key == "pe":
            return
        if self.seen[eng].get(key, 0) >= val:
            return
        self.seen[eng][key] = val
        s = self._semobj(key)
        self.tr[eng].append(("w", key, val))
        self.q[eng].append(lambda e, s=s, val=val: e.wait_ge(s, val))

    def _deps(self, eng, reads, writes):
        for b in reads:
            if b in self.lastw:
                self._wait(eng, *self.lastw[b])
        for b in writes:
            if b in self.lastw:
                self._wait(eng, *self.lastw[b])
            for k, v in self.reads.get(b, {}).items():
                self._wait(eng, k, v)

    def _commit(self, reads, writes, tok):
        for b in writes:
            self.lastw[b] = tok
            self.reads[b] = {}
        for b in reads:
            d = self.reads.setdefault(b, {})
            d[tok[0]] = max(d.get(tok[0], 0), tok[1])

    def op(self, eng, fn, reads=(), writes=()):
        psr = [k for k in reads if isinstance(k, str) and k.startswith("ps")]
        if psr:
            reads = [k for k in reads if k not in psr]
            writes = list(writes) + [k for k in psr if k not in writes]
        self._deps(eng, reads, writes)
        self.cnt[eng] += 1
        v = self.cnt[eng]
        s = self.sem[eng]
        self.tr[eng].append(("i", eng, 1))
        self.q[eng].append(lambda e, fn=fn, s=s: fn(e).then_inc(s, 1))
        self._commit(reads, writes, (eng, v))

    def dma(self, eng, fn, reads=(), writes=()):
        self._deps(eng, reads, writes)
        if eng == "pool":
            k = self.swnext
            self.swnext += 1
            assert k < self.nsw
            s = self.swsem[k]
            self.tr[eng].append(("i", 1000 + k, 16))
            self.q[eng].append(lambda e, fn=fn, s=s: fn(e).then_inc(s, 16))
            self._commit(reads, writes, (1000 + k, 16))
            return
        slot = self.dnext
        self.dnext = (self.dnext + 1) % self.ndma
        if self.dcnt[slot] > 0:
            self._wait(eng, slot, self.dcnt[slot])
        self.dcnt[slot] += 16
        v = self.dcnt[slot]
        s = self.dsem[slot]
        self.tr[eng].append(("i", slot, 16))
        self.q[eng].append(lambda e, fn=fn, s=s: fn(e).then_inc(s, 16))
        self._commit(reads, writes, (slot, v))

    def barrier(self):
        import os
        if os.environ.get("NOBAR"):
            return
        for eng in self.ENG:
            for slot in range(self.ndma):
                if self.dcnt[slot]:
                    self._wait(eng, slot, self.dcnt[slot])
            for k in range(self.swnext):
                self._wait(eng, 1000 + k, 16)
            for e in self.ENG:
                if self.cnt[e] and e != eng:
                    self._wait(eng, e, self.cnt[e])

    def final_wait(self, eng="sp"):
        for slot in range(self.ndma):
            if self.dcnt[slot]:
                self._wait(eng, slot, self.dcnt[slot])
        for k in range(self.swnext):
            self._wait(eng, 1000 + k, 16)
        for e in self.ENG:
            if self.cnt[e]:
                self._wait(eng, e, self.cnt[e])

    def emit(self):
        nc = self.nc
        with nc.Block() as block:
            @block.tensor
            def _(e):
                for f in self.q["pe"]:
                    f(e)

            @block.scalar
            def _(e):
                for f in self.q["act"]:
                    f(e)

            @block.vector
            def _(e):
                for f in self.q["dve"]:
                    f(e)

            @block.gpsimd
            def _(e):
                for f in self.q["pool"]:
                    f(e)

            @block.sync
            def _(e):
                for f in self.q["sp"]:
                    f(e)


def build(stage=99):
    nc = bass.Bass("TRN2", target_bir_lowering=False)
    _uid = [0]

    def _sbuf(name, shape, dt):
        _uid[0] += 1
        return nc.sbuf_tensor("%s_u%d" % (name, _uid[0]), list(shape), dt)

    def din(name, shape, dt=F32):
        return nc.dram_tensor(name, list(shape), dt, kind="ExternalInput").ap()

    def dscr(name, shape, dt=F32):
        kind = "ExternalOutput" if DEBUG else "Internal"
        return nc.dram_tensor(name, list(shape), dt, kind=kind).ap()

    x_own = din("x_own", [LO, D])
    x_ctx = din("x_ctx", [LC, D])
    c2 = din("c2", [128, KT, 2])
    b_ada_l = din("b_ada_l", [128, 48])
    w_ada = din("w_ada", [D, 3 * D])
    w_in = din("w_in", [D, 9776])
    flag = din("flag", [128, 1])
    ident_f = din("ident_f", [128, 128])
    ident_b = din("ident_b", [128, 128], BF16)
    w_bn = din("w_branch_nsa", [1024, D])
    w_bs = din("w_branch_ssm", [1024, D])
    w_out = din("w_out", [D, D])
    ln_g = din("ln_g", [1, D])
    ln_b = din("ln_b", [1, D])
    out = nc.dram_tensor("out", [LO, D], F32, kind="ExternalOutput").ap()

    qT_d = dscr("qT_d", [4, 16, 64, 4, 128], BF16)
    kcT_d = dscr("kcT_d", [256, LT], BF16)
    vcT_d = dscr("vcT_d", [256, LT], BF16)
    ksT_d = dscr("ksT_d", [256, LT], BF16)
    kwT_d = dscr("kwT_d", [256, LT], BF16)
    vs_d = dscr("vs_d", [LT, 4, 65], BF16)
    vw_d = dscr("vw_d", [LT, 4, 65], BF16)
    usT_d = dscr("usT_d", [1024, LT], BF16)
    gT_d = dscr("gT_d", [48, LO], F32)
    zaT_d = dscr("zaT_d", [4, 16, 64, 4, 128], BF16)
    zbT_d = dscr("zbT_d", [1024, LO], BF16)
    gaT_d = dscr("gaT_d", [D, LO], BF16)
    gbT_d = dscr("gbT_d", [D, LO], BF16)
    oaT_d = dscr("oaT_d", [4, 16, 64, 4, 128], BF16)
    obT_d = dscr("obT_d", [1024, LO], BF16)
    mT_d = dscr("mT_d", [D, LO], BF16)
    grow_d = dscr("grow_d", [16, 128], F32)


    w_cmp_k1 = din("w_cmp_k1", [2048, 128]); w_cmp_k2 = din("w_cmp_k2", [128, 64])
    w_cmp_v1 = din("w_cmp_v1", [2048, 128]); w_cmp_v2 = din("w_cmp_v2", [128, 64])
    posT2_k = din("posT2_k", [64, 32, 2]); posT2_v = din("posT2_v", [64, 32, 2])
    fvc = din("fvc", [128, 2])
    tw_raw = din("tw_raw", [3, 128, 16, 128]); ts_raw = din("ts_raw", [2, 128, 16, 128]); tn_raw = din("tn_raw", [16, 16, 128])
    stepb = din("stepb", [128, 16, 2]); SM2_d = din("SM2_d", [16, 376], BF16)
    rb31 = din("rb31", [1, 16])
    A_sel = din("A_sel", [128, 16, 64]); B_sel = din("B_sel", [128, 16, 64]); ovl_d = din("ovl_d", [128, 2, 65])
    EM_d = din("EM_d", [64, LT], BF16); SM_d = din("SM_d", [48, 3072]); ones_d = din("ones_d", [128, 128])
    kcmpT_d = dscr("kcmpT_d", [4, 64, 256], BF16); vcmp_d = dscr("vcmp_d", [4, 256, 65], F32)
    twh_d = dscr("twh_d", [3, 128, 16, 128], BF16); twl_d = dscr("twl_d", [3, 128, 16, 128], BF16)
    tsh_d = dscr("tsh_d", [2, 128, 16, 128], BF16); tsl_d = dscr("tsl_d", [2, 128, 16, 128], BF16)
    tnh_d = dscr("tnh_d", [16, 16, 128], BF16); tnl_d = dscr("tnl_d", [16, 16, 128], BF16)

    a_re2 = din("a_re2", [128, 32]); a_im2 = din("a_im2", [128, 32]); ldt2 = din("ldt2", [128, 32])
    b_re2 = din("b_re2", [128, 32, 16]); b_im2 = din("b_im2", [128, 32, 16])
    c_re2 = din("c_re2", [128, 32, 16]); c_im2 = din("c_im2", [128, 32, 16])
    dsk = din("dsk", [128, 8]); bglu = din("bglu", [128, 8])
    selm8_d = din("selm8_d", [128, 8, 240], BF16); TM_d = din("TM_d", [128, 2, 256])
    w_glu = din("w_glu", [1024, 1024])
    y_d = dscr("y_d", [LO, 1024], F32)
    ugo_d = dscr("ugo_d", [64, 128, 2, 128], BF16)

    w_in_r = w_in.rearrange("(kt p) n -> p kt n", p=128)
    w_ada_r = w_ada.rearrange("(kt p) n -> p kt n", p=128)

    with ExitStack() as st:
        P = Plan(nc, st)
        sb = lambda name, shape, dt=F32: st.enter_context(_sbuf(name, shape, dt))
        PS = [st.enter_context(nc.psum_tensor("ps%d" % i, [128, 512], F32)) for i in range(8)]
        identf = sb("identf", [128, 128])
        identb = sb("identb", [128, 128], BF16)
        flagt = sb("flagt", [128, 1])
        mod = sb("mod", [128, 48])
        scale1 = sb("scale1", [128, 16])
        P.dma("sp", lambda e: e.dma_start(out=identf[:], in_=ident_f[:, :]), writes=["identf"])
        P.dma("sp", lambda e: e.dma_start(out=identb[:], in_=ident_b[:, :]), writes=["identb"])
        P.dma("sp", lambda e: e.dma_start(out=flagt[:], in_=flag[:, :]), writes=["flagt"])

        with ExitStack() as s0:
            sb0 = lambda name, shape, dt=F32: s0.enter_context(_sbuf(name, shape, dt))
            c2t = sb0("c2t", [128, KT, 2])
            badat = sb0("badat", [128, 48])
            wb = [sb0("wada%d" % i, [128, KT, 512]) for i in range(2)]
            P.dma("sp", lambda e: e.dma_start(out=c2t[:], in_=c2[:, :, :]), writes=["c2t"])
            P.dma("sp", lambda e: e.dma_start(out=badat[:], in_=b_ada_l[:, :]), writes=["badat"])
            for ch in range(12):
                w = wb[ch % 2]
                wk = "wada%d" % (ch % 2)
                P.dma("sp", lambda e, w=w, ch=ch: e.dma_start(out=w[:], in_=w_ada_r[:, :, ch * 512:(ch + 1) * 512]),
                      writes=[wk])
                for mt in range(4):
                    t = ch * 4 + mt
                    for kt in range(KT):
                        P.op("pe", lambda e, w=w, mt=mt, kt=kt, t=t: e.matmul(
                            PS[0][:, 2 * t:2 * t + 2], lhsT=w[:, kt, mt * 128:(mt + 1) * 128], rhs=c2t[:, kt, :],
                            start=(kt == 0), stop=(kt == KT - 1)), reads=[wk, "c2t"], writes=["ps0"])
            P.op("dve", lambda e: e.tensor_tensor(out=mod[:], in0=PS[0][:, 0:96:2], in1=badat[:], op=ALU.add),
                 reads=["ps0", "badat"], writes=["mod"])
            P.op("dve", lambda e: e.tensor_scalar(out=scale1[:], in0=mod[:, 16:32], scalar1=1.0, scalar2=None,
                                                  op0=ALU.add), reads=["mod"], writes=["scale1"])
            gr = sb0("gr", [16, 128])
            P.op("pe", lambda e: e.transpose(PS[1][0:16, 0:128], mod[:, 32:48], identf[:]),
                 reads=["mod", "identf"], writes=["ps1"])
            P.op("dve", lambda e: e.tensor_copy(out=gr[:], in_=PS[1][0:16, 0:128]), reads=["ps1"], writes=["gr"])
            P.dma("sp", lambda e: e.dma_start(out=grow_d[:, :], in_=gr[:]), reads=["gr"], writes=["grow_d"])

        P.barrier()

        def ln_phase(xsrc, hT, hkey, ntiles, s1):
            sb1 = lambda name, shape, dt=F32: s1.enter_context(_sbuf(name, shape, dt))
            xb = [sb1("xb%d" % i, [128, D]) for i in range(2)]
            xn = [sb1("xn%d" % i, [128, D]) for i in range(2)]
            stt = [sb1("stt%d" % i, [128, 4, 6]) for i in range(2)]
            mv = [sb1("mv%d" % i, [128, 8]) for i in range(2)]
            def lnA(ti):
                r = ti % 2
                xk, xnk, sk, mk = "xb%d" % r, "xn%d" % r, "stt%d" % r, "mv%d" % r
                x_, xn_, st_, mv_ = xb[r], xn[r], stt[r], mv[r]
                P.dma("sp", lambda e, x_=x_, ti=ti: e.dma_start(out=x_[:], in_=xsrc[ti * 128:(ti + 1) * 128, :]),
                      writes=[xk])
                for c in range(4):
                    P.op("dve", lambda e, x_=x_, st_=st_, c=c: e.bn_stats(out=st_[:, c, :], in_=x_[:, c * 512:(c + 1) * 512]),
                         reads=[xk], writes=[sk])
                P.op("dve", lambda e, st_=st_, mv_=mv_: e.bn_aggr(out=mv_[:, 0:2], in_=st_[:].rearrange("p a b -> p (a b)")),
                     reads=[sk], writes=[mk])
                P.op("dve", lambda e, mv_=mv_: e.tensor_scalar(out=mv_[:, 2:3], in0=mv_[:, 1:2], scalar1=1e-5, scalar2=None,
                                                               op0=ALU.add), reads=[mk], writes=[mk])
                P.op("act", lambda e, mv_=mv_: e.activation(out=mv_[:, 3:4], in_=mv_[:, 2:3], func=AF.Sqrt),
                     reads=[mk], writes=[mk])
                P.op("dve", lambda e, mv_=mv_: e.reciprocal(out=mv_[:, 4:5], in_=mv_[:, 3:4]), reads=[mk], writes=[mk])
                P.op("dve", lambda e, mv_=mv_: e.tensor_scalar(out=mv_[:, 5:6], in0=mv_[:, 0:1], scalar1=mv_[:, 4:5],
                                                               scalar2=-1.0, op0=ALU.mult, op1=ALU.mult),
                     reads=[mk], writes=[mk])
                P.op("act", lambda e, x_=x_, xn_=xn_, mv_=mv_: e.activation(out=xn_[:], in_=x_[:], func=AF.Identity,
                                                                            scale=mv_[:, 4:5], bias=mv_[:, 5:6]),
                     reads=[xk, mk], writes=[xnk])
            def lnB(ti):
                r = ti % 2
                xnk = "xn%d" % r
                xn_ = xn[r]
                for grp in range(4):
                    pb = 2 + (ti * 4 + grp) % 2
                    pk = "ps%d" % pb
                    for q in range(4):
                        kt = grp * 4 + q
                        P.op("pe", lambda e, xn_=xn_, kt=kt, q=q, pb=pb: e.transpose(
                            PS[pb][:, q * 128:(q + 1) * 128], xn_[:, kt * 128:(kt + 1) * 128], identf[:]),
                            reads=[xnk, "identf"], writes=[pk])
                    for q in range(4):
                        kt = grp * 4 + q
                        if grp % 2 == 0:
                            P.op("dve", lambda e, kt=kt, q=q, pb=pb, ti=ti: e.tensor_scalar(
                                out=hT[:, kt, ti * 128:(ti + 1) * 128], in0=PS[pb][:, q * 128:(q + 1) * 128],
                                scalar1=scale1[:, kt:kt + 1], scalar2=mod[:, kt:kt + 1], op0=ALU.mult, op1=ALU.add),
                                reads=[pk, "scale1", "mod"], writes=[(hkey, ti, kt)])
                        else:
                            P.op("act", lambda e, kt=kt, q=q, pb=pb, ti=ti: e.activation(
                                out=hT[:, kt, ti * 128:(ti + 1) * 128], in_=PS[pb][:, q * 128:(q + 1) * 128],
                                func=AF.Identity, scale=scale1[:, kt:kt + 1], bias=mod[:, kt:kt + 1]),
                                reads=[pk, "scale1", "mod"], writes=[(hkey, ti, kt)])
            lnA(0)
            for ti in range(ntiles):
                if ti + 1 < ntiles:
                    lnA(ti + 1)
                lnB(ti)

        cnt = {"w": 0, "ps": 0, "ev": 0}

        def proj_pass(hT, hkey, ntok, segs, s1, is_ctx):
            sb1 = lambda name, shape, dt=F32: s1.enter_context(_sbuf(name, shape, dt))
            wbuf = [sb1("wbuf%d" % i, [128, KT, 512], BF16) for i in range(2)]
            evb = [sb1("evb%d" % i, [128, 512], BF16) for i in range(3)]
            evf = [sb1("evf%d" % i, [128, 512], F32) for i in range(2)]
            vst = [sb1("vst%d" % i, [128, 4, 65], BF16) for i in range(2)]
            for i in range(2):
                P.op("pool", lambda e, i=i: e.memset(vst[i][:], 1.0), writes=["vst%d" % i])
            for (kind, col0, ncols, dst, func, scl, odt, tok0) in segs:
                for c0 in range(0, ncols, 512):
                    ncc = min(512, ncols - c0)
                    wi = cnt["w"] % 2
                    cnt["w"] += 1
                    w = wbuf[wi]
                    wk = "wbuf%d" % wi
                    P.dma("pool", lambda e, w=w, a=col0 + c0, n=ncc: e.dma_start(out=w[:, :, 0:n], in_=w_in_r[:, :, a:a + n]),
                          writes=[wk])
                    if kind in ("fm", "fmq"):
                        for mt in range((ncc + 127) // 128):
                            m = min(128, ncc - mt * 128)
                            for n in range(ntok // 512):
                                pb = 4 + cnt["ps"] % 4
                                cnt["ps"] += 1
                                pk = "ps%d" % pb
                                for kt in range(KT):
                                    P.op("pe", lambda e, w=w, kt=kt, mt=mt, m=m, n=n, pb=pb: e.matmul(
                                        PS[pb][0:m, :], lhsT=w[:, kt, mt * 128:mt * 128 + m],
                                        rhs=hT[:, kt, n * 512:(n + 1) * 512], start=(kt == 0), stop=(kt == KT - 1)),
                                        reads=[wk] + [(hkey, n * 4 + i_, kt) for i_ in range(4)], writes=[pk])
                                ei = cnt["ev"]
                                cnt["ev"] += 1
                                if odt == BF16:
                                    ev, ek = evb[ei % 3], "evb%d" % (ei % 3)
                                else:
                                    ev, ek = evf[ei % 2], "evf%d" % (ei % 2)
                                rd = [pk] + (["flagt"] if scl is not None and not isinstance(scl, float) else [])
                                if func is None and scl is None and ei % 2 == 0:
                                    P.op("dve", lambda e, ev=ev, m=m, pb=pb: e.tensor_copy(out=ev[0:m, :], in_=PS[pb][0:m, :]),
                                         reads=rd, writes=[ek])
                                else:
                                    kw = {} if scl is None else {"scale": scl}
                                    P.op("act", lambda e, ev=ev, m=m, pb=pb, f=(func or AF.Identity), kw=kw: e.activation(
                                        out=ev[0:m, :], in_=PS[pb][0:m, :], func=f, **kw), reads=rd, writes=[ek])
                                r0 = c0 + mt * 128
                                if kind == "fmq":
                                    h0 = r0 // 64
                                    for hh in range(2):
                                        g_, rr_ = (h0 + hh) // 4, (h0 + hh) % 4
                                        P.dma("sp", lambda e, ev=ev, n=n, dst=dst, hh=hh, g_=g_, rr_=rr_: e.dma_start(
                                            out=dst[g_, 4 * n:4 * n + 4, :, rr_, :].rearrange("j d q -> d j q"),
                                            in_=ev[hh * 64:(hh + 1) * 64, :].rearrange("p (j q) -> p j q", q=128)),
                                            reads=[ek], writes=[(id(dst), r0, n, hh)])
                                else:
                                    P.dma("sp", lambda e, ev=ev, m=m, r0=r0, n=n, dst=dst, tok0=tok0: e.dma_start(
                                        out=dst[r0:r0 + m, tok0 + n * 512: tok0 + (n + 1) * 512], in_=ev[0:m, :]),
                                        reads=[ek], writes=[(id(dst), r0, n, tok0)])
                    else:
                        for tt in range(ntok // 128):
                            pb = 4 + cnt["ps"] % 4
                            cnt["ps"] += 1
                            pk = "ps%d" % pb
                            for kt in range(KT):
                                P.op("pe", lambda e, w=w, kt=kt, tt=tt, pb=pb: e.matmul(
                                    PS[pb][:, 0:256], lhsT=hT[:, kt, tt * 128:(tt + 1) * 128], rhs=w[:, kt, 0:256],
                                    start=(kt == 0), stop=(kt == KT - 1)), reads=[wk, (hkey, tt, kt)], writes=[pk])
                            vi = cnt["ev"] % 2
                            cnt["ev"] += 1
                            v_, vk = vst[vi], "vst%d" % vi
                            if is_ctx:
                                P.op("act", lambda e, v_=v_, pb=pb: e.activation(
                                    out=v_[:, :, 0:64], in_=PS[pb][:, 0:256].rearrange("p (g d) -> p g d", g=4),
                                    func=AF.Identity, scale=flagt[:, 0:1]), reads=[pk, "flagt"], writes=[vk])
                                P.op("dve", lambda e, v_=v_: e.tensor_copy(
                                    out=v_[:, :, 64:65], in_=flagt[:, 0:1].unsqueeze(1).broadcast_to([128, 4, 1])),
                                    reads=["flagt"], writes=[vk])
                            else:
                                P.op("act", lambda e, v_=v_, pb=pb: e.activation(
                                    out=v_[:, :, 0:64], in_=PS[pb][:, 0:256].rearrange("p (g d) -> p g d", g=4),
                                    func=AF.Identity), reads=[pk], writes=[vk])
                            P.dma("sp", lambda e, v_=v_, tt=tt, dst=dst, tok0=tok0: e.dma_start(
                                out=dst[tok0 + tt * 128: tok0 + (tt + 1) * 128, :, :], in_=v_[:]),
                                reads=[vk], writes=[(id(dst), tt, tok0)])

        with ExitStack() as s1:
            hT = s1.enter_context(_sbuf("hTc", [128, KT, LC], BF16))
            if True:
                s2 = s1
                ln_phase(x_ctx, hT, "hTc", LC // 128, s2)
                segs = [
                    ("fm", C_KC, 256, kcT_d, None, None, BF16, 0),
                    ("fm", C_VC, 256, vcT_d, None, None, BF16, 0),
                    ("fm", C_KS, 256, ksT_d, None, None, BF16, 0),
                    ("fm", C_KW, 256, kwT_d, None, None, BF16, 0),
                    ("fm", C_US, 1024, usT_d, None, flagt[:, 0:1], BF16, 0),
                    ("tm", C_VS, 256, vs_d, None, None, BF16, 0),
                    ("tm", C_VW, 256, vw_d, None, None, BF16, 0),
                ]
                proj_pass(hT, "hTc", LC, segs, s2, True)
        P.barrier()
        if stage >= 2:
            with ExitStack() as s1:
                hT = s1.enter_context(_sbuf("hTo", [128, KT, LO], BF16))
                if True:
                    s2 = s1
                    ln_phase(x_own, hT, "hTo", LO // 128, s2)
                    segs = [
                        ("fmq", C_Q, 1024, qT_d, None, 0.125, BF16, 0),
                        ("fm", C_KC, 256, kcT_d, None, None, BF16, LC),
                        ("fm", C_VC, 256, vcT_d, None, None, BF16, LC),
                        ("fm", C_KS, 256, ksT_d, None, None, BF16, LC),
                        ("fm", C_KW, 256, kwT_d, None, None, BF16, LC),
                        ("fm", C_US, 1024, usT_d, None, None, BF16, LC),
                        ("tm", C_VS, 256, vs_d, None, None, BF16, LC),
                        ("tm", C_VW, 256, vw_d, None, None, BF16, LC),
                        ("fm", C_NG, 48, gT_d, AF.Sigmoid, None, F32, 0),
                        ("fmq", C_ZA, 1024, zaT_d, AF.Silu, None, BF16, 0),
                        ("fm", C_ZB, 1024, zbT_d, AF.Silu, None, BF16, 0),
                        ("fm", C_GA, 2048, gaT_d, AF.Sigmoid, None, BF16, 0),
                        ("fm", C_GB, 2048, gbT_d, AF.Sigmoid, None, BF16, 0),
                    ]
                    proj_pass(hT, "hTo", LO, segs, s2, False)
        P.barrier()
        V = lambda fn, r=(), w=(): P.op("dve", fn, reads=r, writes=w)
        A = lambda fn, r=(), w=(): P.op("act", fn, reads=r, writes=w)
        T = lambda fn, r=(), w=(): P.op("pe", fn, reads=r, writes=w)
        G = lambda fn, r=(), w=(): P.op("pool", fn, reads=r, writes=w)
        LD = lambda fn, r=(), w=(): P.dma("sp", fn, reads=r, writes=w)
        LDA = lambda fn, r=(), w=(): P.dma("act", fn, reads=r, writes=w)

        if stage >= 4:
            s4 = ExitStack()
            s4a = ExitStack()
            if True:
                sb4 = lambda name, shape, dt=F32: s4.enter_context(_sbuf(name, shape, dt))
                sb4a = lambda name, shape, dt=F32: s4a.enter_context(_sbuf(name, shape, dt))
                cr_ = sb4("cr_", [128, 32, 16])
                ci_ = sb4("ci_", [128, 32, 16])
                TMt = sb4("TMt", [128, 2, 256])
                bbr = sb4("bbr", [128, 32, 16])
                bbi = sb4("bbi", [128, 32, 16])
                pwr = sb4("pwr", [128, 32, 17])
                pwi = sb4("pwi", [128, 32, 17])
                qwr = sb4("qwr", [128, 32, 16])
                qwi = sb4("qwi", [128, 32, 16])
                L2A = sb4("L2A", [128, 2, 32])
                L2B = sb4("L2B", [128, 2, 32])
                HL = sb4("HL", [128, 2, 32, 256])
                tc_ = sb4("tc_", [128, 2, 32])
                m1 = sb4("m1", [128, 2, 32])
                m2 = sb4("m2", [128, 2, 32])
                SP_ = "ssmp"
                def Vp(fn): V(fn, [SP_], [SP_])
                def Ap(fn): A(fn, [SP_], [SP_])
                ar = sb4a("ar", [128, 32]); ai = sb4a("ai", [128, 32]); ldt = sb4a("ldt", [128, 32])
                br_ = sb4a("br_", [128, 32, 16]); bi_ = sb4a("bi_", [128, 32, 16])
                pass
                for t_, d_ in ((ar, a_re2), (ai, a_im2), (ldt, ldt2)):
                    LD(lambda e, t_=t_, d_=d_: e.dma_start(out=t_[:], in_=d_[:, :]), w=[SP_])
                for t_, d_ in ((br_, b_re2), (bi_, b_im2), (cr_, c_re2), (ci_, c_im2)):
                    LD(lambda e, t_=t_, d_=d_: e.dma_start(out=t_[:], in_=d_[:, :, :]), w=[SP_])
                selm8 = sb4a("selm8", [128, 8, 240], BF16)
                LD(lambda e: e.dma_start(out=selm8[:], in_=selm8_d[:, :, :]), w=["selm8"])
                LD(lambda e: e.dma_start(out=TMt[:], in_=TM_d[:, :, :]), w=["TMt"])
                dt_ = sb4a("dt_", [128, 32]); mag = sb4a("mag", [128, 32]); th = sb4a("th", [128, 32])
                u_ = sb4a("u_", [128, 32]); ki = sb4a("ki", [128, 32], mybir.dt.int32); kf = sb4a("kf", [128, 32]); gt_ = sb4a("gt_", [128, 32])
                rr_ = sb4a("rr_", [128, 32]); sn = sb4a("sn", [128, 32]); cs = sb4a("cs", [128, 32])
                lr = sb4a("lr", [128, 32]); li = sb4a("li", [128, 32])
                t1 = sb4a("t1", [128, 32]); t2 = sb4a("t2", [128, 32]); t3 = sb4a("t3", [128, 32])
                fr = sb4a("fr", [128, 32]); fi = sb4a("fi", [128, 32])
                q1r = sb4a("q1r", [128, 32]); q1i = sb4a("q1i", [128, 32])
                pass
                tb1 = sb4a("tb1", [128, 32, 16]); tb2 = sb4a("tb2", [128, 32, 16])
                pass
                pass
                pass
                TT = lambda o, a, b, op: Vp(lambda e: e.tensor_tensor(out=o, in0=a, in1=b, op=op))
                Ap(lambda e: e.activation(out=dt_[:], in_=ldt[:], func=AF.Exp))
                TT(t1[:], ar[:], dt_[:], ALU.mult)
                Ap(lambda e: e.activation(out=mag[:], in_=t1[:], func=AF.Exp))
                TT(th[:], ai[:], dt_[:], ALU.mult)
                C1, C2 = 6.28125, 0.0019353071795864769

                def sinred(out, shift):
                    Vp(lambda e: e.tensor_scalar(out=u_[:], in0=th[:], scalar1=1.0 / (2 * np.pi), scalar2=0.5 + shift / (2 * np.pi),
                                                 op0=ALU.mult, op1=ALU.add))
                    Vp(lambda e: e.tensor_copy(out=ki[:], in_=u_[:]))
                    Vp(lambda e: e.tensor_copy(out=kf[:], in_=ki[:]))
                    TT(gt_[:], kf[:], u_[:], ALU.is_gt)
                    TT(kf[:], kf[:], gt_[:], ALU.subtract)
                    Vp(lambda e: e.tensor_scalar(out=rr_[:], in0=th[:], scalar1=float(shift), scalar2=None, op0=ALU.add))
                    Vp(lambda e: e.scalar_tensor_tensor(out=rr_[:], in0=kf[:], scalar=-C1, in1=rr_[:], op0=ALU.mult, op1=ALU.add))
                    Vp(lambda e: e.scalar_tensor_tensor(out=rr_[:], in0=kf[:], scalar=-C2, in1=rr_[:], op0=ALU.mult, op1=ALU.add))
                    Vp(lambda e: e.tensor_scalar(out=rr_[:], in0=rr_[:], scalar1=-3.141592, scalar2=3.141592, op0=ALU.max, op1=ALU.min))
                    Ap(lambda e: e.activation(out=out, in_=rr_[:], func=AF.Sin))

                sinred(sn[:], 0.0)
                sinred(cs[:], np.pi / 2)
                TT(lr[:], mag[:], cs[:], ALU.mult)
                TT(li[:], mag[:], sn[:], ALU.mult)
                TT(t1[:], ar[:], ar[:], ALU.mult); TT(t2[:], ai[:], ai[:], ALU.mult); TT(t1[:], t1[:], t2[:], ALU.add)
                Vp(lambda e: e.reciprocal(out=t3[:], in_=t1[:]))
                Vp(lambda e: e.tensor_scalar(out=u_[:], in0=lr[:], scalar1=-1.0, scalar2=None, op0=ALU.add))
                TT(t1[:], u_[:], ar[:], ALU.mult); TT(t2[:], li[:], ai[:], ALU.mult); TT(t1[:], t1[:], t2[:], ALU.add); TT(fr[:], t1[:], t3[:], ALU.mult)
                TT(t1[:], li[:], ar[:], ALU.mult); TT(t2[:], u_[:], ai[:], ALU.mult); TT(t1[:], t1[:], t2[:], ALU.subtract); TT(fi[:], t1[:], t3[:], ALU.mult)
                bc = lambda x: x[:].unsqueeze(2).broadcast_to([128, 32, 16])
                TT(tb1[:], br_[:], bc(fr), ALU.mult); TT(tb2[:], bi_[:], bc(fi), ALU.mult); TT(bbr[:], tb1[:], tb2[:], ALU.subtract)
                TT(tb1[:], bi_[:], bc(fr), ALU.mult); TT(tb2[:], br_[:], bc(fi), ALU.mult); TT(bbi[:], tb1[:], tb2[:], ALU.add)
                TT(t1[:], lr[:], lr[:], ALU.mult); TT(t2[:], li[:], li[:], ALU.mult); TT(t1[:], t1[:], t2[:], ALU.add)
                Vp(lambda e: e.reciprocal(out=t3[:], in_=t1[:]))
                TT(q1r[:], lr[:], t3[:], ALU.mult)
                Vp(lambda e: e.scalar_tensor_tensor(out=q1i[:], in0=li[:], scalar=-1.0, in1=t3[:], op0=ALU.mult, op1=ALU.mult))
                def powers(pr_, pi_, n, xr, xi):
                    Vp(lambda e: e.memset(pr_[:, :, 0:1], 1.0)); Vp(lambda e: e.memset(pi_[:, :, 0:1], 0.0))
                    for k in range(1, n):
                        a_r, a_i = pr_[:, :, k - 1], pi_[:, :, k - 1]
                        TT(t1[:], a_r, xr[:], ALU.mult); TT(t2[:], a_i, xi[:], ALU.mult); TT(pr_[:, :, k], t1[:], t2[:], ALU.subtract)
                        TT(t1[:], a_r, xi[:], ALU.mult); TT(t2[:], a_i, xr[:], ALU.mult); TT(pi_[:, :, k], t1[:], t2[:], ALU.add)
                powers(pwr, pwi, 17, lr, li)
                powers(qwr, qwi, 16, q1r, q1i)
                Vp(lambda e: e.tensor_copy(out=L2A[:, 0, :], in_=pwr[:, :, 16])); Vp(lambda e: e.tensor_copy(out=L2A[:, 1, :], in_=pwr[:, :, 16]))
                Vp(lambda e: e.tensor_scalar(out=L2B[:, 0, :], in0=pwi[:, :, 16], scalar1=-1.0, scalar2=None, op0=ALU.mult))
                Vp(lambda e: e.tensor_copy(out=L2B[:, 1, :], in_=pwi[:, :, 16]))

                pass
                pass
                Ablk = sb4a("Ablk", [128, 4, 2, 256]); ta1 = sb4a("ta1", [128, 4, 256]); ta2 = sb4a("ta2", [128, 4, 256])
                v4 = lambda x: x.rearrange("p g (i c) -> p g i c", c=16)

                def build_A(gp0, Ablk, ta1, ta2):
                    bi4 = lambda x: x[:, gp0:gp0 + 4, :].unsqueeze(3).broadcast_to([128, 4, 16, 16])
                    bc4 = lambda x: x[:, gp0:gp0 + 4, :].unsqueeze(2).broadcast_to([128, 4, 16, 16])
                    rw = ["Ablk", SP_, "ta1", "ta2"]
                    V(lambda e: e.tensor_tensor(out=v4(ta1[:]), in0=bi4(qwr), in1=bc4(bbr), op=ALU.mult), rw, rw)
                    V(lambda e: e.tensor_tensor(out=v4(ta2[:]), in0=bi4(qwi), in1=bc4(bbi), op=ALU.mult), rw, rw)
                    V(lambda e: e.tensor_tensor(out=Ablk[:, :, 0, :], in0=ta1[:], in1=ta2[:], op=ALU.subtract), rw, rw)
                    V(lambda e: e.tensor_tensor(out=v4(ta1[:]), in0=bi4(qwr), in1=bc4(bbi), op=ALU.mult), rw, rw)
                    V(lambda e: e.tensor_tensor(out=v4(ta2[:]), in0=bi4(qwi), in1=bc4(bbr), op=ALU.mult), rw, rw)
                    V(lambda e: e.tensor_tensor(out=Ablk[:, :, 1, :], in0=ta1[:], in1=ta2[:], op=ALU.add), rw, rw)

                if True:
                    sb5p = sb4a
                    usb = [sb5p("usb%d" % i, [128, LT], BF16) for i in range(2)]
                    usp = [sb5p("usp%d" % i, [128, 16, 256], BF16) for i in range(2)]
                    ATg = [sb5p("ATg%d" % i, [128, 4, 64], BF16) for i in range(2)]
                    Ug = [sb5p("Ug%d" % i, [128, 2, 256], BF16) for i in range(2)]
                    for gp in range(32):
                        if gp % 4 == 0:
                            build_A(gp, Ablk, ta1, ta2)
                            ct = gp // 4
                            ub0 = usb[ct % 2]; uk0 = "usb%d" % (ct % 2)
                            LD(lambda e, ub0=ub0, ct=ct: e.dma_start(out=ub0[:], in_=usT_d[ct * 128:(ct + 1) * 128, :]), w=[uk0])
                            ub = usp[ct % 2]; uk = "usp%d" % (ct % 2)
                            G(lambda e, ub=ub, ub0=ub0: e.tensor_copy(out=ub[:], in_=ub0[:].rearrange("p (k i) -> p i k", i=16)), [uk0], [uk])
                        gpl = gp % 4
                        for g2 in range(2):
                            g = 2 * gp + g2
                            pr = slice(64 * g2, 64 * g2 + 64)
                            at, ug = ATg[g2], Ug[g2]
                            kat, kug = "ATg%d" % g2, "Ug%d" % g2
                            pa, pu = g2, 2 + g2
                            for a in range(2):
                                for ri in range(2):
                                    c0 = (a * 2 + ri) * 64
                                    T(lambda e, pa=pa, c0=c0, pr=pr, gpl=gpl, ri=ri, a=a: e.transpose(
                                        PS[pa][:, c0:c0 + 64], Ablk[pr, gpl, ri, a * 128:(a + 1) * 128], identf[pr, pr]),
                                        ["Ablk", "identf"], ["ps%d" % pa])
                            A(lambda e, at=at, pa=pa: e.activation(out=at[:].rearrange("p k c -> p (k c)"), in_=PS[pa][:, 0:256], func=AF.Identity),
                              ["ps%d" % pa], [kat])
                            g8 = g % 8
                            for a in range(2):
                                for ip in range(8):
                                    T(lambda e, pu=pu, a=a, ip=ip, g8=g8, ub=ub: e.matmul(
                                        PS[pu][:, a * 256:(a + 1) * 256], lhsT=selm8[:, g8, 112 - 16 * ip:112 - 16 * ip + 128],
                                        rhs=ub[:, 8 * a + ip, :], start=(ip == 0), stop=(ip == 7)), ["selm8", uk], ["ps%d" % pu])
                            V(lambda e, ug=ug, pu=pu: e.tensor_copy(out=ug[:].rearrange("p a k -> p (a k)"), in_=PS[pu][:, :]), ["ps%d" % pu], [kug])
                            LD(lambda e, ug=ug, g=g: e.dma_start(out=ugo_d[g], in_=ug[:, :, 128:256]), r=[kug], w=[("ugo", g)])
                            for ri in range(2):
                                for a in range(2):
                                    T(lambda e, pr=pr, ri=ri, a=a, at=at, ug=ug: e.matmul(
                                        PS[4][pr, ri * 256:(ri + 1) * 256], lhsT=at[:, a * 2 + ri, :], rhs=ug[:, a, :],
                                        start=(a == 0), stop=(a == 1)), [kat, kug], ["ps4"])
                        A(lambda e, gp=gp: e.activation(out=HL[:, :, gp, :], in_=PS[4][:, :].rearrange("p (r k) -> p r k", r=2), func=AF.Identity),
                          ["ps4"], ["HL"])
            s4a.close()
            P.barrier()
            def rec_gen():
                K_ = ["HL", SP_, "rec"]
                for k in range(255):
                    if k == 0:
                        src = HL[:, :, :, 0]
                    else:
                        G(lambda e, k=k: e.tensor_tensor(out=tc_[:], in0=HL[:, :, :, k - 1], in1=HL[:, :, :, k], op=ALU.add), K_, K_)
                        src = tc_[:]
                    G(lambda e, src=src: e.tensor_tensor(out=m1[:], in0=src, in1=L2A[:], op=ALU.mult), K_, K_)
                    s0 = HL[:, 0, :, 0] if k == 0 else tc_[:, 0, :]
                    s1_ = HL[:, 1, :, 0] if k == 0 else tc_[:, 1, :]
                    G(lambda e, s1_=s1_: e.tensor_tensor(out=m2[:, 0, :], in0=s1_, in1=L2B[:, 0, :], op=ALU.mult), K_, K_)
                    G(lambda e, s0=s0: e.tensor_tensor(out=m2[:, 1, :], in0=s0, in1=L2B[:, 1, :], op=ALU.mult), K_, K_)
                    G(lambda e, k=k: e.tensor_tensor(out=HL[:, :, :, k], in0=m1[:], in1=m2[:], op=ALU.add), K_, K_)
                    yield
            recg = rec_gen()
        P.barrier()
        def gelu(x, o, t1, t2, kx, ko, kt1, kt2):
            V(lambda e: e.tensor_tensor(out=t1, in0=x, in1=x, op=ALU.mult), [kx], [kt1])
            V(lambda e: e.tensor_scalar(out=t1, in0=t1, scalar1=0.044715, scalar2=1.0, op0=ALU.mult, op1=ALU.add), [kt1], [kt1])
            V(lambda e: e.tensor_tensor(out=t1, in0=t1, in1=x, op=ALU.mult), [kt1, kx], [kt1])
            A(lambda e: e.activation(out=t2, in_=t1, func=AF.Sigmoid, scale=1.5957691216057308), [kt1], [kt2])
            V(lambda e: e.tensor_tensor(out=o, in0=x, in1=t2, op=ALU.mult), [kx, kt2], [ko])

        if stage >= 3:
            with ExitStack() as s3:
                sb3 = lambda name, shape, dt=F32: s3.enter_context(_sbuf(name, shape, dt))
                cbc = sb3("cbc", [128, 16])
                LD(lambda e: e.dma_start(out=cbc[:], in_=rb31[0:1, :].broadcast_to([128, 16])), w=["cbc"])
                raw = [sb3("raw%d" % i, [128, 16, 128]) for i in range(2)]
                hib = [sb3("hib%d" % i, [128, 16, 128], BF16) for i in range(2)]
                hif = [sb3("hif%d" % i, [128, 16, 128]) for i in range(2)]
                lob = [sb3("lob%d" % i, [128, 16, 128], BF16) for i in range(2)]
                jobs = [(tw_raw[i], twh_d[i], twl_d[i]) for i in range(3)] + \
                       [(ts_raw[i], tsh_d[i], tsl_d[i]) for i in range(2)] + \
                       [(tn_raw, tnh_d, tnl_d)]
                for n_, (src, dh_, dl_) in enumerate(jobs):
                    r = n_ % 2
                    npart = 16 if n_ == len(jobs) - 1 else 128
                    ra, hb, hf, lb = raw[r][0:npart], hib[r][0:npart], hif[r][0:npart], lob[r][0:npart]
                    kr, kh, kfh, kl = "raw%d" % r, "hib%d" % r, "hif%d" % r, "lob%d" % r
                    LD(lambda e, ra=ra, src=src: e.dma_start(out=ra[:], in_=src), w=[kr])
                    V(lambda e, ra=ra, npart=npart: e.tensor_tensor(out=ra[:], in0=ra[:], in1=cbc[0:npart].unsqueeze(2).broadcast_to([npart, 16, 128]),
                                                       op=ALU.subtract), [kr, "cbc"], [kr])
                    A(lambda e, ra=ra, hb=hb: e.activation(out=hb[:], in_=ra[:], func=AF.Identity), [kr], [kh])
                    V(lambda e, hb=hb, hf=hf: e.tensor_copy(out=hf[:], in_=hb[:]), [kh], [kfh])
                    V(lambda e, ra=ra, hf=hf: e.tensor_tensor(out=hf[:], in0=ra[:], in1=hf[:], op=ALU.subtract), [kr, kfh], [kfh])
                    A(lambda e, hf=hf, lb=lb: e.activation(out=lb[:], in_=hf[:], func=AF.Identity), [kfh], [kl])
                    LD(lambda e, hb=hb, dh_=dh_: e.dma_start(out=dh_, in_=hb[:]), r=[kh], w=[("tabh", n_)])
                    LD(lambda e, lb=lb, dl_=dl_: e.dma_start(out=dl_, in_=lb[:]), r=[kl], w=[("tabl", n_)])
            P.barrier()
            with ExitStack() as s3:
                sb3 = lambda name, shape, dt=F32: s3.enter_context(_sbuf(name, shape, dt))
                w1f = sb3("w1f", [64, 32, 128]); w1b = sb3("w1b", [64, 32, 128], BF16)
                w2f = sb3("w2f", [128, 64]); w2b = sb3("w2b", [128, 64], BF16)
                posT = sb3("posT", [64, 32, 2]); c1 = sb3("c1", [128, 2]); fvct = sb3("fvct", [128, 2])
                kin = [sb3("kin%d" % i, [64, LT], BF16) for i in range(2)]
                hx = sb3("hx", [128, 256]); g1 = sb3("g1", [128, 256]); g2_ = sb3("g2_", [128, 256])
                hgf = sb3("hgf", [128, 256]); hgb = sb3("hgb", [128, 256], BF16)
                kst = sb3("kst", [64, 256], BF16); vcst = sb3("vcst", [128, 2, 65])
                LD(lambda e: e.dma_start(out=fvct[:], in_=fvc[:, :]), w=["fvct"])
                G(lambda e: e.memset(hgf[:], 0.0), w=["hgf"])
                G(lambda e: e.memset(hx[:], 0.0), w=["hx"])
                ci = 0
                for which, (w1d, w2d, posd, srcd) in enumerate([(w_cmp_k1, w_cmp_k2, posT2_k, kcT_d), (w_cmp_v1, w_cmp_v2, posT2_v, vcT_d)]):
                    LD(lambda e, w1d=w1d: e.dma_start(out=w1f[:], in_=w1d.rearrange("(l d) h -> d l h", d=64)), w=["w1f"])
                    LD(lambda e, w2d=w2d: e.dma_start(out=w2f[:], in_=w2d[:, :]), w=["w2f"])
                    LD(lambda e, posd=posd: e.dma_start(out=posT[:], in_=posd[:, :, :]), w=["posT"])
                    V(lambda e: e.tensor_copy(out=w1b[:], in_=w1f[:]), ["w1f"], ["w1b"])
                    V(lambda e: e.tensor_copy(out=w2b[:], in_=w2f[:]), ["w2f"], ["w2b"])
                    for l in range(32):
                        T(lambda e, l=l: e.matmul(PS[0][:, 0:2], lhsT=w1f[:, l, :], rhs=posT[:, l, :], start=(l == 0), stop=(l == 31)),
                          ["w1f", "posT"], ["ps0"])
                    V(lambda e: e.tensor_copy(out=c1[:], in_=PS[0][:, 0:2]), ["ps0"], ["c1"])
                    for g in range(4):
                        kn = kin[ci % 2]; kk = "kin%d" % (ci % 2); ci += 1
                        LD(lambda e, kn=kn, g=g, srcd=srcd: e.dma_start(out=kn[:], in_=srcd[g * 64:(g + 1) * 64, :]), w=[kk])
                        for l in range(32):
                            T(lambda e, kn=kn, l=l: e.matmul(PS[1][:, 0:255], lhsT=w1b[:, l, :], rhs=kn[:, l:l + 16 * 254 + 1:16],
                                                             start=(l == 0), stop=(l == 31)), ["w1b", kk], ["ps1"])
                        A(lambda e: e.activation(out=hx[:, 0:255], in_=PS[1][:, 0:255], func=AF.Identity, bias=c1[:, 0:1]),
                          ["ps1", "c1"], ["hx"])
                        gelu(hx[:, 0:255], hgf[:, 0:255], g1[:, 0:255], g2_[:, 0:255], "hx", "hgf", "g1", "g2_")
                        V(lambda e: e.tensor_copy(out=hgb[:], in_=hgf[:]), ["hgf"], ["hgb"])
                        if which == 0:
                            T(lambda e: e.matmul(PS[2][0:64, 0:256], lhsT=w2b[:], rhs=hgb[:], start=True, stop=True), ["w2b", "hgb"], ["ps2"])
                            V(lambda e: e.tensor_copy(out=kst[:], in_=PS[2][0:64, 0:256]), ["ps2"], ["kst"])
                            LD(lambda e, g=g: e.dma_start(out=kcmpT_d[g], in_=kst[:]), r=["kst"], w=[("kcmp", g)])
                        else:
                            for tq2 in range(2):
                                T(lambda e, tq2=tq2: e.matmul(PS[3][:, tq2 * 64:(tq2 + 1) * 64], lhsT=hgb[:, tq2 * 128:(tq2 + 1) * 128], rhs=w2b[:],
                                                            start=True, stop=True), ["w2b", "hgb"], ["ps3"])
                            for tq2 in range(2):
                                A(lambda e, tq2=tq2: e.activation(out=vcst[:, tq2, 0:64], in_=PS[3][:, tq2 * 64:(tq2 + 1) * 64], func=AF.Identity,
                                                                scale=fvct[:, tq2:tq2 + 1]), ["ps3", "fvct"], ["vcst"])
                            V(lambda e: e.tensor_copy(out=vcst[:, :, 64:65], in_=fvct[:].unsqueeze(2)), ["fvct"], ["vcst"])
                            LD(lambda e, g=g: e.dma_start(out=vcmp_d[g].rearrange("(t p) d -> p t d", p=128), in_=vcst[:]),
                               r=["vcst"], w=[("vcmp", g)])
            P.barrier()
            with ExitStack() as s3:
                sb3 = lambda name, shape, dt=F32: s3.enter_context(_sbuf(name, shape, dt))
                Asel = sb3("Asel", [128, 16, 64]); Bsel = sb3("Bsel", [128, 16, 64]); ovl = sb3("ovl", [128, 2, 65])
                LD(lambda e: e.dma_start(out=Asel[:], in_=A_sel[:, :, :]), w=["Asel"])
                LD(lambda e: e.dma_start(out=Bsel[:], in_=B_sel[:, :, :]), w=["Bsel"])
                LD(lambda e: e.dma_start(out=ovl[:], in_=ovl_d[:, :, :]), w=["ovl"])
                kcT = sb3("kcT", [64, 256], BF16); Vc = sb3("Vc", [128, 2, 65])
                ksT = sb3("ksT", [128, LT], BF16); kwT = sb3("kwT", [64, LT], BF16)
                vsb = sb3("vsb", [128, 32, 65], BF16); vwb = sb3("vwb", [128, 32, 65], BF16)
                twh = sb3("twh", [128, 3, 4, 128], BF16); twl = sb3("twl", [128, 3, 4, 128], BF16)
                tsh = sb3("tsh", [128, 2, 4, 128], BF16); tsl = sb3("tsl", [128, 2, 4, 128], BF16)
                qTt = [sb3("qTt%d" % i, [128, 4, 128], BF16) for i in range(2)]
                LD(lambda e: e.dma_start(out=ksT[64:128, :], in_=EM_d[:, :]), w=["ksTem"])
                zat = [sb3("zat%d" % i, [64, 4, 128], BF16) for i in range(3)]
                tnh = sb3("tnh", [16, 4, 128], BF16); tnl = sb3("tnl", [16, 4, 128], BF16)
                SM2t = sb3("SM2t", [16, 376], BF16); stepbt = sb3("stepbt", [128, 16, 2])
                LD(lambda e: e.dma_start(out=SM2t[:], in_=SM2_d[:, :]), w=["SM2t"])
                LD(lambda e: e.dma_start(out=stepbt[:], in_=stepb[:, :, :]), w=["stepbt"])
                Pf = [sb3("Pf%d" % i, [128, 2, 512]) for i in range(2)]
                Pb = [sb3("Pb%d" % i, [128, 512], BF16) for i in range(4)]
                Oc = [sb3("Oc%d" % i, [65, 512]) for i in range(3)]
                Osb = [sb3("Osb%d" % i, [65, 512]) for i in range(4)]
                rct3 = sb3("rct3", [96, 3, 512]); rc4 = sb3("rc4", [128, 4]); bc3 = sb3("bc3", [64, 3, 512])
                g64 = [sb3("g64_%d" % i, [65, 3, 4, 128]) for i in range(3)]
                G(lambda e: e.memset(rct3[:], 1.0), w=["rct3"])
                gT_v = gT_d.rearrange("(h b) t -> b h t", b=3)
                imp2 = sb3("imp2", [128, 64]); top8 = sb3("top8", [128, 8]); selm = sb3("selm", [128, 128])
                G(lambda e: e.memset(selm[:], 0.0), w=["selm"])
                acc = sb3("acc", [64, 512]); tm1 = sb3("tm1", [64, 512]); tm2 = sb3("tm2", [64, 512])
                oab = [sb3("oab%d" % i, [64, 4, 128], BF16) for i in range(2)]
                vs_v = vs_d.rearrange("(t p) g d -> p t g d", p=128)
                vw_v = vw_d.rearrange("(t p) g d -> p t g d", p=128)
                sc = {"s": 0, "pb": 0}
                f2 = lambda x: x.rearrange("p r q -> p (r q)")

                def s_tile(lhsT, lk, qt2, qk, extra, pb=None):
                    if pb is None:
                        pb = sc["s"] % 3; sc["s"] += 1
                    pk = "ps%d" % pb
                    n = len(extra)
                    T(lambda e: e.matmul(PS[pb][:, :], lhsT=lhsT, rhs=qt2, start=True, stop=(n == 0)), list(lk) + list(qk), [pk])
                    for i, (l2, r2, ks2) in enumerate(extra):
                        T(lambda e, l2=l2, r2=r2, i=i: e.matmul(PS[pb][:, :], lhsT=l2, rhs=r2, start=False, stop=(i == n - 1)), ks2, [pk])
                    return pb

                def front(g, j, it):
                    r = it % 2
                    r3 = it % 3
                    q_, za_ = qTt[r], zat[r3]
                    kq, kz = "qTt%d" % r, "zat%d" % r3
                    pf, oc, gb_ = Pf[r], Oc[r3], g64[r3]
                    kpf, koc, kn4, kgb = "Pf%d" % r, "Oc%d" % r3, "qn%d" % r, "g64_%d" % r3
                    tsl_j = slice(j * 128, (j + 1) * 128)
                    qtl = 16 + j
                    LD(lambda e: e.dma_start(out=q_[0:64], in_=qT_d[g, j]), w=[kq])
                    LD(lambda e: e.dma_start(out=za_[:], in_=zaT_d[g, j]), w=[kz])
                    LD(lambda e: e.dma_start(out=gb_[64:65], in_=gT_v[:, 4 * g:4 * g + 4, tsl_j].unsqueeze(0)), w=[kgb])
                    yield
                    q2 = f2(q_[0:64])
                    pbs = []
                    for kt2 in range(2):
                        off = 248 - 8 * qtl + 128 * kt2
                        pbs.append(s_tile(kcT[:, kt2 * 128:(kt2 + 1) * 128], ["kcT"], q2, [kq],
                                          [(SM2t[:, off:off + 128], f2(tnh[:]), ["SM2t", "tnh"]), (SM2t[:, off:off + 128], f2(tnl[:]), ["SM2t", "tnl"])], pb=(3, 7)[kt2]))
                    for kt2 in range(2):
                        A(lambda e, pb=pbs[kt2], kt2=kt2: e.activation(out=pf[:, kt2, :], in_=PS[pb][:, :], func=AF.Exp, bias=stepbt[:, j, kt2:kt2 + 1]),
                          ["ps%d" % pbs[kt2], "stepbt"], [kpf])
                    yield
                    for kt2 in range(2):
                        T(lambda e, kt2=kt2: e.matmul(PS[3][0:65, :], lhsT=Vc[:, kt2, :], rhs=pf[:, kt2, :], start=(kt2 == 0), stop=(kt2 == 1)),
                          ["Vc", kpf], ["ps3"])
                    V(lambda e: e.tensor_copy(out=oc[:], in_=PS[3][0:65, :]), ["ps3"], [koc])
                    yield
                    idx = 0
                    for rr in range(4):
                        for kt2 in range(2):
                            T(lambda e, rr=rr, kt2=kt2: e.matmul(PS[7][:, rr * 65:(rr + 1) * 65], lhsT=pf[:, kt2, rr * 128:(rr + 1) * 128], rhs=ovl[:, kt2, :],
                                                                 start=(kt2 == 0), stop=(kt2 == 1)), [kpf, "ovl"], ["ps7"])
                    U4 = PS[7][:, 0:260].rearrange("p (r c) -> p r c", c=65)
                    V(lambda e: e.tensor_scalar(out=rc4[:], in0=U4[:, :, 64], scalar1=1e-18, scalar2=None, op0=ALU.max), ["ps7"], ["rc4"])
                    V(lambda e: e.reciprocal(out=rc4[:], in_=rc4[:]), ["rc4"], ["rc4"])
                    V(lambda e: e.tensor_scalar(out=imp2[:], in0=U4[:, 0, 0:64], scalar1=rc4[:, 0:1], scalar2=None, op0=ALU.mult), ["ps7", "rc4"], ["imp2"])
                    for rr in range(1, 4):
                        V(lambda e, rr=rr: e.scalar_tensor_tensor(out=imp2[:], in0=U4[:, rr, 0:64], scalar=rc4[:, rr:rr + 1], in1=imp2[:], op0=ALU.mult, op1=ALU.add),
                          ["ps7", "rc4", "imp2"], ["imp2"])
                    V(lambda e: e.tensor_tensor(out=imp2[:], in0=imp2[:], in1=Asel[:, j, :], op=ALU.mult), ["imp2", "Asel"], ["imp2"])
                    V(lambda e: e.tensor_tensor(out=imp2[:], in0=imp2[:], in1=Bsel[:, j, :], op=ALU.add), ["imp2", "Bsel"], ["imp2"])
                    V(lambda e: e.max(out=top8[:], in_=imp2[:]), ["imp2"], ["top8"])
                    V(lambda e: e.tensor_scalar(out=selm[:, 64:128], in0=imp2[:], scalar1=top8[:, 7:8], scalar2=-NEG, op0=ALU.is_ge, op1=ALU.mult),
                      ["imp2", "top8"], ["selm"])
                    V(lambda e: e.tensor_scalar(out=selm[:, 64:128], in0=selm[:, 64:128], scalar1=NEG, scalar2=None, op0=ALU.add), ["selm"], ["selm"])
                    yield
                    yield
                    T(lambda e: e.transpose(PS[3][:, 0:128], selm[:], identf[:]), ["selm", "identf"], ["ps3"])
                    for rr in range(4):
                        V(lambda e, rr=rr: e.tensor_copy(out=q_[64:128, rr, :], in_=PS[3][64:128, 0:128]), ["ps3"], [kn4])
                    yield

                def adv(gen):
                    if gen is None:
                        return False
                    try:
                        next(gen)
                        return True
                    except StopIteration:
                        return False

                def back(g, j, it, nxt, prev_tail):
                    r = it % 2
                    q_ = qTt[r]
                    kq, kn4 = "qTt%d" % r, "qn%d" % r
                    qtl = 16 + j
                    q2 = f2(q_[0:64])
                    qn2 = f2(q_[:])
                    sb_ = 4 if r == 0 else 6
                    steps = []
                    for kt in range(qtl - 4, qtl + 1):
                        d = qtl - kt
                        extra = []
                        if d in (0, 1, 4):
                            di = {0: 0, 1: 1, 4: 2}[d]
                            extra = [(identb[:], f2(twh[:, di]), ["identb", "twh"]), (identb[:], f2(twl[:, di]), ["identb", "twl"])]
                        steps.append((kwT[:, kt * 128:(kt + 1) * 128], ["kwT"], extra, vwb[:, kt, :], "vwb", 5, d == 4, d == 0, q2, [kq]))
                    nk = qtl + 1
                    for kt in range(nk):
                        d = qtl - kt
                        extra = []
                        if d in (0, 1):
                            extra = [(identb[:], f2(tsh[:, d]), ["identb", "tsh"]), (identb[:], f2(tsl[:, d]), ["identb", "tsl"])]
                        steps.append((ksT[:, kt * 128:(kt + 1) * 128], ["ksT", "ksTem"], extra, vsb[:, kt, :], "vsb", sb_, kt == 0, kt == nk - 1, qn2, [kq, kn4]))
                    pend = []

                    def finish():
                        pb, st = pend.pop(0)
                        pi = sc["pb"] % 4; sc["pb"] += 1
                        A(lambda e: e.activation(out=Pb[pi][:], in_=PS[pb][:, :], func=AF.Exp), ["ps%d" % pb], ["Pb%d" % pi])
                        T(lambda e: e.matmul(PS[st[5]][0:65, :], lhsT=st[3], rhs=Pb[pi][:], start=st[6], stop=st[7]), [st[4], "Pb%d" % pi], ["ps%d" % st[5]])

                    Ow_, kow = Osb[2 * r + 1], "Osb%d" % (2 * r + 1)
                    for i, st in enumerate(steps):
                        pb = s_tile(st[0], st[1], st[8], st[9], st[2])
                        pend.append((pb, st))
                        if len(pend) > 2:
                            finish()
                        if i == 10:
                            V(lambda e: e.tensor_copy(out=Ow_[:], in_=PS[5][0:65, :]), ["ps5"], [kow])
                        if i % 3 == 1:
                            adv(nxt)
                        if i % 2 == 1:
                            adv(prev_tail)
                        if i % 4 == 0:
                            adv(recg)
                    while pend:
                        finish()
                    while adv(prev_tail):
                        pass

                def tail(g, j, it):
                    r = it % 2
                    r3 = it % 3
                    za_, oa_ = zat[r3], oab[r]
                    kz, koa = "zat%d" % r3, "oab%d" % r
                    oc, gb_ = Oc[r3], g64[r3]
                    koc, kgb = "Oc%d" % r3, "g64_%d" % r3
                    tsl_j = slice(j * 128, (j + 1) * 128)
                    sb_ = 4 if r == 0 else 6
                    Os_, kos = Osb[2 * r], "Osb%d" % (2 * r)
                    Ow_, kow = Osb[2 * r + 1], "Osb%d" % (2 * r + 1)
                    V(lambda e: e.tensor_copy(out=Os_[:], in_=PS[sb_][0:65, :]), ["ps%d" % sb_], [kos])
                    yield
                    Ol = [(oc, koc), (Os_, kos), (Ow_, kow)]
                    for br in range(3):
                        V(lambda e, br=br: e.tensor_scalar(out=rct3[64:65, br, :], in0=Ol[br][0][64:65, :], scalar1=1e-18, scalar2=None, op0=ALU.max),
                          [Ol[br][1]], ["rct3"])
                    yield
                    yield
                    r3f = rct3[64:65].rearrange("p b q -> p (b q)")
                    A(lambda e: e.activation(out=r3f, in_=r3f, func=AF.Ln), ["rct3"], ["rct3"])
                    yield
                    yield
                    A(lambda e: e.activation(out=r3f, in_=r3f, func=AF.Exp, scale=-1.0), ["rct3"], ["rct3"])
                    yield
                    yield
                    V(lambda e: e.tensor_tensor(out=r3f, in0=r3f, in1=gb_[64:65].rearrange("p b r q -> p (b r q)"), op=ALU.mult), ["rct3", kgb], ["rct3"])
                    for hh in range(2):
                        V(lambda e, hh=hh: e.stream_shuffle(out=bc3[32 * hh:32 * hh + 32].rearrange("p b q -> p (b q)"),
                                                            in_=rct3[64:96].rearrange("p b q -> p (b q)"), mask=[0] * 32), ["rct3"], ["bc3"])
                    yield
                    yield
                    V(lambda e: e.tensor_tensor(out=acc[:], in0=oc[0:64, :], in1=bc3[:, 0, :], op=ALU.mult), [koc, "bc3"], ["acc"])
                    G(lambda e: e.tensor_tensor(out=tm1[:], in0=Os_[0:64, :], in1=bc3[:, 1, :], op=ALU.mult), [kos, "bc3"], ["tm1"])
                    G(lambda e: e.tensor_tensor(out=tm2[:], in0=Ow_[0:64, :], in1=bc3[:, 2, :], op=ALU.mult), [kow, "bc3"], ["tm2"])
                    yield
                    G(lambda e: e.tensor_tensor(out=acc[:], in0=acc[:], in1=tm1[:], op=ALU.add), ["acc", "tm1"], ["acc"])
                    G(lambda e: e.tensor_tensor(out=acc[:], in0=acc[:], in1=tm2[:], op=ALU.add), ["acc", "tm2"], ["acc"])
                    yield
                    V(lambda e: e.tensor_tensor(out=f2(oa_[:]), in0=acc[:], in1=f2(za_[:]), op=ALU.mult), ["acc", kz], [koa])
                    LD(lambda e: e.dma_start(out=oaT_d[g, j], in_=oa_[:]), r=[koa], w=[("oaT", g, j)])

                it = 0
                ptail = None
                for g in range(4):
                    LD(lambda e, g=g: e.dma_start(out=kcT[:], in_=kcmpT_d[g]), w=["kcT"])
                    LD(lambda e, g=g: e.dma_start(out=Vc[:], in_=vcmp_d[g].rearrange("(t p) d -> p t d", p=128)), w=["Vc"])
                    LD(lambda e, g=g: e.dma_start(out=ksT[0:64, :], in_=ksT_d[g * 64:(g + 1) * 64, :]), w=["ksT"])
                    LD(lambda e, g=g: e.dma_start(out=kwT[:], in_=kwT_d[g * 64:(g + 1) * 64, :]), w=["kwT"])
                    LD(lambda e, g=g: e.dma_start(out=vsb[:], in_=vs_v[:, :, g, :]), w=["vsb"])
                    LD(lambda e, g=g: e.dma_start(out=vwb[:], in_=vw_v[:, :, g, :]), w=["vwb"])
                    for i in range(3):
                        LD(lambda e, g=g, i=i: e.dma_start(out=twh[:, i], in_=twh_d[i][:, 4 * g:4 * g + 4, :]), w=["twh"])
                        LD(lambda e, g=g, i=i: e.dma_start(out=twl[:, i], in_=twl_d[i][:, 4 * g:4 * g + 4, :]), w=["twl"])
                    LD(lambda e, g=g: e.dma_start(out=tnh[:], in_=tnh_d[:, 4 * g:4 * g + 4, :]), w=["tnh"])
                    LD(lambda e, g=g: e.dma_start(out=tnl[:], in_=tnl_d[:, 4 * g:4 * g + 4, :]), w=["tnl"])
                    for i in range(2):
                        LD(lambda e, g=g, i=i: e.dma_start(out=tsh[:, i], in_=tsh_d[i][:, 4 * g:4 * g + 4, :]), w=["tsh"])
                        LD(lambda e, g=g, i=i: e.dma_start(out=tsl[:, i], in_=tsl_d[i][:, 4 * g:4 * g + 4, :]), w=["tsl"])
                    cur = front(g, 0, it)
                    while adv(cur):
                        pass
                    for j in range(NJ):
                        nxt = front(g, j + 1, it + 1) if j + 1 < NJ else None
                        back(g, j, it, nxt, ptail)
                        while adv(nxt):
                            pass
                        ptail = tail(g, j, it)
                        adv(ptail)
                        it += 1
                while adv(ptail):
                    pass
                while adv(recg):
                    pass
            P.barrier()
        if stage >= 4:
            if True:
                with ExitStack() as s5:
                    sb5 = lambda name, shape, dt=F32: s5.enter_context(_sbuf(name, shape, dt))
                    Bblk = sb5("Bblk", [128, 4, 2, 256])
                    Ablk2 = sb5("Ablk2", [128, 4, 2, 256]); ta12 = sb5("ta12", [128, 4, 256]); ta22 = sb5("ta22", [128, 4, 256])
                    ugt = [sb5("ugt%d" % i, [128, 2, 128], BF16) for i in range(2)]
                    WTg = [sb5("WTg%d" % i, [128, 2, 256], BF16) for i in range(2)]
                    Yblk = sb5("Yblk", [128, 16, 256])
                    y_v = y_d.rearrange("(k j) c -> k j c", j=16)
                    for gp in range(32):
                        if gp % 4 == 0:
                            build_A(gp, Ablk2, ta12, ta22)
                            bj4 = lambda x, gp=gp: x[:, gp:gp + 4, 0:16].unsqueeze(3).broadcast_to([128, 4, 16, 16])
                            bc4 = lambda x, gp=gp: x[:, gp:gp + 4, :].unsqueeze(2).broadcast_to([128, 4, 16, 16])
                            rw = ["Bblk", SP_, "ta12", "ta22"]
                            V(lambda e, bj4=bj4, bc4=bc4: e.tensor_tensor(out=v4(ta12[:]), in0=bj4(pwr), in1=bc4(cr_), op=ALU.mult), rw, rw)
                            V(lambda e, bj4=bj4, bc4=bc4: e.tensor_tensor(out=v4(ta22[:]), in0=bj4(pwi), in1=bc4(ci_), op=ALU.mult), rw, rw)
                            V(lambda e: e.tensor_tensor(out=Bblk[:, :, 0, :], in0=ta12[:], in1=ta22[:], op=ALU.subtract), rw, rw)
                            V(lambda e, bj4=bj4, bc4=bc4: e.tensor_tensor(out=v4(ta12[:]), in0=bj4(pwi), in1=bc4(cr_), op=ALU.mult), rw, rw)
                            V(lambda e, bj4=bj4, bc4=bc4: e.tensor_tensor(out=v4(ta22[:]), in0=bj4(pwr), in1=bc4(ci_), op=ALU.mult), rw, rw)
                            V(lambda e: e.tensor_tensor(out=ta12[:], in0=ta12[:], in1=ta22[:], op=ALU.add), rw, rw)
                            V(lambda e: e.tensor_scalar(out=Bblk[:, :, 1, :], in0=ta12[:], scalar1=-1.0, scalar2=None, op0=ALU.mult), rw, rw)
                        gpl = gp % 4
                        for g2 in range(2):
                            g = 2 * gp + g2
                            pr = slice(64 * g2, 64 * g2 + 64)
                            wt = WTg[g2]; kw_ = "WTg%d" % g2
                            pw_, py_ = g2, 2 + g2
                            for a in range(2):
                                for ri in range(2):
                                    T(lambda e, pw_=pw_, a=a, ri=ri, pr=pr, gpl=gpl: e.matmul(
                                        PS[pw_][:, a * 256:(a + 1) * 256], lhsT=Ablk2[pr, gpl, ri, a * 128:(a + 1) * 128], rhs=Bblk[pr, gpl, ri, :],
                                        start=(ri == 0), stop=(ri == 1)), ["Ablk", "Bblk"], ["ps%d" % pw_])
                            V(lambda e, wt=wt, pw_=pw_: e.tensor_tensor(out=wt[:].rearrange("p a k -> p (a k)"), in0=PS[pw_][:, :],
                                                                        in1=TMt[:].rearrange("p a k -> p (a k)"), op=ALU.mult),
                              ["ps%d" % pw_, "TMt"], [kw_])
                            ug_ = ugt[g2]
                            LD(lambda e, ug_=ug_, g=g: e.dma_start(out=ug_[:], in_=ugo_d[g]), w=["ugt%d" % g2])
                            for a in range(2):
                                T(lambda e, py_=py_, a=a, ug_=ug_, wt=wt: e.matmul(PS[py_][:, 0:256], lhsT=ug_[:, a, :], rhs=wt[:, a, :],
                                                                                start=(a == 0), stop=False), ["ugt%d" % g2, kw_], ["ps%d" % py_])
                            for ri in range(2):
                                T(lambda e, py_=py_, ri=ri, pr=pr, gp=gp, gpl=gpl: e.matmul(
                                    PS[py_][:, 0:256], lhsT=HL[pr, ri, gp, 127:255], rhs=Bblk[pr, gpl, ri, :], start=False, stop=(ri == 1)),
                                    ["HL", "Bblk"], ["ps%d" % py_])
                            gl = g % 16
                            A(lambda e, py_=py_, gl=gl: e.activation(out=Yblk[:, :, gl * 16:(gl + 1) * 16],
                                                                     in_=PS[py_][:, 0:256].rearrange("p (j c) -> p j c", c=16), func=AF.Identity),
                              ["ps%d" % py_], ["Yblk"])
                            if gl == 15:
                                cb = (g // 16) * 256
                                LD(lambda e, cb=cb: e.dma_start(out=y_v[:, :, cb:cb + 256], in_=Yblk[:]), r=["Yblk"], w=[("y_d", cb)])
            P.barrier()
            s4.close()
            P.barrier()
        if stage >= 5:
            with ExitStack() as s6:
                sb6 = lambda name, shape, dt=F32: s6.enter_context(_sbuf(name, shape, dt))
                yT = sb6("yT", [128, 8, LO], BF16)
                wgl = sb6("wgl", [128, 8, 1024], BF16); bglt = sb6("bglt", [128, 8]); dskt = sb6("dskt", [128, 8])
                LD(lambda e: e.dma_start(out=bglt[:], in_=bglu[:, :]), w=["bglt"])
                LD(lambda e: e.dma_start(out=dskt[:], in_=dsk[:, :]), w=["dskt"])
                for c in range(2):
                    P.dma("pool", lambda e, c=c: e.dma_start(out=wgl[:, :, c * 512:(c + 1) * 512],
                                                             in_=w_glu.rearrange("(k p) n -> p k n", p=128)[:, :, c * 512:(c + 1) * 512]), writes=["wgl"])
                yt = [sb6("yt%d" % i, [128, 1024]) for i in range(2)]
                ust = [sb6("ust%d" % i, [128, 8, 128], BF16) for i in range(2)]
                ypre = sb6("ypre", [128, 8, 128]); yg = sb6("yg", [128, 8, 128]); gt1 = sb6("gt1", [128, 8, 128]); gt2 = sb6("gt2", [128, 8, 128])
                us_v = usT_d.rearrange("(c p) t -> p c t", p=128)
                fl = lambda x: x[:].rearrange("p c t -> p (c t)")
                for tt in range(LO // 128):
                    r = tt % 2
                    y_, u_s = yt[r], ust[r]
                    ky, ku = "yt%d" % r, "ust%d" % r
                    LD(lambda e, y_=y_, tt=tt: e.dma_start(out=y_[:], in_=y_d[tt * 128:(tt + 1) * 128, :]), w=[ky])
                    LD(lambda e, u_s=u_s, tt=tt: e.dma_start(out=u_s[:], in_=us_v[:, :, LC + tt * 128:LC + (tt + 1) * 128]), w=[ku])
                    for ct in range(8):
                        pb = ct // 4
                        T(lambda e, y_=y_, ct=ct, pb=pb: e.transpose(PS[pb][:, (ct % 4) * 128:(ct % 4 + 1) * 128], y_[:, ct * 128:(ct + 1) * 128], identf[:]),
                          [ky, "identf"], ["ps%d" % pb])
                    for ct in range(8):
                        pb = ct // 4
                        V(lambda e, u_s=u_s, ct=ct, pb=pb: e.scalar_tensor_tensor(out=ypre[:, ct, :], in0=u_s[:, ct, :], scalar=dskt[:, ct:ct + 1],
                                                                                  in1=PS[pb][:, (ct % 4) * 128:(ct % 4 + 1) * 128], op0=ALU.mult, op1=ALU.add),
                          [ku, "dskt", "ps%d" % pb], ["ypre"])
                    gelu(fl(ypre), fl(yg), fl(gt1), fl(gt2), "ypre", "yg", "gt1", "gt2")
                    A(lambda e, tt=tt: e.activation(out=yT[:, :, tt * 128:(tt + 1) * 128], in_=yg[:], func=AF.Identity), ["yg"], [("yT", tt)])
                sgb = [sb6("sgb%d" % i, [128, 512]) for i in range(2)]
                zbt = [sb6("zbt%d" % i, [128, 512], BF16) for i in range(2)]
                obs = [sb6("obs%d" % i, [128, 512], BF16) for i in range(2)]
                it = 0
                for co in range(8):
                    for n in range(4):
                        r = it % 2; it += 1
                        pb = 4 + it % 4
                        nsl = slice(n * 512, (n + 1) * 512)
                        LDA(lambda e, r=r, co=co, nsl=nsl: e.dma_start(out=zbt[r][:], in_=zbT_d[co * 128:(co + 1) * 128, nsl]), w=["zbt%d" % r])
                        for ct in range(8):
                            T(lambda e, pb=pb, ct=ct, co=co, nsl=nsl: e.matmul(PS[pb][:, :], lhsT=wgl[:, ct, co * 128:(co + 1) * 128], rhs=yT[:, ct, nsl],
                                                                              start=(ct == 0), stop=(ct == 7)),
                              ["wgl"] + [("yT", n * 4 + i_) for i_ in range(4)], ["ps%d" % pb])
                        A(lambda e, r=r, pb=pb, co=co: e.activation(out=sgb[r][:], in_=PS[pb][:, :], func=AF.Sigmoid, bias=bglt[:, co:co + 1]),
                          ["ps%d" % pb, "bglt"], ["sgb%d" % r])
                        V(lambda e, r=r, co=co, nsl=nsl: e.tensor_tensor(out=sgb[r][:], in0=sgb[r][:], in1=yT[:, co, nsl], op=ALU.mult),
                          ["sgb%d" % r] + [("yT", n * 4 + i_) for i_ in range(4)], ["sgb%d" % r])
                        V(lambda e, r=r: e.tensor_tensor(out=obs[r][:], in0=sgb[r][:], in1=zbt[r][:], op=ALU.mult), ["sgb%d" % r, "zbt%d" % r], ["obs%d" % r])
                        LD(lambda e, r=r, co=co, nsl=nsl: e.dma_start(out=obT_d[co * 128:(co + 1) * 128, nsl], in_=obs[r][:]), r=["obs%d" % r], w=[("obT", co, n)])
            P.barrier()
        s78 = ExitStack()
        wo = s78.enter_context(_sbuf("wo", [128, KT, D], BF16))
        for c in range(4):
            P.dma("pool", lambda e, c=c: e.dma_start(out=wo[:, :, c * 512:(c + 1) * 512],
                                                     in_=w_out.rearrange("(k p) n -> p k n", p=128)[:, :, c * 512:(c + 1) * 512]), writes=[("wo", c)])
        if stage >= 6:
            with ExitStack() as s7:
                sb7 = lambda name, shape, dt=F32: s7.enter_context(_sbuf(name, shape, dt))
                oaT = sb7("oaT", [128, 8, LO], BF16); obT = sb7("obT", [128, 8, LO], BF16)
                for c in range(4):
                    csl = slice(c * 512, (c + 1) * 512)
                    for kt_ in range(8):
                        for hh in range(2):
                            gq_, rq_ = kt_ // 2, 2 * (kt_ % 2) + hh
                            LD(lambda e, c=c, kt_=kt_, hh=hh, gq_=gq_, rq_=rq_: e.dma_start(
                                out=oaT[hh * 64:(hh + 1) * 64, kt_, c * 512:(c + 1) * 512].rearrange("p (j q) -> p j q", q=128),
                                in_=oaT_d[gq_, 4 * c:4 * c + 4, :, rq_, :].rearrange("j d q -> d j q")), w=[("oaTs", c, kt_, hh)])
                    LD(lambda e, csl=csl: e.dma_start(out=obT[:, :, csl], in_=obT_d.rearrange("(k p) t -> p k t", p=128)[:, :, csl]), w=[("obTs", c)])
                wa = [sb7("wa%d" % i, [128, 8, 512], BF16) for i in range(2)]
                wbb = [sb7("wbb%d" % i, [128, 8, 512], BF16) for i in range(2)]
                gat = [sb7("gat%d" % i, [128, 512], BF16) for i in range(2)]
                gbt = [sb7("gbt%d" % i, [128, 512], BF16) for i in range(2)]
                ma = [sb7("ma%d" % i, [128, 512]) for i in range(2)]
                mb = [sb7("mb%d" % i, [128, 512]) for i in range(2)]
                mo = [sb7("mo%d" % i, [128, 512], BF16) for i in range(2)]
                wbn_v = w_bn.rearrange("(k p) n -> p k n", p=128); wbs_v = w_bs.rearrange("(k p) n -> p k n", p=128)
                it = 0
                def ld_w(dc):
                    wr = dc % 2
                    dsl = slice(dc * 512, (dc + 1) * 512)
                    P.dma("pool", lambda e: e.dma_start(out=wa[wr][:], in_=wbn_v[:, :, dsl]), writes=["wa%d" % wr])
                    P.dma("pool", lambda e: e.dma_start(out=wbb[wr][:], in_=wbs_v[:, :, dsl]), writes=["wbb%d" % wr])
                ld_w(0)
                for dc in range(4):
                    wr = dc % 2
                    dsl = slice(dc * 512, (dc + 1) * 512)
                    if dc + 1 < 4:
                        ld_w(dc + 1)
                    for mt in range(4):
                        dt_i = dc * 4 + mt
                        for n in range(4):
                            r = it % 2; it += 1
                            pa, pb = 4 + 2 * r, 5 + 2 * r
                            nsl = slice(n * 512, (n + 1) * 512)
                            rows = slice(dt_i * 128, (dt_i + 1) * 128)
                            LDA(lambda e, r=r, rows=rows, nsl=nsl: e.dma_start(out=gat[r][:], in_=gaT_d[rows, nsl]), w=["gat%d" % r])
                            LDA(lambda e, r=r, rows=rows, nsl=nsl: e.dma_start(out=gbt[r][:], in_=gbT_d[rows, nsl]), w=["gbt%d" % r])
                            for kt in range(8):
                                T(lambda e, pa=pa, kt=kt, mt=mt, wr=wr, nsl=nsl: e.matmul(PS[pa][:, :], lhsT=wa[wr][:, kt, mt * 128:(mt + 1) * 128], rhs=oaT[:, kt, nsl],
                                                                                         start=(kt == 0), stop=(kt == 7)), ["wa%d" % wr, ("oaTs", n, kt, 0), ("oaTs", n, kt, 1)], ["ps%d" % pa])
                            for kt in range(8):
                                T(lambda e, pb=pb, kt=kt, mt=mt, wr=wr, nsl=nsl: e.matmul(PS[pb][:, :], lhsT=wbb[wr][:, kt, mt * 128:(mt + 1) * 128], rhs=obT[:, kt, nsl],
                                                                                         start=(kt == 0), stop=(kt == 7)), ["wbb%d" % wr, ("obTs", n)], ["ps%d" % pb])
                            V(lambda e, r=r, pa=pa: e.tensor_tensor(out=ma[r][:], in0=PS[pa][:, :], in1=gat[r][:], op=ALU.mult), ["ps%d" % pa, "gat%d" % r], ["ma%d" % r])
                            V(lambda e, r=r, pb=pb: e.tensor_tensor(out=mb[r][:], in0=PS[pb][:, :], in1=gbt[r][:], op=ALU.mult), ["ps%d" % pb, "gbt%d" % r], ["mb%d" % r])
                            G(lambda e, r=r: e.tensor_tensor(out=mo[r][:], in0=ma[r][:], in1=mb[r][:], op=ALU.add), ["ma%d" % r, "mb%d" % r], ["mo%d" % r])
                            LD(lambda e, r=r, rows=rows, nsl=nsl: e.dma_start(out=mT_d[rows, nsl], in_=mo[r][:]), r=["mo%d" % r], w=[("mT", dt_i, n)])
            P.barrier()
        if stage >= 7:
            with ExitStack() as s8:
                sb8 = lambda name, shape, dt=F32: s8.enter_context(_sbuf(name, shape, dt))
                gbc = sb8("gbc", [128, D]); lgb = sb8("lgb", [128, D]); lbb = sb8("lbb", [128, D])
                LD(lambda e: e.dma_start(out=gbc[:], in_=grow_d.rearrange("a b -> (a b)").unsqueeze(0).broadcast_to([128, D])), w=["gbc"])
                LD(lambda e: e.dma_start(out=lgb[:], in_=ln_g[0:1, :].broadcast_to([128, D])), w=["lgb"])
                LD(lambda e: e.dma_start(out=lbb[:], in_=ln_b[0:1, :].broadcast_to([128, D])), w=["lbb"])
                mt_ = [sb8("mtt%d" % i, [128, KT, 128], BF16) for i in range(2)]
                xo = [sb8("xo%d" % i, [128, D]) for i in range(2)]
                z = [sb8("z%d" % i, [128, D]) for i in range(2)]
                zt1 = sb8("zt1", [128, 512])
                so = [sb8("so%d" % i, [128, 4, 6]) for i in range(2)]
                mvo = [sb8("mvo%d" % i, [128, 8]) for i in range(2)]
                mT_v = mT_d.rearrange("(k p) t -> p k t", p=128)
                for tt in range(LO // 128):
                    r = tt % 2
                    m_, x_, z_, s_, v_ = mt_[r], xo[r], z[r], so[r], mvo[r]
                    km, kx, kz, ks_, kv_ = "mtt%d" % r, "xo%d" % r, "z%d" % r, "so%d" % r, "mvo%d" % r
                    tks = slice(tt * 128, (tt + 1) * 128)
                    LDA(lambda e, m_=m_, tks=tks: e.dma_start(out=m_[:], in_=mT_v[:, :, tks]), w=[km])
                    LDA(lambda e, x_=x_, tks=tks: e.dma_start(out=x_[:], in_=x_own[tks, :]), w=[kx])
                    for dc in range(4):
                        pb = 4 + (tt * 4 + dc) % 4
                        dsl = slice(dc * 512, (dc + 1) * 512)
                        for kt in range(KT):
                            T(lambda e, pb=pb, kt=kt, m_=m_, dsl=dsl: e.matmul(PS[pb][:, :], lhsT=m_[:, kt, :], rhs=wo[:, kt, dsl], start=(kt == 0), stop=(kt == KT - 1)),
                              [km, ("wo", dc)], ["ps%d" % pb])
                        V(lambda e, pb=pb, dsl=dsl: e.tensor_tensor(out=zt1[:], in0=PS[pb][:, :], in1=gbc[:, dsl], op=ALU.mult), ["ps%d" % pb, "gbc"], ["zt1"])
                        V(lambda e, x_=x_, z_=z_, dsl=dsl: e.scalar_tensor_tensor(out=z_[:, dsl], in0=x_[:, dsl], scalar=float(ALPHA), in1=zt1[:], op0=ALU.mult, op1=ALU.add),
                          [kx, "zt1"], [kz])
                        V(lambda e, z_=z_, s_=s_, dc=dc, dsl=dsl: e.bn_stats(out=s_[:, dc, :], in_=z_[:, dsl]), [kz], [ks_])
                    V(lambda e, s_=s_, v_=v_: e.bn_aggr(out=v_[:, 0:2], in_=s_[:].rearrange("p a b -> p (a b)")), [ks_], [kv_])
                    V(lambda e, v_=v_: e.tensor_scalar(out=v_[:, 2:3], in0=v_[:, 1:2], scalar1=1e-5, scalar2=None, op0=ALU.add), [kv_], [kv_])
                    A(lambda e, v_=v_: e.activation(out=v_[:, 3:4], in_=v_[:, 2:3], func=AF.Sqrt), [kv_], [kv_])
                    V(lambda e, v_=v_: e.reciprocal(out=v_[:, 4:5], in_=v_[:, 3:4]), [kv_], [kv_])
                    V(lambda e, v_=v_: e.tensor_scalar(out=v_[:, 5:6], in0=v_[:, 0:1], scalar1=v_[:, 4:5], scalar2=-1.0, op0=ALU.mult, op1=ALU.mult), [kv_], [kv_])
                    A(lambda e, z_=z_, v_=v_: e.activation(out=z_[:], in_=z_[:], func=AF.Identity, scale=v_[:, 4:5], bias=v_[:, 5:6]), [kz, kv_], [kz])
                    G(lambda e, z_=z_: e.tensor_tensor(out=z_[:], in0=z_[:], in1=lgb[:], op=ALU.mult), [kz, "lgb"], [kz])
                    G(lambda e, z_=z_: e.tensor_tensor(out=z_[:], in0=z_[:], in1=lbb[:], op=ALU.add), [kz, "lbb"], [kz])
                    LD(lambda e, z_=z_, tks=tks: e.dma_start(out=out[tks, :], in_=z_[:]), r=[kz], w=[("out", tt)])

        s78.close()
        P.final_wait("sp")
        P.emit()
        nc._plan_trace = P.tr
    return nc


def _bf16(a):
    return np.ascontiguousarray(a).astype(ml_dtypes.bfloat16)


def _bucket(dist):
    n = np.maximum(dist, 0)
    nf = np.maximum(n, 16).astype(np.float32)
    large = 16 + (np.log(nf / np.float32(16)) / np.float32(np.log(8.0)) * np.float32(16)).astype(np.int32)
    large = np.minimum(large, 31)
    return np.where(n < 16, n, large)


def _attn_tables(rel_bias):
    rb = np.asarray(rel_bias, np.float32)
    key = np.arange(128)[:, None]
    q = np.arange(128)[None, :]
    def tile(d, lo, hi):
        dist = 128 * d + q - key
        valid = (dist >= lo) & (dist < hi)
        t = rb[_bucket(dist)]
        t = np.where(valid[:, :, None], t, np.float32(NEG))
        return np.ascontiguousarray(t.transpose(0, 2, 1))
    tw = np.stack([tile(0, 0, 512), tile(1, 0, 512), tile(4, 0, 512)])
    ts = np.stack([tile(0, 0, 1 << 30), tile(1, 0, 1 << 30)])
    m = np.arange(16)[:, None] - 9
    dist = q - 16 * m - 31
    t = rb[_bucket(dist)]
    t = np.where((dist >= 0)[:, :, None], t, np.float32(NEG))
    tc = np.ascontiguousarray(t.transpose(0, 2, 1))
    return tw.astype(np.float32), ts.astype(np.float32), tc.astype(np.float32)


def _sel_tables(half):
    A = np.zeros((128, 16, 64), np.float32)
    B = np.zeros((128, 16, 64), np.float32)
    qq = np.arange(128)[:, None, None]
    jj = np.arange(16)[None, :, None]
    jb = np.arange(64)[None, None, :]
    t = 128 * (16 + jj) + qq + 2048 * (half - 1)
    cur = t // 64
    gb = jb + 32 * (half - 1)
    allowed = (gb <= cur) & (gb >= 0)
    forced = ((gb == 0) | (gb == cur) | (gb == cur - 1)) & allowed
    A[...] = np.where(forced | ~allowed, 0.0, 1.0)
    B[...] = np.where(~allowed, -1e30, np.where(forced, 1e4, 0.0))
    n = np.arange(256)[:, None]
    sj = np.arange(64)[None, :]
    ov = ((16 * n < 64 * sj + 64) & (16 * n + 32 > 64 * sj)).astype(np.float32)
    ov[255] = 0
    if half == 0:
        ov[:128] = 0
    valid = np.ones((256, 1), np.float32)
    valid[255] = 0
    if half == 0:
        valid[:128] = 0
    ov = np.concatenate([ov, valid], axis=1)
    ovl = np.ascontiguousarray(ov.reshape(2, 128, 65).transpose(1, 0, 2))
    fvc = np.ones((128, 2), np.float32)
    fvc[:, 0] = float(half)
    fvc[127, 1] = 0.0
    return A, B, ovl, fvc


def make_maps(inp):
    x = np.asarray(inp["x"], np.float32)
    c = np.asarray(inp["c"], np.float32)
    shared = {
        "b_ada_l": np.ascontiguousarray(np.asarray(inp["b_ada"], np.float32)[0].reshape(48, 128).T),
        "w_ada": np.ascontiguousarray(np.asarray(inp["w_ada"], np.float32)[0]),
        "w_in": np.ascontiguousarray(np.asarray(inp["w_in"], np.float32)[0]),
        "ident_f": np.eye(128, dtype=np.float32),
        "ident_b": _bf16(np.eye(128, dtype=np.float32)),
        "w_branch_nsa": np.ascontiguousarray(np.asarray(inp["w_branch_nsa"], np.float32)[0]),
        "w_branch_ssm": np.ascontiguousarray(np.asarray(inp["w_branch_ssm"], np.float32)[0]),
        "w_out": np.ascontiguousarray(np.asarray(inp["w_out"], np.float32)[0]),
        "ln_g": np.ascontiguousarray(np.asarray(inp["ln_g"], np.float32)),
        "ln_b": np.ascontiguousarray(np.asarray(inp["ln_b"], np.float32)),
        "w_cmp_k1": np.ascontiguousarray(np.asarray(inp["w_cmp_k1"], np.float32)[0]),
        "w_cmp_k2": np.ascontiguousarray(np.asarray(inp["w_cmp_k2"], np.float32)[0]),
        "w_cmp_v1": np.ascontiguousarray(np.asarray(inp["w_cmp_v1"], np.float32)[0]),
        "w_cmp_v2": np.ascontiguousarray(np.asarray(inp["w_cmp_v2"], np.float32)[0]),
        "posT2_k": np.ascontiguousarray(np.repeat(np.asarray(inp["cmp_pos_k"], np.float32)[0].T[:, :, None], 2, axis=2)),
        "posT2_v": np.ascontiguousarray(np.repeat(np.asarray(inp["cmp_pos_v"], np.float32)[0].T[:, :, None], 2, axis=2)),
        "rb31": np.ascontiguousarray(np.asarray(inp["rel_bias"], np.float32)[31:32]),
        "EM_d": _bf16((np.arange(LT)[None, :] // 64 == np.arange(64)[:, None]).astype(np.float32)),
        "SM_d": (np.arange(3072)[None, :] // 64 == np.arange(48)[:, None]).astype(np.float32),
        "ones_d": np.ones((128, 128), np.float32),
    }
    f32 = lambda k: np.asarray(inp[k], np.float32)[0]
    g2p = lambda a: np.ascontiguousarray(a.reshape(32, 2, 64, *a.shape[2:]).transpose(1, 2, 0, *range(3, a.ndim + 1)).reshape(128, 32, *a.shape[2:]))
    shared["a_re2"] = g2p(f32("ssm_a_re")); shared["a_im2"] = g2p(f32("ssm_a_im"))
    shared["ldt2"] = g2p(np.repeat(f32("ssm_log_dt")[:, None], 64, axis=1))
    shared["b_re2"] = g2p(f32("ssm_b_re")); shared["b_im2"] = g2p(f32("ssm_b_im"))
    shared["c_re2"] = g2p(np.ascontiguousarray(f32("ssm_c_re").transpose(0, 2, 1))); shared["c_im2"] = g2p(np.ascontiguousarray(f32("ssm_c_im").transpose(0, 2, 1)))
    shared["dsk"] = np.ascontiguousarray(f32("ssm_d").reshape(8, 128).T); shared["bglu"] = np.ascontiguousarray(f32("b_glu").reshape(8, 128).T)
    shared["w_glu"] = np.ascontiguousarray(f32("w_glu"))
    row = np.arange(128)
    sm8 = np.zeros((128, 8, 240), np.float32)
    sm8[row, row // 16, 112 + row % 16] = 1.0
    shared["selm8_d"] = _bf16(sm8)
    tm = np.zeros((128, 2, 16, 16), np.float32)
    for a_ in range(2):
        tm[:, a_] = (np.arange(16)[None, :, None] >= (8 * a_ + row // 16)[:, None, None]).astype(np.float32)
    shared["TM_d"] = np.ascontiguousarray(tm.reshape(128, 2, 256))
    tw, ts, tc = _attn_tables(inp["rel_bias"])
    shared["tw_raw"], shared["ts_raw"], shared["tn_raw"] = tw, ts, tc
    pp = np.arange(128)[:, None, None]; jj_ = np.arange(16)[None, :, None]; k2 = np.arange(2)[None, None, :]
    shared["stepb"] = np.where(128 * k2 + pp >= 8 * (16 + jj_) + 7, np.float32(NEG), np.float32(0.0)).astype(np.float32)
    shared["SM2_d"] = _bf16((np.arange(376)[None, :] == np.arange(16)[:, None] + 239).astype(np.float32))
    seltabs = [_sel_tables(h) for h in range(2)]
    maps = []
    for core in range(8):
        b, half = core // 2, core % 2
        m = dict(shared)
        m["x_own"] = np.ascontiguousarray(x[b, half * LO:(half + 1) * LO])
        m["x_ctx"] = np.ascontiguousarray(x[b, 0:LC]) if half == 1 else np.zeros((LC, D), np.float32)
        cc = c[b].reshape(16, 128).T
        m["c2"] = np.ascontiguousarray(np.stack([cc, cc], axis=-1))
        m["flag"] = np.full((128, 1), float(half), np.float32)
        m["A_sel"], m["B_sel"], m["ovl_d"], m["fvc"] = seltabs[half]
        maps.append(m)
    return maps


def kernel(**inp):
    nc = build()
    maps = make_maps(inp)
    res = run_bass_kernel_spmd(nc, maps, core_ids=list(range(8)))
    out = np.zeros((4, 4096, D), np.float32)
    for core in range(8):
        b, half = core // 2, core % 2
        out[b, half * LO:(half + 1) * LO] = np.asarray(res.results[core]["out"])
    return out
```

```python
import numpy as np
import ml_dtypes
import concourse.bass as bass
import concourse.mybir as mybir
from concourse.bass_utils import run_bass_kernel_spmd
from contextlib import ExitStack

F32 = mybir.dt.float32
BF16 = mybir.dt.bfloat16
AF = mybir.ActivationFunctionType
ALU = mybir.AluOpType

DEBUG = False
NEG = -30000.0
LO = 2048
LC = 2048
LT = 4096
D = 2048
KT = 16
C_Q, C_KC, C_VC, C_KS, C_VS, C_KW, C_VW, C_NG, C_ZA, C_US, C_ZB, C_GA, C_GB = (
    0, 1024, 1280, 1536, 1792, 2048, 2304, 2560, 2608, 3632, 4656, 5680, 7728)
ALPHA = 2 ** 0.25
NJ = 16


class Plan:
    ENG = ["pe", "act", "dve", "pool", "sp"]

    def __init__(self, nc, stack):
        self.nc = nc
        self.q = {e: [] for e in self.ENG}
        self.tr = {e: [] for e in self.ENG}
        self.sem = {e: stack.enter_context(nc.semaphore("s_" + e)) for e in self.ENG}
        self.cnt = {e: 0 for e in self.ENG}
        self.ndma = 24
        self.dsem = [stack.enter_context(nc.semaphore("d%d" % i)) for i in range(self.ndma)]
        self.dcnt = [0] * self.ndma
        self.nsw = 46
        self.swsem = [stack.enter_context(nc.semaphore("w%d" % i)) for i in range(self.nsw)]
        self.swnext = 0
        self.dnext = 0
        self.seen = {e: {} for e in self.ENG}
        self.lastw = {}
        self.reads = {}

    def _semobj(self, key):
        if isinstance(key, str):
            return self.sem[key]
        return self.dsem[key] if key < 1000 else self.swsem[key - 1000]

    def _wait(self, eng, key, val):
        if eng == "pe" and key == "pe":
            return
        if self.seen[eng].get(key, 0) >= val:
            return
        self.seen[eng][key] = val
        s = self._semobj(key)
        self.tr[eng].append(("w", key, val))
        self.q[eng].append(lambda e, s=s, val=val: e.wait_ge(s, val))

    def _deps(self, eng, reads, writes):
        for b in reads:
            if b in self.lastw:
                self._wait(eng, *self.lastw[b])
        for b in writes:
            if b in self.lastw:
                self._wait(eng, *self.lastw[b])
            for k, v in self.reads.get(b, {}).items():
                self._wait(eng, k, v)

    def _commit(self, reads, writes, tok):
        for b in writes:
            self.lastw[b] = tok
            self.reads[b] = {}
        for b in reads:
            d = self.reads.setdefault(b, {})
            d[tok[0]] = max(d.get(tok[0], 0), tok[1])

    def op(self, eng, fn, reads=(), writes=()):
        psr = [k for k in reads if isinstance(k, str) and k.startswith("ps")]
        if psr:
            reads = [k for k in reads if k not in psr]
            writes = list(writes) + [k for k in psr if k not in writes]
        self._deps(eng, reads, writes)
        self.cnt[eng] += 1
        v = self.cnt[eng]
        s = self.sem[eng]
        self.tr[eng].append(("i", eng, 1))
        self.q[eng].append(lambda e, fn=fn, s=s: fn(e).then_inc(s, 1))
        self._commit(reads, writes, (eng, v))

    def dma(self, eng, fn, reads=(), writes=()):
        self._deps(eng, reads, writes)
        if eng == "pool":
            k = self.swnext
            self.swnext += 1
            assert k < self.nsw
            s = self.swsem[k]
            self.tr[eng].append(("i", 1000 + k, 16))
            self.q[eng].append(lambda e, fn=fn, s=s: fn(e).then_inc(s, 16))
            self._commit(reads, writes, (1000 + k, 16))
            return
        slot = self.dnext
        self.dnext = (self.dnext + 1) % self.ndma
        if self.dcnt[slot] > 0:
            self._wait(eng, slot, self.dcnt[slot])
        self.dcnt[slot] += 16
        v = self.dcnt[slot]
        s = self.dsem[slot]
        self.tr[eng].append(("i", slot, 16))
        self.q[eng].append(lambda e, fn=fn, s=s: fn(e).then_inc(s, 16))
        self._commit(reads, writes, (slot, v))

    def barrier(self):
        import os
        if os.environ.get("NOBAR"):
            return
        for eng in self.ENG:
            for slot in range(self.ndma):
                if self.dcnt[slot]:
                    self._wait(eng, slot, self.dcnt[slot])
            for k in range(self.swnext):
                self._wait(eng, 1000 + k, 16)
            for e in self.ENG:
                if self.cnt[e] and e != eng:
                    self._wait(eng, e, self.cnt[e])

    def final_wait(self, eng="sp"):
        for slot in range(self.ndma):
            if self.dcnt[slot]:
                self._wait(eng, slot, self.dcnt[slot])
        for k in range(self.swnext):
            self._wait(eng, 1000 + k, 16)
        for e in self.ENG:
            if self.cnt[e]:
                self._wait(eng, e, self.cnt[e])

    def emit(self):
        nc = self.nc
        with nc.Block() as block:
            @block.tensor
            def _(e):
                for f in self.q["pe"]:
                    f(e)

            @block.scalar
            def _(e):
                for f in self.q["act"]:
                    f(e)

            @block.vector
            def _(e):
                for f in self.q["dve"]:
                    f(e)

            @block.gpsimd
            def _(e):
                for f in self.q["pool"]:
                    f(e)

            @block.sync
            def _(e):
                for f in self.q["sp"]:
                    f(e)


def build(stage=99):
    nc = bass.Bass("TRN2", target_bir_lowering=False)
    _uid = [0]

    def _sbuf(name, shape, dt):
        _uid[0] += 1
        return nc.sbuf_tensor("%s_u%d" % (name, _uid[0]), list(shape), dt)

    def din(name, shape, dt=F32):
        return nc.dram_tensor(name, list(shape), dt, kind="ExternalInput").ap()

    def dscr(name, shape, dt=F32):
        kind = "ExternalOutput" if DEBUG else "Internal"
        return nc.dram_tensor(name, list(shape), dt, kind=kind).ap()

    x_own = din("x_own", [LO, D])
    x_ctx = din("x_ctx", [LC, D])
    c2 = din("c2", [128, KT, 2])
    b_ada_l = din("b_ada_l", [128, 48])
    w_ada = din("w_ada", [D, 3 * D])
    w_in = din("w_in", [D, 9776])
    flag = din("flag", [128, 1])
    ident_f = din("ident_f", [128, 128])
    ident_b = din("ident_b", [128, 128], BF16)
    w_bn = din("w_branch_nsa", [1024, D])
    w_bs = din("w_branch_ssm", [1024, D])
    w_out = din("w_out", [D, D])
    ln_g = din("ln_g", [1, D])
    ln_b = din("ln_b", [1, D])
    out = nc.dram_tensor("out", [LO, D], F32, kind="ExternalOutput").ap()

    qT_d = dscr("qT_d", [4, 16, 64, 4, 128], BF16)
    kcT_d = dscr("kcT_d", [256, LT], BF16)
    vcT_d = dscr("vcT_d", [256, LT], BF16)
    ksT_d = dscr("ksT_d", [256, LT], BF16)
    kwT_d = dscr("kwT_d", [256, LT], BF16)
    vs_d = dscr("vs_d", [LT, 4, 65], BF16)
    vw_d = dscr("vw_d", [LT, 4, 65], BF16)
    usT_d = dscr("usT_d", [1024, LT], BF16)
    gT_d = dscr("gT_d", [48, LO], F32)
    zaT_d = dscr("zaT_d", [4, 16, 64, 4, 128], BF16)
    zbT_d = dscr("zbT_d", [1024, LO], BF16)
    gaT_d = dscr("gaT_d", [D, LO], BF16)
    gbT_d = dscr("gbT_d", [D, LO], BF16)
    oaT_d = dscr("oaT_d", [4, 16, 64, 4, 128], BF16)
    obT_d = dscr("obT_d", [1024, LO], BF16)
    mT_d = dscr("mT_d", [D, LO], BF16)
    grow_d = dscr("grow_d", [16, 128], F32)


    w_cmp_k1 = din("w_cmp_k1", [2048, 128]); w_cmp_k2 = din("w_cmp_k2", [128, 64])
    w_cmp_v1 = din("w_cmp_v1", [2048, 128]); w_cmp_v2 = din("w_cmp_v2", [128, 64])
    posT2_k = din("posT2_k", [64, 32, 2]); posT2_v = din("posT2_v", [64, 32, 2])
    fvc = din("fvc", [128, 2])
    tw_raw = din("tw_raw", [3, 128, 16, 128]); ts_raw = din("ts_raw", [2, 128, 16, 128]); tn_raw = din("tn_raw", [16, 16, 128])
    stepb = din("stepb", [128, 16, 2]); SM2_d = din("SM2_d", [16, 376], BF16)
    rb31 = din("rb31", [1, 16])
    A_sel = din("A_sel", [128, 16, 64]); B_sel = din("B_sel", [128, 16, 64]); ovl_d = din("ovl_d", [128, 2, 65])
    EM_d = din("EM_d", [64, LT], BF16); SM_d = din("SM_d", [48, 3072]); ones_d = din("ones_d", [128, 128])
    kcmpT_d = dscr("kcmpT_d", [4, 64, 256], BF16); vcmp_d = dscr("vcmp_d", [4, 256, 65], F32)
    twh_d = dscr("twh_d", [3, 128, 16, 128], BF16); twl_d = dscr("twl_d", [3, 128, 16, 128], BF16)
    tsh_d = dscr("tsh_d", [2, 128, 16, 128], BF16); tsl_d = dscr("tsl_d", [2, 128, 16, 128], BF16)
    tnh_d = dscr("tnh_d", [16, 16, 128], BF16); tnl_d = dscr("tnl_d", [16, 16, 128], BF16)

    a_re2 = din("a_re2", [128, 32]); a_im2 = din("a_im2", [128, 32]); ldt2 = din("ldt2", [128, 32])
    b_re2 = din("b_re2", [128, 32, 16]); b_im2 = din("b_im2", [128, 32, 16])
    c_re2 = din("c_re2", [128, 32, 16]); c_im2 = din("c_im2", [128, 32, 16])
    dsk = din("dsk", [128, 8]); bglu = din("bglu", [128, 8])
    selm8_d = din("selm8_d", [128, 8, 240], BF16); TM_d = din("TM_d", [128, 2, 256])
    w_glu = din("w_glu", [1024, 1024])
    y_d = dscr("y_d", [LO, 1024], F32)
    ugo_d = dscr("ugo_d", [64, 128, 2, 128], BF16)

    w_in_r = w_in.rearrange("(kt p) n -> p kt n", p=128)
    w_ada_r = w_ada.rearrange("(kt p) n -> p kt n", p=128)

    with ExitStack() as st:
        P = Plan(nc, st)
        sb = lambda name, shape, dt=F32: st.enter_context(_sbuf(name, shape, dt))
        PS = [st.enter_context(nc.psum_tensor("ps%d" % i, [128, 512], F32)) for i in range(8)]
        identf = sb("identf", [128, 128])
        identb = sb("identb", [128, 128], BF16)
        flagt = sb("flagt", [128, 1])
        mod = sb("mod", [128, 48])
        scale1 = sb("scale1", [128, 16])
        P.dma("sp", lambda e: e.dma_start(out=identf[:], in_=ident_f[:, :]), writes=["identf"])
        P.dma("sp", lambda e: e.dma_start(out=identb[:], in_=ident_b[:, :]), writes=["identb"])
        P.dma("sp", lambda e: e.dma_start(out=flagt[:], in_=flag[:, :]), writes=["flagt"])

        with ExitStack() as s0:
            sb0 = lambda name, shape, dt=F32: s0.enter_context(_sbuf(name, shape, dt))
            c2t = sb0("c2t", [128, KT, 2])
            badat = sb0("badat", [128, 48])
            wb = [sb0("wada%d" % i, [128, KT, 512]) for i in range(2)]
            P.dma("sp", lambda e: e.dma_start(out=c2t[:], in_=c2[:, :, :]), writes=["c2t"])
            P.dma("sp", lambda e: e.dma_start(out=badat[:], in_=b_ada_l[:, :]), writes=["badat"])
            for ch in range(12):
                w = wb[ch % 2]
                wk = "wada%d" % (ch % 2)
                P.dma("sp", lambda e, w=w, ch=ch: e.dma_start(out=w[:], in_=w_ada_r[:, :, ch * 512:(ch + 1) * 512]),
                      writes=[wk])
                for mt in range(4):
                    t = ch * 4 + mt
                    for kt in range(KT):
                        P.op("pe", lambda e, w=w, mt=mt, kt=kt, t=t: e.matmul(
                            PS[0][:, 2 * t:2 * t + 2], lhsT=w[:, kt, mt * 128:(mt + 1) * 128], rhs=c2t[:, kt, :],
                            start=(kt == 0), stop=(kt == KT - 1)), reads=[wk, "c2t"], writes=["ps0"])
            P.op("dve", lambda e: e.tensor_tensor(out=mod[:], in0=PS[0][:, 0:96:2], in1=badat[:], op=ALU.add),
                 reads=["ps0", "badat"], writes=["mod"])
            P.op("dve", lambda e: e.tensor_scalar(out=scale1[:], in0=mod[:, 16:32], scalar1=1.0, scalar2=None,
                                                  op0=ALU.add), reads=["mod"], writes=["scale1"])
            gr = sb0("gr", [16, 128])
            P.op("pe", lambda e: e.transpose(PS[1][0:16, 0:128], mod[:, 32:48], identf[:]),
                 reads=["mod", "identf"], writes=["ps1"])
            P.op("dve", lambda e: e.tensor_copy(out=gr[:], in_=PS[1][0:16, 0:128]), reads=["ps1"], writes=["gr"])
            P.dma("sp", lambda e: e.dma_start(out=grow_d[:, :], in_=gr[:]), reads=["gr"], writes=["grow_d"])

        P.barrier()

        def ln_phase(xsrc, hT, hkey, ntiles, s1):
            sb1 = lambda name, shape, dt=F32: s1.enter_context(_sbuf(name, shape, dt))
            xb = [sb1("xb%d" % i, [128, D]) for i in range(2)]
            xn = [sb1("xn%d" % i, [128, D]) for i in range(2)]
            stt = [sb1("stt%d" % i, [128, 4, 6]) for i in range(2)]
            mv = [sb1("mv%d" % i, [128, 8]) for i in range(2)]
            def lnA(ti):
                r = ti % 2
                xk, xnk, sk, mk = "xb%d" % r, "xn%d" % r, "stt%d" % r, "mv%d" % r
                x_, xn_, st_, mv_ = xb[r], xn[r], stt[r], mv[r]
                P.dma("sp", lambda e, x_=x_, ti=ti: e.dma_start(out=x_[:], in_=xsrc[ti * 128:(ti + 1) * 128, :]),
                      writes=[xk])
                for c in range(4):
                    P.op("dve", lambda e, x_=x_, st_=st_, c=c: e.bn_stats(out=st_[:, c, :], in_=x_[:, c * 512:(c + 1) * 512]),
                         reads=[xk], writes=[sk])
                P.op("dve", lambda e, st_=st_, mv_=mv_: e.bn_aggr(out=mv_[:, 0:2], in_=st_[:].rearrange("p a b -> p (a b)")),
                     reads=[sk], writes=[mk])
                P.op("dve", lambda e, mv_=mv_: e.tensor_scalar(out=mv_[:, 2:3], in0=mv_[:, 1:2], scalar1=1e-5, scalar2=None,
                                                               op0=ALU.add), reads=[mk], writes=[mk])
                P.op("act", lambda e, mv_=mv_: e.activation(out=mv_[:, 3:4], in_=mv_[:, 2:3], func=AF.Sqrt),
                     reads=[mk], writes=[mk])
                P.op("dve", lambda e, mv_=mv_: e.reciprocal(out=mv_[:, 4:5], in_=mv_[:, 3:4]), reads=[mk], writes=[mk])
                P.op("dve", lambda e, mv_=mv_: e.tensor_scalar(out=mv_[:, 5:6], in0=mv_[:, 0:1], scalar1=mv_[:, 4:5],
                                                               scalar2=-1.0, op0=ALU.mult, op1=ALU.mult),
                     reads=[mk], writes=[mk])
                P.op("pool", lambda e, x_=x_, xn_=xn_, mv_=mv_: e.tensor_scalar(out=xn_[:], in0=x_[:], scalar1=mv_[:, 4:5],
                                                                                scalar2=mv_[:, 5:6], op0=ALU.mult, op1=ALU.add),
                     reads=[xk, mk], writes=[xnk])
            def lnB(ti):
                r = ti % 2
                xnk = "xn%d" % r
                xn_ = xn[r]
                for grp in range(4):
                    pb = 2 + (ti * 4 + grp) % 2
                    pk = "ps%d" % pb
                    for q in range(4):
                        kt = grp * 4 + q
                        P.op("pe", lambda e, xn_=xn_, kt=kt, q=q, pb=pb: e.transpose(
                            PS[pb][:, q * 128:(q + 1) * 128], xn_[:, kt * 128:(kt + 1) * 128], identf[:]),
                            reads=[xnk, "identf"], writes=[pk])
                    for q in range(4):
                        kt = grp * 4 + q
                        if grp == 0:
                            P.op("dve", lambda e, kt=kt, q=q, pb=pb, ti=ti: e.tensor_scalar(
                                out=hT[:, kt, ti * 128:(ti + 1) * 128], in0=PS[pb][:, q * 128:(q + 1) * 128],
                                scalar1=scale1[:, kt:kt + 1], scalar2=mod[:, kt:kt + 1], op0=ALU.mult, op1=ALU.add),
                                reads=[pk, "scale1", "mod"], writes=[(hkey, ti, kt)])
                        else:
                            P.op("act", lambda e, kt=kt, q=q, pb=pb, ti=ti: e.activation(
                                out=hT[:, kt, ti * 128:(ti + 1) * 128], in_=PS[pb][:, q * 128:(q + 1) * 128],
                                func=AF.Identity, scale=scale1[:, kt:kt + 1], bias=mod[:, kt:kt + 1]),
                                reads=[pk, "scale1", "mod"], writes=[(hkey, ti, kt)])
            lnA(0)
            for ti in range(ntiles):
                if ti + 1 < ntiles:
                    lnA(ti + 1)
                lnB(ti)

        cnt = {"w": 0, "ps": 0, "ev": 0}

        def proj_pass(hT, hkey, ntok, segs, s1, is_ctx):
            sb1 = lambda name, shape, dt=F32: s1.enter_context(_sbuf(name, shape, dt))
            wbuf = [sb1("wbuf%d" % i, [128, KT, 512], BF16) for i in range(2)]
            evb = [sb1("evb%d" % i, [128, 512], BF16) for i in range(3)]
            evf = [sb1("evf%d" % i, [128, 512], F32) for i in range(2)]
            vst = [sb1("vst%d" % i, [128, 4, 65], BF16) for i in range(2)]
            for i in range(2):
                P.op("pool", lambda e, i=i: e.memset(vst[i][:], 1.0), writes=["vst%d" % i])
            for (kind, col0, ncols, dst, func, scl, odt, tok0) in segs:
                for c0 in range(0, ncols, 512):
                    ncc = min(512, ncols - c0)
                    wi = cnt["w"] % 2
                    cnt["w"] += 1
                    w = wbuf[wi]
                    wk = "wbuf%d" % wi
                    P.dma("pool", lambda e, w=w, a=col0 + c0, n=ncc: e.dma_start(out=w[:, :, 0:n], in_=w_in_r[:, :, a:a + n]),
                          writes=[wk])
                    if kind in ("fm", "fmq"):
                        for mt in range((ncc + 127) // 128):
                            m = min(128, ncc - mt * 128)
                            for n in range(ntok // 512):
                                pb = 4 + cnt["ps"] % 4
                                cnt["ps"] += 1
                                pk = "ps%d" % pb
                                for kt in range(KT):
                                    P.op("pe", lambda e, w=w, kt=kt, mt=mt, m=m, n=n, pb=pb: e.matmul(
                                        PS[pb][0:m, :], lhsT=w[:, kt, mt * 128:mt * 128 + m],
                                        rhs=hT[:, kt, n * 512:(n + 1) * 512], start=(kt == 0), stop=(kt == KT - 1)),
                                        reads=[wk] + [(hkey, n * 4 + i_, kt) for i_ in range(4)], writes=[pk])
                                ei = cnt["ev"]
                                cnt["ev"] += 1
                                if odt == BF16:
                                    ev, ek = evb[ei % 3], "evb%d" % (ei % 3)
                                else:
                                    ev, ek = evf[ei % 2], "evf%d" % (ei % 2)
                                rd = [pk] + (["flagt"] if scl is not None and not isinstance(scl, float) else [])
                                if func is None and scl is None and ei % 2 == 0:
                                    P.op("dve", lambda e, ev=ev, m=m, pb=pb: e.tensor_copy(out=ev[0:m, :], in_=PS[pb][0:m, :]),
                                         reads=rd, writes=[ek])
                                else:
                                    kw = {} if scl is None else {"scale": scl}
                                    P.op("act", lambda e, ev=ev, m=m, pb=pb, f=(func or AF.Identity), kw=kw: e.activation(
                                        out=ev[0:m, :], in_=PS[pb][0:m, :], func=f, **kw), reads=rd, writes=[ek])
                                r0 = c0 + mt * 128
                                if kind == "fmq":
                                    h0 = r0 // 64
                                    for hh in range(2):
                                        g_, rr_ = (h0 + hh) // 4, (h0 + hh) % 4
                                        P.dma("sp", lambda e, ev=ev, n=n, dst=dst, hh=hh, g_=g_, rr_=rr_: e.dma_start(
                                            out=dst[g_, 4 * n:4 * n + 4, :, rr_, :].rearrange("j d q -> d j q"),
                                            in_=ev[hh * 64:(hh + 1) * 64, :].rearrange("p (j q) -> p j q", q=128)),
                                            reads=[ek], writes=[(id(dst), r0, n, hh)])
                                else:
                                    P.dma("sp", lambda e, ev=ev, m=m, r0=r0, n=n, dst=dst, tok0=tok0: e.dma_start(
                                        out=dst[r0:r0 + m, tok0 + n * 512: tok0 + (n + 1) * 512], in_=ev[0:m, :]),
                                        reads=[ek], writes=[(id(dst), r0, n, tok0)])
                    else:
                        for tt in range(ntok // 128):
                            pb = 4 + cnt["ps"] % 4
                            cnt["ps"] += 1
                            pk = "ps%d" % pb
                            for kt in range(KT):
                                P.op("pe", lambda e, w=w, kt=kt, tt=tt, pb=pb: e.matmul(
                                    PS[pb][:, 0:256], lhsT=hT[:, kt, tt * 128:(tt + 1) * 128], rhs=w[:, kt, 0:256],
                                    start=(kt == 0), stop=(kt == KT - 1)), reads=[wk, (hkey, tt, kt)], writes=[pk])
                            vi = cnt["ev"] % 2
                            cnt["ev"] += 1
                            v_, vk = vst[vi], "vst%d" % vi
                            if is_ctx:
                                P.op("act", lambda e, v_=v_, pb=pb: e.activation(
                                    out=v_[:, :, 0:64], in_=PS[pb][:, 0:256].rearrange("p (g d) -> p g d", g=4),
                                    func=AF.Identity, scale=flagt[:, 0:1]), reads=[pk, "flagt"], writes=[vk])
                                P.op("dve", lambda e, v_=v_: e.tensor_copy(
                                    out=v_[:, :, 64:65], in_=flagt[:, 0:1].unsqueeze(1).broadcast_to([128, 4, 1])),
                                    reads=["flagt"], writes=[vk])
                            else:
                                P.op("act", lambda e, v_=v_, pb=pb: e.activation(
                                    out=v_[:, :, 0:64], in_=PS[pb][:, 0:256].rearrange("p (g d) -> p g d", g=4),
                                    func=AF.Identity), reads=[pk], writes=[vk])
                            P.dma("sp", lambda e, v_=v_, tt=tt, dst=dst, tok0=tok0: e.dma_start(
                                out=dst[tok0 + tt * 128: tok0 + (tt + 1) * 128, :, :], in_=v_[:]),
                                reads=[vk], writes=[(id(dst), tt, tok0)])

        with ExitStack() as s1:
            hT = s1.enter_context(_sbuf("hTc", [128, KT, LC], BF16))
            if True:
                s2 = s1
                ln_phase(x_ctx, hT, "hTc", LC // 128, s2)
                segs = [
                    ("fm", C_KC, 256, kcT_d, None, None, BF16, 0),
                    ("fm", C_VC, 256, vcT_d, None, None, BF16, 0),
                    ("fm", C_KS, 256, ksT_d, None, None, BF16, 0),
                    ("fm", C_KW, 256, kwT_d, None, None, BF16, 0),
                    ("fm", C_US, 1024, usT_d, None, flagt[:, 0:1], BF16, 0),
                    ("tm", C_VS, 256, vs_d, None, None, BF16, 0),
                    ("tm", C_VW, 256, vw_d, None, None, BF16, 0),
                ]
                proj_pass(hT, "hTc", LC, segs, s2, True)
        P.barrier()
        if stage >= 2:
            with ExitStack() as s1:
                hT = s1.enter_context(_sbuf("hTo", [128, KT, LO], BF16))
                if True:
                    s2 = s1
                    ln_phase(x_own, hT, "hTo", LO // 128, s2)
                    segs = [
                        ("fmq", C_Q, 1024, qT_d, None, 0.125, BF16, 0),
                        ("fm", C_KC, 256, kcT_d, None, None, BF16, LC),
                        ("fm", C_VC, 256, vcT_d, None, None, BF16, LC),
                        ("fm", C_KS, 256, ksT_d, None, None, BF16, LC),
                        ("fm", C_KW, 256, kwT_d, None, None, BF16, LC),
                        ("fm", C_US, 1024, usT_d, None, None, BF16, LC),
                        ("tm", C_VS, 256, vs_d, None, None, BF16, LC),
                        ("tm", C_VW, 256, vw_d, None, None, BF16, LC),
                        ("fm", C_NG, 48, gT_d, AF.Sigmoid, None, F32, 0),
                        ("fmq", C_ZA, 1024, zaT_d, AF.Silu, None, BF16, 0),
                        ("fm", C_ZB, 1024, zbT_d, AF.Silu, None, BF16, 0),
                        ("fm", C_GA, 2048, gaT_d, AF.Sigmoid, None, BF16, 0),
                        ("fm", C_GB, 2048, gbT_d, AF.Sigmoid, None, BF16, 0),
                    ]
                    proj_pass(hT, "hTo", LO, segs, s2, False)
        P.barrier()
        V = lambda fn, r=(), w=(): P.op("dve", fn, reads=r, writes=w)
        A = lambda fn, r=(), w=(): P.op("act", fn, reads=r, writes=w)
        T = lambda fn, r=(), w=(): P.op("pe", fn, reads=r, writes=w)
        G = lambda fn, r=(), w=(): P.op("pool", fn, reads=r, writes=w)
        LD = lambda fn, r=(), w=(): P.dma("sp", fn, reads=r, writes=w)
        LDA = lambda fn, r=(), w=(): P.dma("act", fn, reads=r, writes=w)

        if stage >= 4:
            s4 = ExitStack()
            s4a = ExitStack()
            if True:
                sb4 = lambda name, shape, dt=F32: s4.enter_context(_sbuf(name, shape, dt))
                sb4a = lambda name, shape, dt=F32: s4a.enter_context(_sbuf(name, shape, dt))
                cr_ = sb4("cr_", [128, 32, 16])
                ci_ = sb4("ci_", [128, 32, 16])
                TMt = sb4("TMt", [128, 2, 256])
                bbr = sb4("bbr", [128, 32, 16])
                bbi = sb4("bbi", [128, 32, 16])
                pwr = sb4("pwr", [128, 32, 17])
                pwi = sb4("pwi", [128, 32, 17])
                qwr = sb4("qwr", [128, 32, 16])
                qwi = sb4("qwi", [128, 32, 16])
                L2A = sb4("L2A", [128, 2, 32])
                L2B = sb4("L2B", [128, 2, 32])
                HL = sb4("HL", [128, 2, 32, 256])
                tc_ = sb4("tc_", [128, 2, 32])
                m1 = sb4("m1", [128, 2, 32])
                m2 = sb4("m2", [128, 2, 32])
                SP_ = "ssmp"
                def Vp(fn): V(fn, [SP_], [SP_])
                def Ap(fn): A(fn, [SP_], [SP_])
                ar = sb4a("ar", [128, 32]); ai = sb4a("ai", [128, 32]); ldt = sb4a("ldt", [128, 32])
                br_ = sb4a("br_", [128, 32, 16]); bi_ = sb4a("bi_", [128, 32, 16])
                pass
                for t_, d_ in ((ar, a_re2), (ai, a_im2), (ldt, ldt2)):
                    LD(lambda e, t_=t_, d_=d_: e.dma_start(out=t_[:], in_=d_[:, :]), w=[SP_])
                for t_, d_ in ((br_, b_re2), (bi_, b_im2), (cr_, c_re2), (ci_, c_im2)):
                    LD(lambda e, t_=t_, d_=d_: e.dma_start(out=t_[:], in_=d_[:, :, :]), w=[SP_])
                selm8 = sb4a("selm8", [128, 8, 240], BF16)
                LD(lambda e: e.dma_start(out=selm8[:], in_=selm8_d[:, :, :]), w=["selm8"])
                LD(lambda e: e.dma_start(out=TMt[:], in_=TM_d[:, :, :]), w=["TMt"])
                dt_ = sb4a("dt_", [128, 32]); mag = sb4a("mag", [128, 32]); th = sb4a("th", [128, 32])
                u_ = sb4a("u_", [128, 32]); ki = sb4a("ki", [128, 32], mybir.dt.int32); kf = sb4a("kf", [128, 32]); gt_ = sb4a("gt_", [128, 32])
                rr_ = sb4a("rr_", [128, 32]); sn = sb4a("sn", [128, 32]); cs = sb4a("cs", [128, 32])
                lr = sb4a("lr", [128, 32]); li = sb4a("li", [128, 32])
                t1 = sb4a("t1", [128, 32]); t2 = sb4a("t2", [128, 32]); t3 = sb4a("t3", [128, 32])
                fr = sb4a("fr", [128, 32]); fi = sb4a("fi", [128, 32])
                q1r = sb4a("q1r", [128, 32]); q1i = sb4a("q1i", [128, 32])
                pass
                tb1 = sb4a("tb1", [128, 32, 16]); tb2 = sb4a("tb2", [128, 32, 16])
                pass
                pass
                pass
                TT = lambda o, a, b, op: Vp(lambda e: e.tensor_tensor(out=o, in0=a, in1=b, op=op))
                Ap(lambda e: e.activation(out=dt_[:], in_=ldt[:], func=AF.Exp))
                TT(t1[:], ar[:], dt_[:], ALU.mult)
                Ap(lambda e: e.activation(out=mag[:], in_=t1[:], func=AF.Exp))
                TT(th[:], ai[:], dt_[:], ALU.mult)
                C1, C2 = 6.28125, 0.0019353071795864769

                def sinred(out, shift):
                    Vp(lambda e: e.tensor_scalar(out=u_[:], in0=th[:], scalar1=1.0 / (2 * np.pi), scalar2=0.5 + shift / (2 * np.pi),
                                                 op0=ALU.mult, op1=ALU.add))
                    Vp(lambda e: e.tensor_copy(out=ki[:], in_=u_[:]))
                    Vp(lambda e: e.tensor_copy(out=kf[:], in_=ki[:]))
                    TT(gt_[:], kf[:], u_[:], ALU.is_gt)
                    TT(kf[:], kf[:], gt_[:], ALU.subtract)
                    Vp(lambda e: e.tensor_scalar(out=rr_[:], in0=th[:], scalar1=float(shift), scalar2=None, op0=ALU.add))
                    Vp(lambda e: e.scalar_tensor_tensor(out=rr_[:], in0=kf[:], scalar=-C1, in1=rr_[:], op0=ALU.mult, op1=ALU.add))
                    Vp(lambda e: e.scalar_tensor_tensor(out=rr_[:], in0=kf[:], scalar=-C2, in1=rr_[:], op0=ALU.mult, op1=ALU.add))
                    Vp(lambda e: e.tensor_scalar(out=rr_[:], in0=rr_[:], scalar1=-3.141592, scalar2=3.141592, op0=ALU.max, op1=ALU.min))
                    Ap(lambda e: e.activation(out=out, in_=rr_[:], func=AF.Sin))

                sinred(sn[:], 0.0)
                sinred(cs[:], np.pi / 2)
                TT(lr[:], mag[:], cs[:], ALU.mult)
                TT(li[:], mag[:], sn[:], ALU.mult)
                TT(t1[:], ar[:], ar[:], ALU.mult); TT(t2[:], ai[:], ai[:], ALU.mult); TT(t1[:], t1[:], t2[:], ALU.add)
                Vp(lambda e: e.reciprocal(out=t3[:], in_=t1[:]))
                Vp(lambda e: e.tensor_scalar(out=u_[:], in0=lr[:], scalar1=-1.0, scalar2=None, op0=ALU.add))
                TT(t1[:], u_[:], ar[:], ALU.mult); TT(t2[:], li[:], ai[:], ALU.mult); TT(t1[:], t1[:], t2[:], ALU.add); TT(fr[:], t1[:], t3[:], ALU.mult)
                TT(t1[:], li[:], ar[:], ALU.mult); TT(t2[:], u_[:], ai[:], ALU.mult); TT(t1[:], t1[:], t2[:], ALU.subtract); TT(fi[:], t1[:], t3[:], ALU.mult)
                bc = lambda x: x[:].unsqueeze(2).broadcast_to([128, 32, 16])
                TT(tb1[:], br_[:], bc(fr), ALU.mult); TT(tb2[:], bi_[:], bc(fi), ALU.mult); TT(bbr[:], tb1[:], tb2[:], ALU.subtract)
                TT(tb1[:], bi_[:], bc(fr), ALU.mult); TT(tb2[:], br_[:], bc(fi), ALU.mult); TT(bbi[:], tb1[:], tb2[:], ALU.add)
                TT(t1[:], lr[:], lr[:], ALU.mult); TT(t2[:], li[:], li[:], ALU.mult); TT(t1[:], t1[:], t2[:], ALU.add)
                Vp(lambda e: e.reciprocal(out=t3[:], in_=t1[:]))
                TT(q1r[:], lr[:], t3[:], ALU.mult)
                Vp(lambda e: e.scalar_tensor_tensor(out=q1i[:], in0=li[:], scalar=-1.0, in1=t3[:], op0=ALU.mult, op1=ALU.mult))
                def powers(pr_, pi_, n, xr, xi):
                    Vp(lambda e: e.memset(pr_[:, :, 0:1], 1.0)); Vp(lambda e: e.memset(pi_[:, :, 0:1], 0.0))
                    for k in range(1, n):
                        a_r, a_i = pr_[:, :, k - 1], pi_[:, :, k - 1]
                        TT(t1[:], a_r, xr[:], ALU.mult); TT(t2[:], a_i, xi[:], ALU.mult); TT(pr_[:, :, k], t1[:], t2[:], ALU.subtract)
                        TT(t1[:], a_r, xi[:], ALU.mult); TT(t2[:], a_i, xr[:], ALU.mult); TT(pi_[:, :, k], t1[:], t2[:], ALU.add)
                powers(pwr, pwi, 17, lr, li)
                powers(qwr, qwi, 16, q1r, q1i)
                Vp(lambda e: e.tensor_copy(out=L2A[:, 0, :], in_=pwr[:, :, 16])); Vp(lambda e: e.tensor_copy(out=L2A[:, 1, :], in_=pwr[:, :, 16]))
                Vp(lambda e: e.tensor_scalar(out=L2B[:, 0, :], in0=pwi[:, :, 16], scalar1=-1.0, scalar2=None, op0=ALU.mult))
                Vp(lambda e: e.tensor_copy(out=L2B[:, 1, :], in_=pwi[:, :, 16]))

                pass
                pass
                Ablk = sb4a("Ablk", [128, 4, 2, 256]); ta1 = sb4a("ta1", [128, 4, 256]); ta2 = sb4a("ta2", [128, 4, 256])
                v4 = lambda x: x.rearrange("p g (i c) -> p g i c", c=16)

                def build_A(gp0, Ablk, ta1, ta2):
                    bi4 = lambda x: x[:, gp0:gp0 + 4, :].unsqueeze(3).broadcast_to([128, 4, 16, 16])
                    bc4 = lambda x: x[:, gp0:gp0 + 4, :].unsqueeze(2).broadcast_to([128, 4, 16, 16])
                    rw = ["Ablk", SP_, "ta1", "ta2"]
                    V(lambda e: e.tensor_tensor(out=v4(ta1[:]), in0=bi4(qwr), in1=bc4(bbr), op=ALU.mult), rw, rw)
                    V(lambda e: e.tensor_tensor(out=v4(ta2[:]), in0=bi4(qwi), in1=bc4(bbi), op=ALU.mult), rw, rw)
                    V(lambda e: e.tensor_tensor(out=Ablk[:, :, 0, :], in0=ta1[:], in1=ta2[:], op=ALU.subtract), rw, rw)
                    V(lambda e: e.tensor_tensor(out=v4(ta1[:]), in0=bi4(qwr), in1=bc4(bbi), op=ALU.mult), rw, rw)
                    V(lambda e: e.tensor_tensor(out=v4(ta2[:]), in0=bi4(qwi), in1=bc4(bbr), op=ALU.mult), rw, rw)
                    V(lambda e: e.tensor_tensor(out=Ablk[:, :, 1, :], in0=ta1[:], in1=ta2[:], op=ALU.add), rw, rw)

                if True:
                    sb5p = sb4a
                    usb = [sb5p("usb%d" % i, [128, LT], BF16) for i in range(2)]
                    usp = [sb5p("usp%d" % i, [128, 16, 256], BF16) for i in range(2)]
                    ATg = [sb5p("ATg%d" % i, [128, 4, 64], BF16) for i in range(2)]
                    Ug = [sb5p("Ug%d" % i, [128, 2, 256], BF16) for i in range(2)]
                    for gp in range(32):
                        if gp % 4 == 0:
                            build_A(gp, Ablk, ta1, ta2)
                            ct = gp // 4
                            ub0 = usb[ct % 2]; uk0 = "usb%d" % (ct % 2)
                            LD(lambda e, ub0=ub0, ct=ct: e.dma_start(out=ub0[:], in_=usT_d[ct * 128:(ct + 1) * 128, :]), w=[uk0])
                            ub = usp[ct % 2]; uk = "usp%d" % (ct % 2)
                            G(lambda e, ub=ub, ub0=ub0: e.tensor_copy(out=ub[:], in_=ub0[:].rearrange("p (k i) -> p i k", i=16)), [uk0], [uk])
                        gpl = gp % 4
                        for g2 in range(2):
                            g = 2 * gp + g2
                            pr = slice(64 * g2, 64 * g2 + 64)
                            at, ug = ATg[g2], Ug[g2]
                            kat, kug = "ATg%d" % g2, "Ug%d" % g2
                            pa, pu = g2, 2 + g2
                            for a in range(2):
                                for ri in range(2):
                                    c0 = (a * 2 + ri) * 64
                                    T(lambda e, pa=pa, c0=c0, pr=pr, gpl=gpl, ri=ri, a=a: e.transpose(
                                        PS[pa][:, c0:c0 + 64], Ablk[pr, gpl, ri, a * 128:(a + 1) * 128], identf[pr, pr]),
                                        ["Ablk", "identf"], ["ps%d" % pa])
                            A(lambda e, at=at, pa=pa: e.activation(out=at[:].rearrange("p k c -> p (k c)"), in_=PS[pa][:, 0:256], func=AF.Identity),
                              ["ps%d" % pa], [kat])
                            g8 = g % 8
                            for a in range(2):
                                for ip in range(8):
                                    T(lambda e, pu=pu, a=a, ip=ip, g8=g8, ub=ub: e.matmul(
                                        PS[pu][:, a * 256:(a + 1) * 256], lhsT=selm8[:, g8, 112 - 16 * ip:112 - 16 * ip + 128],
                                        rhs=ub[:, 8 * a + ip, :], start=(ip == 0), stop=(ip == 7)), ["selm8", uk], ["ps%d" % pu])
                            V(lambda e, ug=ug, pu=pu: e.tensor_copy(out=ug[:].rearrange("p a k -> p (a k)"), in_=PS[pu][:, :]), ["ps%d" % pu], [kug])
                            LD(lambda e, ug=ug, g=g: e.dma_start(out=ugo_d[g], in_=ug[:, :, 128:256]), r=[kug], w=[("ugo", g)])
                            for ri in range(2):
                                for a in range(2):
                                    T(lambda e, pr=pr, ri=ri, a=a, at=at, ug=ug: e.matmul(
                                        PS[4][pr, ri * 256:(ri + 1) * 256], lhsT=at[:, a * 2 + ri, :], rhs=ug[:, a, :],
                                        start=(a == 0), stop=(a == 1)), [kat, kug], ["ps4"])
                        A(lambda e, gp=gp: e.activation(out=HL[:, :, gp, :], in_=PS[4][:, :].rearrange("p (r k) -> p r k", r=2), func=AF.Identity),
                          ["ps4"], ["HL"])
            s4a.close()
            P.barrier()
            def rec_gen():
                K_ = ["HL", SP_, "rec"]
                for k in range(255):
                    if k == 0:
                        src = HL[:, :, :, 0]
                    else:
                        G(lambda e, k=k: e.tensor_tensor(out=tc_[:], in0=HL[:, :, :, k - 1], in1=HL[:, :, :, k], op=ALU.add), K_, K_)
                        src = tc_[:]
                    G(lambda e, src=src: e.tensor_tensor(out=m1[:], in0=src, in1=L2A[:], op=ALU.mult), K_, K_)
                    s0 = HL[:, 0, :, 0] if k == 0 else tc_[:, 0, :]
                    s1_ = HL[:, 1, :, 0] if k == 0 else tc_[:, 1, :]
                    G(lambda e, s1_=s1_: e.tensor_tensor(out=m2[:, 0, :], in0=s1_, in1=L2B[:, 0, :], op=ALU.mult), K_, K_)
                    G(lambda e, s0=s0: e.tensor_tensor(out=m2[:, 1, :], in0=s0, in1=L2B[:, 1, :], op=ALU.mult), K_, K_)
                    G(lambda e, k=k: e.tensor_tensor(out=HL[:, :, :, k], in0=m1[:], in1=m2[:], op=ALU.add), K_, K_)
                    yield
            recg = rec_gen()
        P.barrier()
        def gelu(x, o, t1, t2, kx, ko, kt1, kt2):
            V(lambda e: e.tensor_tensor(out=t1, in0=x, in1=x, op=ALU.mult), [kx], [kt1])
            V(lambda e: e.tensor_scalar(out=t1, in0=t1, scalar1=0.044715, scalar2=1.0, op0=ALU.mult, op1=ALU.add), [kt1], [kt1])
            V(lambda e: e.tensor_tensor(out=t1, in0=t1, in1=x, op=ALU.mult), [kt1, kx], [kt1])
            A(lambda e: e.activation(out=t2, in_=t1, func=AF.Sigmoid, scale=1.5957691216057308), [kt1], [kt2])
            V(lambda e: e.tensor_tensor(out=o, in0=x, in1=t2, op=ALU.mult), [kx, kt2], [ko])

        if stage >= 3:
            with ExitStack() as s3:
                sb3 = lambda name, shape, dt=F32: s3.enter_context(_sbuf(name, shape, dt))
                cbc = sb3("cbc", [128, 16])
                LD(lambda e: e.dma_start(out=cbc[:], in_=rb31[0:1, :].broadcast_to([128, 16])), w=["cbc"])
                raw = [sb3("raw%d" % i, [128, 16, 128]) for i in range(2)]
                hib = [sb3("hib%d" % i, [128, 16, 128], BF16) for i in range(2)]
                hif = [sb3("hif%d" % i, [128, 16, 128]) for i in range(2)]
                lob = [sb3("lob%d" % i, [128, 16, 128], BF16) for i in range(2)]
                jobs = [(tw_raw[i], twh_d[i], twl_d[i]) for i in range(3)] + \
                       [(ts_raw[i], tsh_d[i], tsl_d[i]) for i in range(2)] + \
                       [(tn_raw, tnh_d, tnl_d)]
                for n_, (src, dh_, dl_) in enumerate(jobs):
                    r = n_ % 2
                    npart = 16 if n_ == len(jobs) - 1 else 128
                    ra, hb, hf, lb = raw[r][0:npart], hib[r][0:npart], hif[r][0:npart], lob[r][0:npart]
                    kr, kh, kfh, kl = "raw%d" % r, "hib%d" % r, "hif%d" % r, "lob%d" % r
                    LD(lambda e, ra=ra, src=src: e.dma_start(out=ra[:], in_=src), w=[kr])
                    V(lambda e, ra=ra, npart=npart: e.tensor_tensor(out=ra[:], in0=ra[:], in1=cbc[0:npart].unsqueeze(2).broadcast_to([npart, 16, 128]),
                                                       op=ALU.subtract), [kr, "cbc"], [kr])
                    A(lambda e, ra=ra, hb=hb: e.activation(out=hb[:], in_=ra[:], func=AF.Identity), [kr], [kh])
                    V(lambda e, hb=hb, hf=hf: e.tensor_copy(out=hf[:], in_=hb[:]), [kh], [kfh])
                    V(lambda e, ra=ra, hf=hf: e.tensor_tensor(out=hf[:], in0=ra[:], in1=hf[:], op=ALU.subtract), [kr, kfh], [kfh])
                    A(lambda e, hf=hf, lb=lb: e.activation(out=lb[:], in_=hf[:], func=AF.Identity), [kfh], [kl])
                    LD(lambda e, hb=hb, dh_=dh_: e.dma_start(out=dh_, in_=hb[:]), r=[kh], w=[("tabh", n_)])
                    LD(lambda e, lb=lb, dl_=dl_: e.dma_start(out=dl_, in_=lb[:]), r=[kl], w=[("tabl", n_)])
            P.barrier()
            with ExitStack() as s3:
                sb3 = lambda name, shape, dt=F32: s3.enter_context(_sbuf(name, shape, dt))
                w1f = sb3("w1f", [64, 32, 128]); w1b = sb3("w1b", [64, 32, 128], BF16)
                w2f = sb3("w2f", [128, 64]); w2b = sb3("w2b", [128, 64], BF16)
                posT = sb3("posT", [64, 32, 2]); c1 = sb3("c1", [128, 2]); fvct = sb3("fvct", [128, 2])
                kin = [sb3("kin%d" % i, [64, LT], BF16) for i in range(2)]
                hx = sb3("hx", [128, 256]); g1 = sb3("g1", [128, 256]); g2_ = sb3("g2_", [128, 256])
                hgf = sb3("hgf", [128, 256]); hgb = sb3("hgb", [128, 256], BF16)
                kst = sb3("kst", [64, 256], BF16); vcst = sb3("vcst", [128, 2, 65])
                LD(lambda e: e.dma_start(out=fvct[:], in_=fvc[:, :]), w=["fvct"])
                G(lambda e: e.memset(hgf[:], 0.0), w=["hgf"])
                G(lambda e: e.memset(hx[:], 0.0), w=["hx"])
                ci = 0
                for which, (w1d, w2d, posd, srcd) in enumerate([(w_cmp_k1, w_cmp_k2, posT2_k, kcT_d), (w_cmp_v1, w_cmp_v2, posT2_v, vcT_d)]):
                    LD(lambda e, w1d=w1d: e.dma_start(out=w1f[:], in_=w1d.rearrange("(l d) h -> d l h", d=64)), w=["w1f"])
                    LD(lambda e, w2d=w2d: e.dma_start(out=w2f[:], in_=w2d[:, :]), w=["w2f"])
                    LD(lambda e, posd=posd: e.dma_start(out=posT[:], in_=posd[:, :, :]), w=["posT"])
                    V(lambda e: e.tensor_copy(out=w1b[:], in_=w1f[:]), ["w1f"], ["w1b"])
                    V(lambda e: e.tensor_copy(out=w2b[:], in_=w2f[:]), ["w2f"], ["w2b"])
                    for l in range(32):
                        T(lambda e, l=l: e.matmul(PS[0][:, 0:2], lhsT=w1f[:, l, :], rhs=posT[:, l, :], start=(l == 0), stop=(l == 31)),
                          ["w1f", "posT"], ["ps0"])
                    V(lambda e: e.tensor_copy(out=c1[:], in_=PS[0][:, 0:2]), ["ps0"], ["c1"])
                    for g in range(4):
                        kn = kin[ci % 2]; kk = "kin%d" % (ci % 2); ci += 1
                        LD(lambda e, kn=kn, g=g, srcd=srcd: e.dma_start(out=kn[:], in_=srcd[g * 64:(g + 1) * 64, :]), w=[kk])
                        for l in range(32):
                            T(lambda e, kn=kn, l=l: e.matmul(PS[1][:, 0:255], lhsT=w1b[:, l, :], rhs=kn[:, l:l + 16 * 254 + 1:16],
                                                             start=(l == 0), stop=(l == 31)), ["w1b", kk], ["ps1"])
                        A(lambda e: e.activation(out=hx[:, 0:255], in_=PS[1][:, 0:255], func=AF.Identity, bias=c1[:, 0:1]),
                          ["ps1", "c1"], ["hx"])
                        gelu(hx[:, 0:255], hgf[:, 0:255], g1[:, 0:255], g2_[:, 0:255], "hx", "hgf", "g1", "g2_")
                        V(lambda e: e.tensor_copy(out=hgb[:], in_=hgf[:]), ["hgf"], ["hgb"])
                        if which == 0:
                            T(lambda e: e.matmul(PS[2][0:64, 0:256], lhsT=w2b[:], rhs=hgb[:], start=True, stop=True), ["w2b", "hgb"], ["ps2"])
                            V(lambda e: e.tensor_copy(out=kst[:], in_=PS[2][0:64, 0:256]), ["ps2"], ["kst"])
                            LD(lambda e, g=g: e.dma_start(out=kcmpT_d[g], in_=kst[:]), r=["kst"], w=[("kcmp", g)])
                        else:
                            for tq2 in range(2):
                                T(lambda e, tq2=tq2: e.matmul(PS[3][:, tq2 * 64:(tq2 + 1) * 64], lhsT=hgb[:, tq2 * 128:(tq2 + 1) * 128], rhs=w2b[:],
                                                            start=True, stop=True), ["w2b", "hgb"], ["ps3"])
                            for tq2 in range(2):
                                A(lambda e, tq2=tq2: e.activation(out=vcst[:, tq2, 0:64], in_=PS[3][:, tq2 * 64:(tq2 + 1) * 64], func=AF.Identity,
                                                                scale=fvct[:, tq2:tq2 + 1]), ["ps3", "fvct"], ["vcst"])
                            V(lambda e: e.tensor_copy(out=vcst[:, :, 64:65], in_=fvct[:].unsqueeze(2)), ["fvct"], ["vcst"])
                            LD(lambda e, g=g: e.dma_start(out=vcmp_d[g].rearrange("(t p) d -> p t d", p=128), in_=vcst[:]),
                               r=["vcst"], w=[("vcmp", g)])
            P.barrier()
            with ExitStack() as s3:
                sb3 = lambda name, shape, dt=F32: s3.enter_context(_sbuf(name, shape, dt))
                Asel = sb3("Asel", [128, 16, 64]); Bsel = sb3("Bsel", [128, 16, 64]); ovl = sb3("ovl", [128, 2, 65])
                LD(lambda e: e.dma_start(out=Asel[:], in_=A_sel[:, :, :]), w=["Asel"])
                LD(lambda e: e.dma_start(out=Bsel[:], in_=B_sel[:, :, :]), w=["Bsel"])
                LD(lambda e: e.dma_start(out=ovl[:], in_=ovl_d[:, :, :]), w=["ovl"])
                kcT = sb3("kcT", [64, 256], BF16); Vc = sb3("Vc", [128, 2, 65])
                ksT = sb3("ksT", [128, LT], BF16); kwT = sb3("kwT", [64, LT], BF16)
                vsb = sb3("vsb", [128, 32, 65], BF16); vwb = sb3("vwb", [128, 32, 65], BF16)
                twh = sb3("twh", [128, 3, 4, 128], BF16); twl = sb3("twl", [128, 3, 4, 128], BF16)
                tsh = sb3("tsh", [128, 2, 4, 128], BF16); tsl = sb3("tsl", [128, 2, 4, 128], BF16)
                qTt = [sb3("qTt%d" % i, [128, 4, 128], BF16) for i in range(2)]
                LD(lambda e: e.dma_start(out=ksT[64:128, :], in_=EM_d[:, :]), w=["ksTem"])
                zat = [sb3("zat%d" % i, [64, 4, 128], BF16) for i in range(3)]
                tnh = sb3("tnh", [16, 4, 128], BF16); tnl = sb3("tnl", [16, 4, 128], BF16)
                SM2t = sb3("SM2t", [16, 376], BF16); stepbt = sb3("stepbt", [128, 16, 2])
                LD(lambda e: e.dma_start(out=SM2t[:], in_=SM2_d[:, :]), w=["SM2t"])
                LD(lambda e: e.dma_start(out=stepbt[:], in_=stepb[:, :, :]), w=["stepbt"])
                Pf = [sb3("Pf%d" % i, [128, 2, 512]) for i in range(2)]
                Pb = [sb3("Pb%d" % i, [128, 512], BF16) for i in range(4)]
                Oc = [sb3("Oc%d" % i, [65, 512]) for i in range(3)]
                Osb = [sb3("Osb%d" % i, [65, 512]) for i in range(4)]
                rct3 = sb3("rct3", [96, 3, 512]); rc4 = sb3("rc4", [128, 4]); bc3 = sb3("bc3", [64, 3, 512])
                g64 = [sb3("g64_%d" % i, [65, 3, 4, 128]) for i in range(3)]
                G(lambda e: e.memset(rct3[:], 1.0), w=["rct3"])
                gT_v = gT_d.rearrange("(h b) t -> b h t", b=3)
                imp2 = sb3("imp2", [128, 64]); top8 = sb3("top8", [128, 8]); selm = sb3("selm", [128, 128])
                G(lambda e: e.memset(selm[:], 0.0), w=["selm"])
                acc = sb3("acc", [64, 512]); tm1 = sb3("tm1", [64, 512]); tm2 = sb3("tm2", [64, 512])
                oab = [sb3("oab%d" % i, [64, 4, 128], BF16) for i in range(2)]
                vs_v = vs_d.rearrange("(t p) g d -> p t g d", p=128)
                vw_v = vw_d.rearrange("(t p) g d -> p t g d", p=128)
                sc = {"s": 0, "pb": 0}
                f2 = lambda x: x.rearrange("p r q -> p (r q)")

                def s_tile(lhsT, lk, qt2, qk, extra, pb=None):
                    if pb is None:
                        pb = sc["s"] % 3; sc["s"] += 1
                    pk = "ps%d" % pb
                    n = len(extra)
                    T(lambda e: e.matmul(PS[pb][:, :], lhsT=lhsT, rhs=qt2, start=True, stop=(n == 0)), list(lk) + list(qk), [pk])
                    for i, (l2, r2, ks2) in enumerate(extra):
                        T(lambda e, l2=l2, r2=r2, i=i: e.matmul(PS[pb][:, :], lhsT=l2, rhs=r2, start=False, stop=(i == n - 1)), ks2, [pk])
                    return pb

                def front(g, j, it):
                    r = it % 2
                    r3 = it % 3
                    q_, za_ = qTt[r], zat[r3]
                    kq, kz = "qTt%d" % r, "zat%d" % r3
                    pf, oc, gb_ = Pf[r], Oc[r3], g64[r3]
                    kpf, koc, kn4, kgb = "Pf%d" % r, "Oc%d" % r3, "qn%d" % r, "g64_%d" % r3
                    tsl_j = slice(j * 128, (j + 1) * 128)
                    qtl = 16 + j
                    LD(lambda e: e.dma_start(out=q_[0:64], in_=qT_d[g, j]), w=[kq])
                    LD(lambda e: e.dma_start(out=za_[:], in_=zaT_d[g, j]), w=[kz])
                    LD(lambda e: e.dma_start(out=gb_[64:65], in_=gT_v[:, 4 * g:4 * g + 4, tsl_j].unsqueeze(0)), w=[kgb])
                    yield
                    q2 = f2(q_[0:64])
                    pbs = []
                    for kt2 in range(2):
                        off = 248 - 8 * qtl + 128 * kt2
                        pbs.append(s_tile(kcT[:, kt2 * 128:(kt2 + 1) * 128], ["kcT"], q2, [kq],
                                          [(SM2t[:, off:off + 128], f2(tnh[:]), ["SM2t", "tnh"]), (SM2t[:, off:off + 128], f2(tnl[:]), ["SM2t", "tnl"])], pb=(3, 7)[kt2]))
                    for kt2 in range(2):
                        A(lambda e, pb=pbs[kt2], kt2=kt2: e.activation(out=pf[:, kt2, :], in_=PS[pb][:, :], func=AF.Exp, bias=stepbt[:, j, kt2:kt2 + 1]),
                          ["ps%d" % pbs[kt2], "stepbt"], [kpf])
                    yield
                    for kt2 in range(2):
                        T(lambda e, kt2=kt2: e.matmul(PS[3][0:65, :], lhsT=Vc[:, kt2, :], rhs=pf[:, kt2, :], start=(kt2 == 0), stop=(kt2 == 1)),
                          ["Vc", kpf], ["ps3"])
                    V(lambda e: e.tensor_copy(out=oc[:], in_=PS[3][0:65, :]), ["ps3"], [koc])
                    yield
                    idx = 0
                    for rr in range(4):
                        for kt2 in range(2):
                            T(lambda e, rr=rr, kt2=kt2: e.matmul(PS[7][:, rr * 65:(rr + 1) * 65], lhsT=pf[:, kt2, rr * 128:(rr + 1) * 128], rhs=ovl[:, kt2, :],
                                                                 start=(kt2 == 0), stop=(kt2 == 1)), [kpf, "ovl"], ["ps7"])
                    U4 = PS[7][:, 0:260].rearrange("p (r c) -> p r c", c=65)
                    V(lambda e: e.tensor_scalar(out=rc4[:], in0=U4[:, :, 64], scalar1=1e-18, scalar2=None, op0=ALU.max), ["ps7"], ["rc4"])
                    V(lambda e: e.reciprocal(out=rc4[:], in_=rc4[:]), ["rc4"], ["rc4"])
                    V(lambda e: e.tensor_scalar(out=imp2[:], in0=U4[:, 0, 0:64], scalar1=rc4[:, 0:1], scalar2=None, op0=ALU.mult), ["ps7", "rc4"], ["imp2"])
                    for rr in range(1, 4):
                        V(lambda e, rr=rr: e.scalar_tensor_tensor(out=imp2[:], in0=U4[:, rr, 0:64], scalar=rc4[:, rr:rr + 1], in1=imp2[:], op0=ALU.mult, op1=ALU.add),
                          ["ps7", "rc4", "imp2"], ["imp2"])
                    V(lambda e: e.tensor_tensor(out=imp2[:], in0=imp2[:], in1=Asel[:, j, :], op=ALU.mult), ["imp2", "Asel"], ["imp2"])
                    V(lambda e: e.tensor_tensor(out=imp2[:], in0=imp2[:], in1=Bsel[:, j, :], op=ALU.add), ["imp2", "Bsel"], ["imp2"])
                    V(lambda e: e.max(out=top8[:], in_=imp2[:]), ["imp2"], ["top8"])
                    V(lambda e: e.tensor_scalar(out=selm[:, 64:128], in0=imp2[:], scalar1=top8[:, 7:8], scalar2=-NEG, op0=ALU.is_ge, op1=ALU.mult),
                      ["imp2", "top8"], ["selm"])
                    V(lambda e: e.tensor_scalar(out=selm[:, 64:128], in0=selm[:, 64:128], scalar1=NEG, scalar2=None, op0=ALU.add), ["selm"], ["selm"])
                    yield
                    yield
                    T(lambda e: e.transpose(PS[3][:, 0:128], selm[:], identf[:]), ["selm", "identf"], ["ps3"])
                    for rr in range(4):
                        V(lambda e, rr=rr: e.tensor_copy(out=q_[64:128, rr, :], in_=PS[3][64:128, 0:128]), ["ps3"], [kn4])
                    yield

                def adv(gen):
                    if gen is None:
                        return False
                    try:
                        next(gen)
                        return True
                    except StopIteration:
                        return False

                def back(g, j, it, nxt, prev_tail):
                    r = it % 2
                    q_ = qTt[r]
                    kq, kn4 = "qTt%d" % r, "qn%d" % r
                    qtl = 16 + j
                    q2 = f2(q_[0:64])
                    qn2 = f2(q_[:])
                    sb_ = 4 if r == 0 else 6
                    steps = []
                    for kt in range(qtl - 4, qtl + 1):
                        d = qtl - kt
                        extra = []
                        if d in (0, 1, 4):
                            di = {0: 0, 1: 1, 4: 2}[d]
                            extra = [(identb[:], f2(twh[:, di]), ["identb", "twh"]), (identb[:], f2(twl[:, di]), ["identb", "twl"])]
                        steps.append((kwT[:, kt * 128:(kt + 1) * 128], ["kwT"], extra, vwb[:, kt, :], "vwb", 5, d == 4, d == 0, q2, [kq]))
                    nk = qtl + 1
                    for kt in range(nk):
                        d = qtl - kt
                        extra = []
                        if d in (0, 1):
                            extra = [(identb[:], f2(tsh[:, d]), ["identb", "tsh"]), (identb[:], f2(tsl[:, d]), ["identb", "tsl"])]
                        steps.append((ksT[:, kt * 128:(kt + 1) * 128], ["ksT", "ksTem"], extra, vsb[:, kt, :], "vsb", sb_, kt == 0, kt == nk - 1, qn2, [kq, kn4]))
                    pend = []

                    def finish():
                        pb, st = pend.pop(0)
                        pi = sc["pb"] % 4; sc["pb"] += 1
                        A(lambda e: e.activation(out=Pb[pi][:], in_=PS[pb][:, :], func=AF.Exp), ["ps%d" % pb], ["Pb%d" % pi])
                        T(lambda e: e.matmul(PS[st[5]][0:65, :], lhsT=st[3], rhs=Pb[pi][:], start=st[6], stop=st[7]), [st[4], "Pb%d" % pi], ["ps%d" % st[5]])

                    Ow_, kow = Osb[2 * r + 1], "Osb%d" % (2 * r + 1)
                    for i, st in enumerate(steps):
                        pb = s_tile(st[0], st[1], st[8], st[9], st[2])
                        pend.append((pb, st))
                        if len(pend) > 2:
                            finish()
                        if i == 10:
                            V(lambda e: e.tensor_copy(out=Ow_[:], in_=PS[5][0:65, :]), ["ps5"], [kow])
                        if i % 3 == 1:
                            adv(nxt)
                        if i % 2 == 1:
                            adv(prev_tail)
                        if i % 4 == 0:
                            adv(recg)
                    while pend:
                        finish()
                    while adv(prev_tail):
                        pass

                def tail(g, j, it):
                    r = it % 2
                    r3 = it % 3
                    za_, oa_ = zat[r3], oab[r]
                    kz, koa = "zat%d" % r3, "oab%d" % r
                    oc, gb_ = Oc[r3], g64[r3]
                    koc, kgb = "Oc%d" % r3, "g64_%d" % r3
                    tsl_j = slice(j * 128, (j + 1) * 128)
                    sb_ = 4 if r == 0 else 6
                    Os_, kos = Osb[2 * r], "Osb%d" % (2 * r)
                    Ow_, kow = Osb[2 * r + 1], "Osb%d" % (2 * r + 1)
                    V(lambda e: e.tensor_copy(out=Os_[:], in_=PS[sb_][0:65, :]), ["ps%d" % sb_], [kos])
                    yield
                    Ol = [(oc, koc), (Os_, kos), (Ow_, kow)]
                    for br in range(3):
                        V(lambda e, br=br: e.tensor_scalar(out=rct3[64:65, br, :], in0=Ol[br][0][64:65, :], scalar1=1e-18, scalar2=None, op0=ALU.max),
                          [Ol[br][1]], ["rct3"])
                    yield
                    yield
                    r3f = rct3[64:65].rearrange("p b q -> p (b q)")
                    A(lambda e: e.activation(out=r3f, in_=r3f, func=AF.Ln), ["rct3"], ["rct3"])
                    yield
                    yield
                    A(lambda e: e.activation(out=r3f, in_=r3f, func=AF.Exp, scale=-1.0), ["rct3"], ["rct3"])
                    yield
                    yield
                    V(lambda e: e.tensor_tensor(out=r3f, in0=r3f, in1=gb_[64:65].rearrange("p b r q -> p (b r q)"), op=ALU.mult), ["rct3", kgb], ["rct3"])
                    for hh in range(2):
                        V(lambda e, hh=hh: e.stream_shuffle(out=bc3[32 * hh:32 * hh + 32].rearrange("p b q -> p (b q)"),
                                                            in_=rct3[64:96].rearrange("p b q -> p (b q)"), mask=[0] * 32), ["rct3"], ["bc3"])
                    yield
                    yield
                    V(lambda e: e.tensor_tensor(out=acc[:], in0=oc[0:64, :], in1=bc3[:, 0, :], op=ALU.mult), [koc, "bc3"], ["acc"])
                    G(lambda e: e.tensor_tensor(out=tm1[:], in0=Os_[0:64, :], in1=bc3[:, 1, :], op=ALU.mult), [kos, "bc3"], ["tm1"])
                    G(lambda e: e.tensor_tensor(out=tm2[:], in0=Ow_[0:64, :], in1=bc3[:, 2, :], op=ALU.mult), [kow, "bc3"], ["tm2"])
                    yield
                    G(lambda e: e.tensor_tensor(out=acc[:], in0=acc[:], in1=tm1[:], op=ALU.add), ["acc", "tm1"], ["acc"])
                    G(lambda e: e.tensor_tensor(out=acc[:], in0=acc[:], in1=tm2[:], op=ALU.add), ["acc", "tm2"], ["acc"])
                    yield
                    V(lambda e: e.tensor_tensor(out=f2(oa_[:]), in0=acc[:], in1=f2(za_[:]), op=ALU.mult), ["acc", kz], [koa])
                    LD(lambda e: e.dma_start(out=oaT_d[g, j], in_=oa_[:]), r=[koa], w=[("oaT", g, j)])

                it = 0
                ptail = None
                for g in range(4):
                    LD(lambda e, g=g: e.dma_start(out=kcT[:], in_=kcmpT_d[g]), w=["kcT"])
                    LD(lambda e, g=g: e.dma_start(out=Vc[:], in_=vcmp_d[g].rearrange("(t p) d -> p t d", p=128)), w=["Vc"])
                    LD(lambda e, g=g: e.dma_start(out=ksT[0:64, :], in_=ksT_d[g * 64:(g + 1) * 64, :]), w=["ksT"])
                    LD(lambda e, g=g: e.dma_start(out=kwT[:], in_=kwT_d[g * 64:(g + 1) * 64, :]), w=["kwT"])
                    LD(lambda e, g=g: e.dma_start(out=vsb[:], in_=vs_v[:, :, g, :]), w=["vsb"])
                    LD(lambda e, g=g: e.dma_start(out=vwb[:], in_=vw_v[:, :, g, :]), w=["vwb"])
                    for i in range(3):
                        LD(lambda e, g=g, i=i: e.dma_start(out=twh[:, i], in_=twh_d[i][:, 4 * g:4 * g + 4, :]), w=["twh"])
                        LD(lambda e, g=g, i=i: e.dma_start(out=twl[:, i], in_=twl_d[i][:, 4 * g:4 * g + 4, :]), w=["twl"])
                    LD(lambda e, g=g: e.dma_start(out=tnh[:], in_=tnh_d[:, 4 * g:4 * g + 4, :]), w=["tnh"])
                    LD(lambda e, g=g: e.dma_start(out=tnl[:], in_=tnl_d[:, 4 * g:4 * g + 4, :]), w=["tnl"])
                    for i in range(2):
                        LD(lambda e, g=g, i=i: e.dma_start(out=tsh[:, i], in_=tsh_d[i][:, 4 * g:4 * g + 4, :]), w=["tsh"])
                        LD(lambda e, g=g, i=i: e.dma_start(out=tsl[:, i], in_=tsl_d[i][:, 4 * g:4 * g + 4, :]), w=["tsl"])
                    cur = front(g, 0, it)
                    while adv(cur):
                        pass
                    for j in range(NJ):
                        nxt = front(g, j + 1, it + 1) if j + 1 < NJ else None
                        back(g, j, it, nxt, ptail)
                        while adv(nxt):
                            pass
                        ptail = tail(g, j, it)
                        adv(ptail)
                        it += 1
                while adv(ptail):
                    pass
                while adv(recg):
                    pass
            P.barrier()
        if stage >= 4:
            if True:
                with ExitStack() as s5:
                    sb5 = lambda name, shape, dt=F32: s5.enter_context(_sbuf(name, shape, dt))
                    Bblk = sb5("Bblk", [128, 4, 2, 256])
                    Ablk2 = sb5("Ablk2", [128, 4, 2, 256]); ta12 = sb5("ta12", [128, 4, 256]); ta22 = sb5("ta22", [128, 4, 256])
                    ugt = [sb5("ugt%d" % i, [128, 2, 128], BF16) for i in range(2)]
                    WTg = [sb5("WTg%d" % i, [128, 2, 256], BF16) for i in range(2)]
                    Yblk = sb5("Yblk", [128, 16, 256])
                    y_v = y_d.rearrange("(k j) c -> k j c", j=16)
                    for gp in range(32):
                        if gp % 4 == 0:
                            build_A(gp, Ablk2, ta12, ta22)
                            bj4 = lambda x, gp=gp: x[:, gp:gp + 4, 0:16].unsqueeze(3).broadcast_to([128, 4, 16, 16])
                            bc4 = lambda x, gp=gp: x[:, gp:gp + 4, :].unsqueeze(2).broadcast_to([128, 4, 16, 16])
                            rw = ["Bblk", SP_, "ta12", "ta22"]
                            V(lambda e, bj4=bj4, bc4=bc4: e.tensor_tensor(out=v4(ta12[:]), in0=bj4(pwr), in1=bc4(cr_), op=ALU.mult), rw, rw)
                            V(lambda e, bj4=bj4, bc4=bc4: e.tensor_tensor(out=v4(ta22[:]), in0=bj4(pwi), in1=bc4(ci_), op=ALU.mult), rw, rw)
                            V(lambda e: e.tensor_tensor(out=Bblk[:, :, 0, :], in0=ta12[:], in1=ta22[:], op=ALU.subtract), rw, rw)
                            V(lambda e, bj4=bj4, bc4=bc4: e.tensor_tensor(out=v4(ta12[:]), in0=bj4(pwi), in1=bc4(cr_), op=ALU.mult), rw, rw)
                            V(lambda e, bj4=bj4, bc4=bc4: e.tensor_tensor(out=v4(ta22[:]), in0=bj4(pwr), in1=bc4(ci_), op=ALU.mult), rw, rw)
                            V(lambda e: e.tensor_tensor(out=ta12[:], in0=ta12[:], in1=ta22[:], op=ALU.add), rw, rw)
                            V(lambda e: e.tensor_scalar(out=Bblk[:, :, 1, :], in0=ta12[:], scalar1=-1.0, scalar2=None, op0=ALU.mult), rw, rw)
                        gpl = gp % 4
                        for g2 in range(2):
                            g = 2 * gp + g2
                            pr = slice(64 * g2, 64 * g2 + 64)
                            wt = WTg[g2]; kw_ = "WTg%d" % g2
                            pw_, py_ = g2, 2 + g2
                            for a in range(2):
                                for ri in range(2):
                                    T(lambda e, pw_=pw_, a=a, ri=ri, pr=pr, gpl=gpl: e.matmul(
                                        PS[pw_][:, a * 256:(a + 1) * 256], lhsT=Ablk2[pr, gpl, ri, a * 128:(a + 1) * 128], rhs=Bblk[pr, gpl, ri, :],
                                        start=(ri == 0), stop=(ri == 1)), ["Ablk", "Bblk"], ["ps%d" % pw_])
                            V(lambda e, wt=wt, pw_=pw_: e.tensor_tensor(out=wt[:].rearrange("p a k -> p (a k)"), in0=PS[pw_][:, :],
                                                                        in1=TMt[:].rearrange("p a k -> p (a k)"), op=ALU.mult),
                              ["ps%d" % pw_, "TMt"], [kw_])
                            ug_ = ugt[g2]
                            LD(lambda e, ug_=ug_, g=g: e.dma_start(out=ug_[:], in_=ugo_d[g]), w=["ugt%d" % g2])
                            for a in range(2):
                                T(lambda e, py_=py_, a=a, ug_=ug_, wt=wt: e.matmul(PS[py_][:, 0:256], lhsT=ug_[:, a, :], rhs=wt[:, a, :],
                                                                                start=(a == 0), stop=False), ["ugt%d" % g2, kw_], ["ps%d" % py_])
                            for ri in range(2):
                                T(lambda e, py_=py_, ri=ri, pr=pr, gp=gp, gpl=gpl: e.matmul(
                                    PS[py_][:, 0:256], lhsT=HL[pr, ri, gp, 127:255], rhs=Bblk[pr, gpl, ri, :], start=False, stop=(ri == 1)),
                                    ["HL", "Bblk"], ["ps%d" % py_])
                            gl = g % 16
                            A(lambda e, py_=py_, gl=gl: e.activation(out=Yblk[:, :, gl * 16:(gl + 1) * 16],
                                                                     in_=PS[py_][:, 0:256].rearrange("p (j c) -> p j c", c=16), func=AF.Identity),
                              ["ps%d" % py_], ["Yblk"])
                            if gl == 15:
                                cb = (g // 16) * 256
                                LD(lambda e, cb=cb: e.dma_start(out=y_v[:, :, cb:cb + 256], in_=Yblk[:]), r=["Yblk"], w=[("y_d", cb)])
            P.barrier()
            s4.close()
            P.barrier()
        if stage >= 5:
            with ExitStack() as s6:
                sb6 = lambda name, shape, dt=F32: s6.enter_context(_sbuf(name, shape, dt))
                yT = sb6("yT", [128, 8, LO], BF16)
                wgl = sb6("wgl", [128, 8, 1024], BF16); bglt = sb6("bglt", [128, 8]); dskt = sb6("dskt", [128, 8])
                LD(lambda e: e.dma_start(out=bglt[:], in_=bglu[:, :]), w=["bglt"])
                LD(lambda e: e.dma_start(out=dskt[:], in_=dsk[:, :]), w=["dskt"])
                for c in range(2):
                    P.dma("pool", lambda e, c=c: e.dma_start(out=wgl[:, :, c * 512:(c + 1) * 512],
                                                             in_=w_glu.rearrange("(k p) n -> p k n", p=128)[:, :, c * 512:(c + 1) * 512]), writes=["wgl"])
                yt = [sb6("yt%d" % i, [128, 1024]) for i in range(2)]
                ust = [sb6("ust%d" % i, [128, 8, 128], BF16) for i in range(2)]
                ypre = sb6("ypre", [128, 8, 128]); yg = sb6("yg", [128, 8, 128]); gt1 = sb6("gt1", [128, 8, 128]); gt2 = sb6("gt2", [128, 8, 128])
                us_v = usT_d.rearrange("(c p) t -> p c t", p=128)
                fl = lambda x: x[:].rearrange("p c t -> p (c t)")
                for tt in range(LO // 128):
                    r = tt % 2
                    y_, u_s = yt[r], ust[r]
                    ky, ku = "yt%d" % r, "ust%d" % r
                    LD(lambda e, y_=y_, tt=tt: e.dma_start(out=y_[:], in_=y_d[tt * 128:(tt + 1) * 128, :]), w=[ky])
                    LD(lambda e, u_s=u_s, tt=tt: e.dma_start(out=u_s[:], in_=us_v[:, :, LC + tt * 128:LC + (tt + 1) * 128]), w=[ku])
                    for ct in range(8):
                        pb = ct // 4
                        T(lambda e, y_=y_, ct=ct, pb=pb: e.transpose(PS[pb][:, (ct % 4) * 128:(ct % 4 + 1) * 128], y_[:, ct * 128:(ct + 1) * 128], identf[:]),
                          [ky, "identf"], ["ps%d" % pb])
                    for ct in range(8):
                        pb = ct // 4
                        V(lambda e, u_s=u_s, ct=ct, pb=pb: e.scalar_tensor_tensor(out=ypre[:, ct, :], in0=u_s[:, ct, :], scalar=dskt[:, ct:ct + 1],
                                                                                  in1=PS[pb][:, (ct % 4) * 128:(ct % 4 + 1) * 128], op0=ALU.mult, op1=ALU.add),
                          [ku, "dskt", "ps%d" % pb], ["ypre"])
                    gelu(fl(ypre), fl(yg), fl(gt1), fl(gt2), "ypre", "yg", "gt1", "gt2")
                    A(lambda e, tt=tt: e.activation(out=yT[:, :, tt * 128:(tt + 1) * 128], in_=yg[:], func=AF.Identity), ["yg"], [("yT", tt)])
                sgb = [sb6("sgb%d" % i, [128, 512]) for i in range(2)]
                zbt = [sb6("zbt%d" % i, [128, 512], BF16) for i in range(2)]
                obs = [sb6("obs%d" % i, [128, 512], BF16) for i in range(2)]
                it = 0
                for co in range(8):
                    for n in range(4):
                        r = it % 2; it += 1
                        pb = 4 + it % 4
                        nsl = slice(n * 512, (n + 1) * 512)
                        LDA(lambda e, r=r, co=co, nsl=nsl: e.dma_start(out=zbt[r][:], in_=zbT_d[co * 128:(co + 1) * 128, nsl]), w=["zbt%d" % r])
                        for ct in range(8):
                            T(lambda e, pb=pb, ct=ct, co=co, nsl=nsl: e.matmul(PS[pb][:, :], lhsT=wgl[:, ct, co * 128:(co + 1) * 128], rhs=yT[:, ct, nsl],
                                                                              start=(ct == 0), stop=(ct == 7)),
                              ["wgl"] + [("yT", n * 4 + i_) for i_ in range(4)], ["ps%d" % pb])
                        A(lambda e, r=r, pb=pb, co=co: e.activation(out=sgb[r][:], in_=PS[pb][:, :], func=AF.Sigmoid, bias=bglt[:, co:co + 1]),
                          ["ps%d" % pb, "bglt"], ["sgb%d" % r])
                        V(lambda e, r=r, co=co, nsl=nsl: e.tensor_tensor(out=sgb[r][:], in0=sgb[r][:], in1=yT[:, co, nsl], op=ALU.mult),
                          ["sgb%d" % r] + [("yT", n * 4 + i_) for i_ in range(4)], ["sgb%d" % r])
                        V(lambda e, r=r: e.tensor_tensor(out=obs[r][:], in0=sgb[r][:], in1=zbt[r][:], op=ALU.mult), ["sgb%d" % r, "zbt%d" % r], ["obs%d" % r])
                        LD(lambda e, r=r, co=co, nsl=nsl: e.dma_start(out=obT_d[co * 128:(co + 1) * 128, nsl], in_=obs[r][:]), r=["obs%d" % r], w=[("obT", co, n)])
            P.barrier()
        if stage >= 6:
            with ExitStack() as s7:
                sb7 = lambda name, shape, dt=F32: s7.enter_context(_sbuf(name, shape, dt))
                oaT = sb7("oaT", [128, 8, LO], BF16); obT = sb7("obT", [128, 8, LO], BF16)
                for c in range(4):
                    csl = slice(c * 512, (c + 1) * 512)
                    for kt_ in range(8):
                        for hh in range(2):
                            gq_, rq_ = kt_ // 2, 2 * (kt_ % 2) + hh
                            LD(lambda e, c=c, kt_=kt_, hh=hh, gq_=gq_, rq_=rq_: e.dma_start(
                                out=oaT[hh * 64:(hh + 1) * 64, kt_, c * 512:(c + 1) * 512].rearrange("p (j q) -> p j q", q=128),
                                in_=oaT_d[gq_, 4 * c:4 * c + 4, :, rq_, :].rearrange("j d q -> d j q")), w=[("oaTs", c, kt_, hh)])
                    LD(lambda e, csl=csl: e.dma_start(out=obT[:, :, csl], in_=obT_d.rearrange("(k p) t -> p k t", p=128)[:, :, csl]), w=[("obTs", c)])
                wa = [sb7("wa%d" % i, [128, 8, 512], BF16) for i in range(2)]
                wbb = [sb7("wbb%d" % i, [128, 8, 512], BF16) for i in range(2)]
                gat = [sb7("gat%d" % i, [128, 512], BF16) for i in range(2)]
                gbt = [sb7("gbt%d" % i, [128, 512], BF16) for i in range(2)]
                ma = [sb7("ma%d" % i, [128, 512]) for i in range(2)]
                mb = [sb7("mb%d" % i, [128, 512]) for i in range(2)]
                mo = [sb7("mo%d" % i, [128, 512], BF16) for i in range(2)]
                wbn_v = w_bn.rearrange("(k p) n -> p k n", p=128); wbs_v = w_bs.rearrange("(k p) n -> p k n", p=128)
                it = 0
                def ld_w(dc):
                    wr = dc % 2
                    dsl = slice(dc * 512, (dc + 1) * 512)
                    P.dma("pool", lambda e: e.dma_start(out=wa[wr][:], in_=wbn_v[:, :, dsl]), writes=["wa%d" % wr])
                    P.dma("pool", lambda e: e.dma_start(out=wbb[wr][:], in_=wbs_v[:, :, dsl]), writes=["wbb%d" % wr])
                ld_w(0)
                for dc in range(4):
                    wr = dc % 2
                    dsl = slice(dc * 512, (dc + 1) * 512)
                    if dc + 1 < 4:
                        ld_w(dc + 1)
                    for mt in range(4):
                        dt_i = dc * 4 + mt
                        for n in range(4):
                            r = it % 2; it += 1
                            pa, pb = 4 + 2 * r, 5 + 2 * r
                            nsl = slice(n * 512, (n + 1) * 512)
                            rows = slice(dt_i * 128, (dt_i + 1) * 128)
                            LDA(lambda e, r=r, rows=rows, nsl=nsl: e.dma_start(out=gat[r][:], in_=gaT_d[rows, nsl]), w=["gat%d" % r])
                            LDA(lambda e, r=r, rows=rows, nsl=nsl: e.dma_start(out=gbt[r][:], in_=gbT_d[rows, nsl]), w=["gbt%d" % r])
                            for kt in range(8):
                                T(lambda e, pa=pa, kt=kt, mt=mt, wr=wr, nsl=nsl: e.matmul(PS[pa][:, :], lhsT=wa[wr][:, kt, mt * 128:(mt + 1) * 128], rhs=oaT[:, kt, nsl],
                                                                                         start=(kt == 0), stop=(kt == 7)), ["wa%d" % wr, ("oaTs", n, kt, 0), ("oaTs", n, kt, 1)], ["ps%d" % pa])
                            for kt in range(8):
                                T(lambda e, pb=pb, kt=kt, mt=mt, wr=wr, nsl=nsl: e.matmul(PS[pb][:, :], lhsT=wbb[wr][:, kt, mt * 128:(mt + 1) * 128], rhs=obT[:, kt, nsl],
                                                                                         start=(kt == 0), stop=(kt == 7)), ["wbb%d" % wr, ("obTs", n)], ["ps%d" % pb])
                            V(lambda e, r=r, pa=pa: e.tensor_tensor(out=ma[r][:], in0=PS[pa][:, :], in1=gat[r][:], op=ALU.mult), ["ps%d" % pa, "gat%d" % r], ["ma%d" % r])
                            V(lambda e, r=r, pb=pb: e.tensor_tensor(out=mb[r][:], in0=PS[pb][:, :], in1=gbt[r][:], op=ALU.mult), ["ps%d" % pb, "gbt%d" % r], ["mb%d" % r])
                            G(lambda e, r=r: e.tensor_tensor(out=mo[r][:], in0=ma[r][:], in1=mb[r][:], op=ALU.add), ["ma%d" % r, "mb%d" % r], ["mo%d" % r])
                            LD(lambda e, r=r, rows=rows, nsl=nsl: e.dma_start(out=mT_d[rows, nsl], in_=mo[r][:]), r=["mo%d" % r], w=[("mT", dt_i, n)])
            P.barrier()
        if stage >= 7:
            with ExitStack() as s8:
                sb8 = lambda name, shape, dt=F32: s8.enter_context(_sbuf(name, shape, dt))
                wo = sb8("wo", [128, KT, D], BF16)
                for c in range(4):
                    P.dma("pool", lambda e, c=c: e.dma_start(out=wo[:, :, c * 512:(c + 1) * 512],
                                                             in_=w_out.rearrange("(k p) n -> p k n", p=128)[:, :, c * 512:(c + 1) * 512]), writes=[("wo", c)])
                gbc = sb8("gbc", [128, D]); lgb = sb8("lgb", [128, D]); lbb = sb8("lbb", [128, D])
                LD(lambda e: e.dma_start(out=gbc[:], in_=grow_d.rearrange("a b -> (a b)").unsqueeze(0).broadcast_to([128, D])), w=["gbc"])
                LD(lambda e: e.dma_start(out=lgb[:], in_=ln_g[0:1, :].broadcast_to([128, D])), w=["lgb"])
                LD(lambda e: e.dma_start(out=lbb[:], in_=ln_b[0:1, :].broadcast_to([128, D])), w=["lbb"])
                mt_ = [sb8("mtt%d" % i, [128, KT, 128], BF16) for i in range(2)]
                xo = [sb8("xo%d" % i, [128, D]) for i in range(2)]
                z = [sb8("z%d" % i, [128, D]) for i in range(2)]
                zt1 = sb8("zt1", [128, 512])
                so = [sb8("so%d" % i, [128, 4, 6]) for i in range(2)]
                mvo = [sb8("mvo%d" % i, [128, 8]) for i in range(2)]
                mT_v = mT_d.rearrange("(k p) t -> p k t", p=128)
                for tt in range(LO // 128):
                    r = tt % 2
                    m_, x_, z_, s_, v_ = mt_[r], xo[r], z[r], so[r], mvo[r]
                    km, kx, kz, ks_, kv_ = "mtt%d" % r, "xo%d" % r, "z%d" % r, "so%d" % r, "mvo%d" % r
                    tks = slice(tt * 128, (tt + 1) * 128)
                    LDA(lambda e, m_=m_, tks=tks: e.dma_start(out=m_[:], in_=mT_v[:, :, tks]), w=[km])
                    LDA(lambda e, x_=x_, tks=tks: e.dma_start(out=x_[:], in_=x_own[tks, :]), w=[kx])
                    for dc in range(4):
                        pb = 4 + (tt * 4 + dc) % 4
                        dsl = slice(dc * 512, (dc + 1) * 512)
                        for kt in range(KT):
                            T(lambda e, pb=pb, kt=kt, m_=m_, dsl=dsl: e.matmul(PS[pb][:, :], lhsT=m_[:, kt, :], rhs=wo[:, kt, dsl], start=(kt == 0), stop=(kt == KT - 1)),
                              [km, ("wo", dc)], ["ps%d" % pb])
                        V(lambda e, pb=pb, dsl=dsl: e.tensor_tensor(out=zt1[:], in0=PS[pb][:, :], in1=gbc[:, dsl], op=ALU.mult), ["ps%d" % pb, "gbc"], ["zt1"])
                        V(lambda e, x_=x_, z_=z_, dsl=dsl: e.scalar_tensor_tensor(out=z_[:, dsl], in0=x_[:, dsl], scalar=float(ALPHA), in1=zt1[:], op0=ALU.mult, op1=ALU.add),
                          [kx, "zt1"], [kz])
                        V(lambda e, z_=z_, s_=s_, dc=dc, dsl=dsl: e.bn_stats(out=s_[:, dc, :], in_=z_[:, dsl]), [kz], [ks_])
                    V(lambda e, s_=s_, v_=v_: e.bn_aggr(out=v_[:, 0:2], in_=s_[:].rearrange("p a b -> p (a b)")), [ks_], [kv_])
                    V(lambda e, v_=v_: e.tensor_scalar(out=v_[:, 2:3], in0=v_[:, 1:2], scalar1=1e-5, scalar2=None, op0=ALU.add), [kv_], [kv_])
                    A(lambda e, v_=v_: e.activation(out=v_[:, 3:4], in_=v_[:, 2:3], func=AF.Sqrt), [kv_], [kv_])
                    V(lambda e, v_=v_: e.reciprocal(out=v_[:, 4:5], in_=v_[:, 3:4]), [kv_], [kv_])
                    V(lambda e, v_=v_: e.tensor_scalar(out=v_[:, 5:6], in0=v_[:, 0:1], scalar1=v_[:, 4:5], scalar2=-1.0, op0=ALU.mult, op1=ALU.mult), [kv_], [kv_])
                    A(lambda e, z_=z_, v_=v_: e.activation(out=z_[:], in_=z_[:], func=AF.Identity, scale=v_[:, 4:5], bias=v_[:, 5:6]), [kz, kv_], [kz])
                    G(lambda e, z_=z_: e.tensor_tensor(out=z_[:], in0=z_[:], in1=lgb[:], op=ALU.mult), [kz, "lgb"], [kz])
                    G(lambda e, z_=z_: e.tensor_tensor(out=z_[:], in0=z_[:], in1=lbb[:], op=ALU.add), [kz, "lbb"], [kz])
                    LD(lambda e, z_=z_, tks=tks: e.dma_start(out=out[tks, :], in_=z_[:]), r=[kz], w=[("out", tt)])

        P.final_wait("sp")
        P.emit()
        nc._plan_trace = P.tr
    return nc


def _bf16(a):
    return np.ascontiguousarray(a).astype(ml_dtypes.bfloat16)


def _bucket(dist):
    n = np.maximum(dist, 0)
    nf = np.maximum(n, 16).astype(np.float32)
    large = 16 + (np.log(nf / np.float32(16)) / np.float32(np.log(8.0)) * np.float32(16)).astype(np.int32)
    large = np.minimum(large, 31)
    return np.where(n < 16, n, large)


def _attn_tables(rel_bias):
    rb = np.asarray(rel_bias, np.float32)
    key = np.arange(128)[:, None]
    q = np.arange(128)[None, :]
    def tile(d, lo, hi):
        dist = 128 * d + q - key
        valid = (dist >= lo) & (dist < hi)
        t = rb[_bucket(dist)]
        t = np.where(valid[:, :, None], t, np.float32(NEG))
        return np.ascontiguousarray(t.transpose(0, 2, 1))
    tw = np.stack([tile(0, 0, 512), tile(1, 0, 512), tile(4, 0, 512)])
    ts = np.stack([tile(0, 0, 1 << 30), tile(1, 0, 1 << 30)])
    m = np.arange(16)[:, None] - 9
    dist = q - 16 * m - 31
    t = rb[_bucket(dist)]
    t = np.where((dist >= 0)[:, :, None], t, np.float32(NEG))
    tc = np.ascontiguousarray(t.transpose(0, 2, 1))
    return tw.astype(np.float32), ts.astype(np.float32), tc.astype(np.float32)


def _sel_tables(half):
    A = np.zeros((128, 16, 64), np.float32)
    B = np.zeros((128, 16, 64), np.float32)
    qq = np.arange(128)[:, None, None]
    jj = np.arange(16)[None, :, None]
    jb = np.arange(64)[None, None, :]
    t = 128 * (16 + jj) + qq + 2048 * (half - 1)
    cur = t // 64
    gb = jb + 32 * (half - 1)
    allowed = (gb <= cur) & (gb >= 0)
    forced = ((gb == 0) | (gb == cur) | (gb == cur - 1)) & allowed
    A[...] = np.where(forced | ~allowed, 0.0, 1.0)
    B[...] = np.where(~allowed, -1e30, np.where(forced, 1e4, 0.0))
    n = np.arange(256)[:, None]
    sj = np.arange(64)[None, :]
    ov = ((16 * n < 64 * sj + 64) & (16 * n + 32 > 64 * sj)).astype(np.float32)
    ov[255] = 0
    if half == 0:
        ov[:128] = 0
    valid = np.ones((256, 1), np.float32)
    valid[255] = 0
    if half == 0:
        valid[:128] = 0
    ov = np.concatenate([ov, valid], axis=1)
    ovl = np.ascontiguousarray(ov.reshape(2, 128, 65).transpose(1, 0, 2))
    fvc = np.ones((128, 2), np.float32)
    fvc[:, 0] = float(half)
    fvc[127, 1] = 0.0
    return A, B, ovl, fvc


def make_maps(inp):
    x = np.asarray(inp["x"], np.float32)
    c = np.asarray(inp["c"], np.float32)
    shared = {
        "b_ada_l": np.ascontiguousarray(np.asarray(inp["b_ada"], np.float32)[0].reshape(48, 128).T),
        "w_ada": np.ascontiguousarray(np.asarray(inp["w_ada"], np.float32)[0]),
        "w_in": np.ascontiguousarray(np.asarray(inp["w_in"], np.float32)[0]),
        "ident_f": np.eye(128, dtype=np.float32),
        "ident_b": _bf16(np.eye(128, dtype=np.float32)),
        "w_branch_nsa": np.ascontiguousarray(np.asarray(inp["w_branch_nsa"], np.float32)[0]),
        "w_branch_ssm": np.ascontiguousarray(np.asarray(inp["w_branch_ssm"], np.float32)[0]),
        "w_out": np.ascontiguousarray(np.asarray(inp["w_out"], np.float32)[0]),
        "ln_g": np.ascontiguousarray(np.asarray(inp["ln_g"], np.float32)),
        "ln_b": np.ascontiguousarray(np.asarray(inp["ln_b"], np.float32)),
        "w_cmp_k1": np.ascontiguousarray(np.asarray(inp["w_cmp_k1"], np.float32)[0]),
        "w_cmp_k2": np.ascontiguousarray(np.asarray(inp["w_cmp_k2"], np.float32)[0]),
        "w_cmp_v1": np.ascontiguousarray(np.asarray(inp["w_cmp_v1"], np.float32)[0]),
        "w_cmp_v2": np.ascontiguousarray(np.asarray(inp["w_cmp_v2"], np.float32)[0]),
        "posT2_k": np.ascontiguousarray(np.repeat(np.asarray(inp["cmp_pos_k"], np.float32)[0].T[:, :, None], 2, axis=2)),
        "posT2_v": np.ascontiguousarray(np.repeat(np.asarray(inp["cmp_pos_v"], np.float32)[0].T[:, :, None], 2, axis=2)),
        "rb31": np.ascontiguousarray(np.asarray(inp["rel_bias"], np.float32)[31:32]),
        "EM_d": _bf16((np.arange(LT)[None, :] // 64 == np.arange(64)[:, None]).astype(np.float32)),
        "SM_d": (np.arange(3072)[None, :] // 64 == np.arange(48)[:, None]).astype(np.float32),
        "ones_d": np.ones((128, 128), np.float32),
    }
    f32 = lambda k: np.asarray(inp[k], np.float32)[0]
    g2p = lambda a: np.ascontiguousarray(a.reshape(32, 2, 64, *a.shape[2:]).transpose(1, 2, 0, *range(3, a.ndim + 1)).reshape(128, 32, *a.shape[2:]))
    shared["a_re2"] = g2p(f32("ssm_a_re")); shared["a_im2"] = g2p(f32("ssm_a_im"))
    shared["ldt2"] = g2p(np.repeat(f32("ssm_log_dt")[:, None], 64, axis=1))
    shared["b_re2"] = g2p(f32("ssm_b_re")); shared["b_im2"] = g2p(f32("ssm_b_im"))
    shared["c_re2"] = g2p(np.ascontiguousarray(f32("ssm_c_re").transpose(0, 2, 1))); shared["c_im2"] = g2p(np.ascontiguousarray(f32("ssm_c_im").transpose(0, 2, 1)))
    shared["dsk"] = np.ascontiguousarray(f32("ssm_d").reshape(8, 128).T); shared["bglu"] = np.ascontiguousarray(f32("b_glu").reshape(8, 128).T)
    shared["w_glu"] = np.ascontiguousarray(f32("w_glu"))
    row = np.arange(128)
    sm8 = np.zeros((128, 8, 240), np.float32)
    sm8[row, row // 16, 112 + row % 16] = 1.0
    shared["selm8_d"] = _bf16(sm8)
    tm = np.zeros((128, 2, 16, 16), np.float32)
    for a_ in range(2):
        tm[:, a_] = (np.arange(16)[None, :, None] >= (8 * a_ + row // 16)[:, None, None]).astype(np.float32)
    shared["TM_d"] = np.ascontiguousarray(tm.reshape(128, 2, 256))
    tw, ts, tc = _attn_tables(inp["rel_bias"])
    shared["tw_raw"], shared["ts_raw"], shared["tn_raw"] = tw, ts, tc
    pp = np.arange(128)[:, None, None]; jj_ = np.arange(16)[None, :, None]; k2 = np.arange(2)[None, None, :]
    shared["stepb"] = np.where(128 * k2 + pp >= 8 * (16 + jj_) + 7, np.float32(NEG), np.float32(0.0)).astype(np.float32)
    shared["SM2_d"] = _bf16((np.arange(376)[None, :] == np.arange(16)[:, None] + 239).astype(np.float32))
    seltabs = [_sel_tables(h) for h in range(2)]
    maps = []
    for core in range(8):
        b, half = core // 2, core % 2
        m = dict(shared)
        m["x_own"] = np.ascontiguousarray(x[b, half * LO:(half + 1) * LO])
        m["x_ctx"] = np.ascontiguousarray(x[b, 0:LC]) if half == 1 else np.zeros((LC, D), np.float32)
        cc = c[b].reshape(16, 128).T
        m["c2"] = np.ascontiguousarray(np.stack([cc, cc], axis=-1))
        m["flag"] = np.full((128, 1), float(half), np.float32)
        m["A_sel"], m["B_sel"], m["ovl_d"], m["fvc"] = seltabs[half]
        maps.append(m)
    return maps


def kernel(**inp):
    nc = build()
    maps = make_maps(inp)
    res = run_bass_kernel_spmd(nc, maps, core_ids=list(range(8)))
    out = np.zeros((4, 4096, D), np.float32)
    for core in range(8):
        b, half = core // 2, core % 2
        out[b, half * LO:(half + 1) * LO] = np.asarray(res.results[core]["out"])
    return out
```
